# Optimizing a Trainium2 kernel written in Bass

```python
import jax, jax.numpy as jnp
from jax import lax
import numpy as np

D_MODEL = 1024
BATCH = 32
SEQ = 2048
DEPTH = 1
DEC_BATCH = 4
DEC_SEQ = 4096
PAST_LEN = 128

N_META = 16
GRID_W = 64
WIN_ROWS_MAX = 8
WIN_COLS = 16
Q_BLOCK_COLS = 16
K_BLOCK_COLS = Q_BLOCK_COLS + WIN_COLS
CONV_WIDTH = 31
D_CONV = D_MODEL
N_HEADS = 16
HEAD_DIM = 64
D_ATT = N_HEADS * HEAD_DIM
D_MIX = D_CONV + D_ATT
D_IN = 3 * D_CONV + 4 * D_ATT
RMS_EPS = 1e-6
LN_EPS = 1e-5
NEG_INF = -1e30

kernel_name = "hybrid_conformer_natten_encoder"


def rms_norm(x, g):
    x32 = x.astype(jnp.float32)
    y = x32 * lax.rsqrt(jnp.mean(x32 * x32, axis=-1, keepdims=True) + RMS_EPS)
    return (y * g.astype(jnp.float32)).astype(x.dtype)


def layer_norm(x, g, b):
    x32 = x.astype(jnp.float32)
    mu = jnp.mean(x32, axis=-1, keepdims=True)
    xc = x32 - mu
    var = jnp.mean(xc * xc, axis=-1, keepdims=True)
    y = xc * lax.rsqrt(var + LN_EPS) * g.astype(jnp.float32) + b.astype(jnp.float32)
    return y.astype(x.dtype)


def conv_module(val, glu_gate, conv_w, conv_b, ln_g, ln_b):
    u = val * jax.nn.sigmoid(glu_gate)
    pad = CONV_WIDTH // 2
    y = lax.conv_general_dilated(
        u, conv_w.astype(u.dtype)[:, None, :], window_strides=(1,),
        padding=[(pad, pad)], dimension_numbers=('NWC', 'WIO', 'NWC'),
        feature_group_count=u.shape[-1])
    y = y + conv_b.astype(u.dtype)
    y = layer_norm(y, ln_g, ln_b)
    return jax.nn.silu(y)


def _column_blocks():
    n_cb = GRID_W // Q_BLOCK_COLS
    j = np.arange(n_cb)
    kc0 = np.clip(j * Q_BLOCK_COLS - WIN_COLS // 2, 0, GRID_W - K_BLOCK_COLS)
    key_cols = kc0[:, None] + np.arange(K_BLOCK_COLS)[None, :]
    q_cols = j[:, None] * Q_BLOCK_COLS + np.arange(Q_BLOCK_COLS)[None, :]
    start = np.clip(q_cols - WIN_COLS // 2, 0, GRID_W - WIN_COLS)
    off = key_cols[:, None, :] - start[:, :, None]
    valid = (off >= 0) & (off < WIN_COLS)
    rel = key_cols[:, None, :] - q_cols[:, :, None]
    rel_idx = np.clip(rel + WIN_COLS - 1, 0, 2 * WIN_COLS - 2)
    return key_cols, valid, rel_idx


def neighbourhood_attention(q, k, v, q_meta, k_meta, v_meta, rel_bias):
    B, T, H, dh = q.shape
    rows = T // GRID_W
    wr = min(WIN_ROWS_MAX, rows)
    n_cb = GRID_W // Q_BLOCK_COLS
    scale = dh ** -0.5
    qg = (q * scale).reshape(B, rows, n_cb, Q_BLOCK_COLS, H, dh)
    kg = k.reshape(B, rows, GRID_W, H, dh)
    vg = v.reshape(B, rows, GRID_W, H, dh)
    key_cols, valid, rel_idx = _column_blocks()
    bias_c = rel_bias.astype(jnp.float32)[:, :, rel_idx]
    mask_add = jnp.asarray(np.where(valid, 0.0, NEG_INF), jnp.float32)
    k_meta_s = k_meta

    def one_row(r):
        sr = jnp.clip(r - wr // 2, 0, rows - wr)
        k_rows = lax.dynamic_slice_in_dim(kg, sr, wr, axis=1)
        v_rows = lax.dynamic_slice_in_dim(vg, sr, wr, axis=1)
        k_blk = k_rows[:, :, key_cols]
        v_blk = v_rows[:, :, key_cols]
        q_r = lax.dynamic_index_in_dim(qg, r, axis=1, keepdims=False)
        s = jnp.einsum('bjqhd,brjkhd->bhjqrk', q_r, k_blk,
                       preferred_element_type=jnp.float32)
        row_idx = sr + jnp.arange(wr) - r + (WIN_ROWS_MAX - 1)
        bias = jnp.take(bias_c, row_idx, axis=1)
        s = s + bias.transpose(0, 2, 3, 1, 4)[None] + mask_add[:, :, None, :]
        s = s.reshape(B, H, n_cb, Q_BLOCK_COLS, wr * K_BLOCK_COLS)
        s_meta = jnp.einsum('bjqhd,bmhd->bhjqm', q_r, k_meta_s,
                            preferred_element_type=jnp.float32)
        p = jax.nn.softmax(jnp.concatenate([s, s_meta], axis=-1), axis=-1).astype(v.dtype)
        p_grid = p[..., :wr * K_BLOCK_COLS].reshape(B, H, n_cb, Q_BLOCK_COLS, wr, K_BLOCK_COLS)
        p_meta = p[..., wr * K_BLOCK_COLS:]
        o = (jnp.einsum('bhjqrk,brjkhd->bjqhd', p_grid, v_blk)
             + jnp.einsum('bhjqm,bmhd->bjqhd', p_meta, v_meta))
        return o.reshape(B, GRID_W, H, dh)

    o_grid = lax.map(one_row, jnp.arange(rows))
    o_grid = o_grid.transpose(1, 0, 2, 3, 4).reshape(B, T, H, dh)
    s_mm = jnp.einsum('bqhd,bkhd->bhqk', q_meta * scale, k_meta,
                      preferred_element_type=jnp.float32)
    p_mm = jax.nn.softmax(s_mm, axis=-1).astype(v.dtype)
    o_meta = jnp.einsum('bhqk,bkhd->bqhd', p_mm, v_meta)
    return o_meta, o_grid


def encoder_layer(h, pre_g, w_in, conv_w, conv_b, ln_g, ln_b, rel_bias, post_g, w_out):
    B, L, _ = h.shape
    u = rms_norm(h, pre_g)
    z = jnp.einsum('bld,de->ble', u, w_in.astype(u.dtype))
    c_val, c_glu, c_gate, q, k, v, a_gate = jnp.split(
        z, [D_CONV, 2 * D_CONV, 3 * D_CONV, 3 * D_CONV + D_ATT,
            3 * D_CONV + 2 * D_ATT, 3 * D_CONV + 3 * D_ATT], axis=-1)
    conv_out = conv_module(c_val, c_glu, conv_w, conv_b, ln_g, ln_b) * jax.nn.silu(c_gate)
    q = q.reshape(B, L, N_HEADS, HEAD_DIM)
    k = k.reshape(B, L, N_HEADS, HEAD_DIM)
    v = v.reshape(B, L, N_HEADS, HEAD_DIM)
    o_meta, o_grid = neighbourhood_attention(
        q[:, N_META:], k[:, N_META:], v[:, N_META:],
        q[:, :N_META], k[:, :N_META], v[:, :N_META], rel_bias)
    att = jnp.concatenate([o_meta, o_grid], axis=1).reshape(B, L, D_ATT) * jax.nn.silu(a_gate)
    mixed = jnp.concatenate([conv_out, att], axis=-1)
    y = jnp.einsum('ble,ed->bld', mixed, w_out.astype(mixed.dtype))
    return h + rms_norm(y, post_g)


def run_trunk(x, meta_tokens, pre_norm_g, w_in, conv_w, conv_b, conv_ln_g, conv_ln_b,
              rel_bias, post_norm_g, w_out):
    B = x.shape[0]
    meta = jnp.broadcast_to(meta_tokens.astype(x.dtype)[None], (B, N_META, x.shape[-1]))
    h = jnp.concatenate([meta, x], axis=1)
    for i in range(DEPTH):
        h = encoder_layer(h, pre_norm_g[i], w_in[i], conv_w[i], conv_b[i], conv_ln_g[i],
                          conv_ln_b[i], rel_bias[i], post_norm_g[i], w_out[i])
    return h[:, N_META:]


def setup_inputs(seed: int = 0) -> dict:
    key = jax.random.key(seed)
    ks = jax.random.split(key, 13)
    f32 = jnp.float32
    nrm = lambda k, s: jax.random.normal(k, s, dtype=f32)
    return {
        "x_prompt": nrm(ks[0], (BATCH, SEQ, D_MODEL)),
        "x_sample": nrm(ks[1], (DEC_BATCH, DEC_SEQ, D_MODEL)),
        "meta_tokens": nrm(ks[2], (N_META, D_MODEL)),
        "pre_norm_g": 1.0 + 0.05 * nrm(ks[3], (DEPTH, D_MODEL)),
        "w_in": nrm(ks[4], (DEPTH, D_MODEL, D_IN)) * D_MODEL ** -0.5,
        "conv_w": nrm(ks[5], (DEPTH, CONV_WIDTH, D_CONV)) * CONV_WIDTH ** -0.5,
        "conv_b": 0.02 * nrm(ks[6], (DEPTH, D_CONV)),
        "conv_ln_g": 1.0 + 0.05 * nrm(ks[7], (DEPTH, D_CONV)),
        "conv_ln_b": 0.02 * nrm(ks[8], (DEPTH, D_CONV)),
        "rel_bias": 0.1 * nrm(ks[9], (DEPTH, N_HEADS, 2 * WIN_ROWS_MAX - 1, 2 * WIN_COLS - 1)),
        "post_norm_g": 1.0 + 0.05 * nrm(ks[10], (DEPTH, D_MODEL)),
        "w_out": nrm(ks[11], (DEPTH, D_MIX, D_MODEL)) * D_MIX ** -0.5,
    }


def reference(x_prompt, x_sample, meta_tokens, pre_norm_g, w_in, conv_w, conv_b,
              conv_ln_g, conv_ln_b, rel_bias, post_norm_g, w_out):
    y_prompt = run_trunk(x_prompt, meta_tokens, pre_norm_g, w_in, conv_w, conv_b,
                         conv_ln_g, conv_ln_b, rel_bias, post_norm_g, w_out)
    y_sample = run_trunk(x_sample, meta_tokens, pre_norm_g, w_in, conv_w, conv_b,
                         conv_ln_g, conv_ln_b, rel_bias, post_norm_g, w_out)
    return (y_prompt, y_sample)
```

```python
import os
import numpy as np
import concourse.bass as bass
import concourse.mybir as mybir
from concourse.bass_utils import run_bass_kernel_spmd

F32 = mybir.dt.float32
BF16 = mybir.dt.bfloat16
I32 = mybir.dt.int32
MAGIC = 0x5f3759df
ALU = mybir.AluOpType
AF = mybir.ActivationFunctionType

D = 1024
NMETA = 16
GW = 64
SEG_ROWS = 16
HALO = 4
EXT_ROWS = SEG_ROWS + 2 * HALO
NE = EXT_ROWS * GW
NM = SEG_ROWS * GW
M0 = HALO * GW
NTILE = NE // 128
CONVW = 31
PAD = 15
NUG = NM + 2 * PAD
NEG = -30000.0
NCORES = 8
WRING = 8


class Res:
    __slots__ = ("name", "lw", "rd", "sem", "cnt")

    def __init__(self, name):
        self.name = name
        self.lw = None
        self.rd = []
        self.sem = None
        self.cnt = 0


class Group:
    def __init__(self, sem):
        self.sem = sem
        self.cnt = 0


class Op:
    __slots__ = ("eng", "fn", "deps", "signal", "val", "dsem", "dval", "grp", "xw")

    def __init__(self, eng, fn):
        self.eng = eng
        self.fn = fn
        self.deps = []
        self.signal = False
        self.val = 0
        self.dsem = None
        self.dval = 0
        self.grp = None
        self.xw = None


class Sched:
    ENGS = ("pe", "act", "dve", "pool", "sp")

    def __init__(self, nc):
        self.nc = nc
        self.ops = {e: [] for e in self.ENGS}
        self.esem = {}
        self.out_ops = []

    def _deps(self, op, reads, writes):
        deps = []
        for r in reads:
            if r.lw is not None:
                deps.append(r.lw)
        for w in writes:
            if w.lw is not None:
                deps.append(w.lw)
            deps.extend(w.rd)
        seen = set()
        for d in deps:
            if id(d) in seen or d is op:
                continue
            seen.add(id(d))
            if d.dsem is None and d.grp is None:
                if d.eng == "pe" and op.eng == "pe":
                    continue
                d.signal = True
            op.deps.append(d)
        for r in reads:
            r.rd.append(op)
        for w in writes:
            w.lw = op
            w.rd = []

    def add(self, eng, fn, reads=(), writes=()):
        op = Op(eng, fn)
        self._deps(op, reads, writes)
        self.ops[eng].append(op)
        return op

    def dma(self, q, fn, reads=(), writes=(), own=None, grp=None):
        op = Op(q, fn)
        self._deps(op, reads, writes)
        if own is not None:
            own.cnt += 16
            op.dsem = own.sem
            op.dval = own.cnt
        else:
            op.deps = [d for d in op.deps if d.grp is not grp]
            grp.cnt += 16
            op.grp = grp
            if q == "pool" and grp.cnt > 16 * 24:
                op.xw = (grp.sem, grp.cnt - 16 * 24)
        self.ops[q].append(op)
        return op

    def check_deadlock(self, sems, final_waits):
        for e in self.ENGS:
            c = 0
            for op in self.ops[e]:
                if op.dsem is None and op.grp is None and op.signal:
                    c += 1
                    op.val = c

        def ev(d):
            if d.dsem is not None:
                return id(d.dsem), d.dval
            if d.grp is not None:
                return id(d.grp.sem), d.grp.cnt
            return id(sems[d.eng]), d.val
        prog = {}
        for e in self.ENGS:
            lst = []
            for op in self.ops[e]:
                waits = [ev(d) for d in op.deps]
                if op.xw is not None:
                    waits.append((id(op.xw[0]), op.xw[1]))
                if op.dsem is not None:
                    inc = (id(op.dsem), 16)
                elif op.grp is not None:
                    inc = (id(op.grp.sem), 16)
                elif op.signal:
                    inc = (id(sems[e]), 1)
                else:
                    inc = None
                lst.append((waits, inc))
            if e == "sp":
                lst.append(([(id(s_), v) for s_, v in final_waits()], None))
            prog[e] = lst
        val = {}
        pc = {e: 0 for e in self.ENGS}
        progress = True
        while progress:
            progress = False
            for e in self.ENGS:
                while pc[e] < len(prog[e]):
                    waits, inc = prog[e][pc[e]]
                    if all(val.get(s_, 0) >= v for s_, v in waits):
                        if inc is not None:
                            val[inc[0]] = val.get(inc[0], 0) + inc[1]
                        pc[e] += 1
                        progress = True
                    else:
                        break
        stuck = {e: (pc[e], len(prog[e])) for e in self.ENGS if pc[e] < len(prog[e])}
        if stuck:
            names = {id(v): k for k, v in sems.items()}
            msg = []
            for e, (p, n) in stuck.items():
                waits, _ = prog[e][p]
                msg.append(f"{e} stuck at {p}/{n} waiting " + str([(names.get(s_, s_), v, val.get(s_, 0)) for s_, v in waits if val.get(s_, 0) < v]))
            raise RuntimeError("DEADLOCK in schedule: " + "; ".join(msg))

    def emit(self, block, sems, final_waits):
        nc = self.nc
        for e in self.ENGS:
            c = 0
            for op in self.ops[e]:
                if op.dsem is None and op.grp is None and op.signal:
                    c += 1
                    op.val = c
        esem = sems
        sched = self

        def ev(d):
            if d.dsem is not None:
                return d.dsem, d.dval
            if d.grp is not None:
                return d.grp.sem, d.grp.cnt
            return esem[d.eng], d.val

        def run(e, eng):
            known = {}
            for op in sched.ops[e]:
                for d in op.deps:
                    s, v = ev(d)
                    key = id(s)
                    if known.get(key, 0) < v:
                        eng.wait_ge(s, v)
                        known[key] = v
                if op.xw is not None:
                    eng.wait_ge(op.xw[0], op.xw[1])
                ins = op.fn(eng)
                if op.dsem is not None:
                    ins.then_inc(op.dsem, 16)
                elif op.grp is not None:
                    ins.then_inc(op.grp.sem, 16)
                elif op.signal:
                    ins.then_inc(esem[e], 1)
            if e == "sp":
                for s, v in final_waits():
                    eng.wait_ge(s, v)

        @block.tensor
        def _(eng):
            run("pe", eng)

        @block.scalar
        def _(eng):
            run("act", eng)

        @block.vector
        def _(eng):
            run("dve", eng)

        @block.gpsimd
        def _(eng):
            run("pool", eng)

        @block.sync
        def _(eng):
            run("sp", eng)


def tile_rows(p):
    lo = max(0, 2 * p - 7)
    hi = min(SEG_ROWS - 1, 2 * p + 1)
    rows = set(range(lo, hi + 1)) if lo <= hi else set()
    if p <= 5:
        rows |= set(range(0, 4))
    if p >= 6:
        rows |= set(range(12, 16))
    rows = sorted(rows)
    assert rows == list(range(rows[0], rows[-1] + 1))
    return rows[0], rows[-1]


def build_program(nseg):
    NJUNK = int(os.environ.get('KJUNK', '0'))
    STAGE = int(os.environ.get('KSTAGE', '9'))
    nc = bass.Bass("TRN2", target_bir_lowering=False)

    def din(name, shape, dt=F32):
        return nc.dram_tensor(name, list(shape), dt, kind="ExternalInput").ap()

    xe = din("xe", [nseg, NE, D])
    maskA = din("maskA", [nseg, 16, NE])
    win = din("win", [56, 128, 1024])
    wout = din("wout", [128, 16 * 1024])
    convwT = din("convwT", [128, 8 * CONVW])
    chvec = din("chvec", [128, 24])
    gpre = din("gpre", [128, D])
    gpost = din("gpost", [128, D])
    meta = din("meta", [NMETA, D])
    Bt = din("Bt", [128, 16 * 16 * 64])
    Mt = din("Mt", [128, 16 * 64])
    conehot = din("conehot", [16, NM])
    identf = din("ident", [128, 128])
    ye = nc.dram_tensor("ye", [nseg, NM, D], F32, kind="ExternalOutput").ap()
    DBG = int(os.environ.get('KDBG', '0'))
    if DBG:
        dbg_mix = nc.dram_tensor("dbg_mix", [128, 16 * NM], BF16, kind="ExternalOutput").ap()
        dbg_uT = nc.dram_tensor("dbg_uT", [128, 8 * NE], BF16, kind="ExternalOutput").ap()
        dbg_w = nc.dram_tensor("dbg_w", [128, 1024], BF16, kind="ExternalOutput").ap()
    win_bf = nc.dram_tensor("win_bf", [56, 128, 1024], BF16, kind="Internal").ap()
    wout_bf = nc.dram_tensor("wout_bf", [128, 16 * 1024], BF16, kind="Internal").ap()
    T_bf = nc.dram_tensor("T_bf", [8, 128, 2048], BF16, kind="Internal").ap()
    diag_bf = nc.dram_tensor("diag_bf", [8, 128, CONVW * 128], BF16, kind="Internal").ap()

    import contextlib
    es = contextlib.ExitStack()
    with es:
        def sb(name, shape, dt):
            return es.enter_context(nc.sbuf_tensor(name, list(shape), dt))

        def ps(name, shape, dt):
            return es.enter_context(nc.psum_tensor(name, list(shape), dt))

        def sem(name):
            return es.enter_context(nc.semaphore(name))

        uT = sb("uT", [128, 8 * NE], BF16)
        mixT = sb("mixT", [128, 16 * NM], BF16)
        wring = sb("wring", [128, WRING * 1024], BF16)
        xt = [sb(f"xt{i}", [128, D], F32) for i in range(2)]
        ubf = [sb(f"ubf{i}", [128, D], BF16) for i in range(2)]
        gpre_s = sb("gpre_s", [128, D], F32)
        gpost_s = sb("gpost_s", [128, D], F32)
        ident_s = sb("ident_s", [128, 128], BF16)
        ones_s = sb("ones_s", [128, 128], BF16)
        kmeta = sb("kmeta", [64, 8 * 2 * 16], BF16)
        vmA = sb("vmA", [16, 8 * 128], BF16)
        vmB = sb("vmB", [16, 8 * 128], BF16)
        uTm = sb("uTm", [128, 8 * 16], BF16)
        chv = sb("chv", [128, 24], F32)
        cwT = sb("cwT", [128, 8 * CONVW], F32)
        halfg = sb("halfg", [128, 8], F32)
        halfb = sb("halfb", [128, 8], F32)
        stat = [sb(f"stat{i}", [128, 8], F32) for i in range(4)]
        A16N = 41984
        A32N = 4096
        ar16 = sb("ar16", [128, A16N], BF16)
        ar32 = sb("ar32", [128, A32N], F32)
        G16 = 512
        G32 = 256
        r16 = [Res(f"a16_{i}") for i in range((A16N + G16 - 1) // G16)]
        r32 = [Res(f"a32_{i}") for i in range((A32N + G32 - 1) // G32)]

        class Buf:
            def __init__(self, ap, res):
                self.ap = ap
                self.res = res

        def a16(off, n, parts=128):
            return Buf(ar16[0:parts, off:off + n], r16[off // G16:(off + n - 1) // G16 + 1])

        def a32(off, n, parts=128):
            return Buf(ar32[0:parts, off:off + n], r32[off // G32:(off + n - 1) // G32 + 1])

        UGP = 1056
        tmpb = [a16(i * 512, 512) for i in range(8)]
        UG0 = 4096 + NTILE * 1536
        gc2 = [a16(UG0 + j * 1024, 1024) for j in range(8)]
        ugT = [a16(UG0 + 8192 + i * UGP, UGP) for i in range(2)]
        DG0 = UG0 + 8192 + 2 * UGP
        diag = [a16(DG0 + i * 3968, 3968) for i in range(2)]
        assert DG0 + 2 * 3968 <= A16N
        stA = a32(0, 1024)
        stB = a32(1024, 1024)
        tf = [a32(2048 + i * 512, 512) for i in range(3)]
        Ttab = [a16(i * 2048, 2048) for i in range(2)]
        VW = 1536
        V0 = 4096
        Vt = [a16(V0 + t * VW, VW) for t in range(NTILE)]
        QK0 = V0 + NTILE * VW
        QKS = 2 * 1024 + 2 * NE + 1024
        QT = [[a16(QK0 + i * QKS + hh * 1024, 1024) for hh in range(2)] for i in range(2)]
        KT = [[a16(QK0 + i * QKS + 2048 + hh * NE, NE) for hh in range(2)] for i in range(2)]
        AG = [a16(QK0 + i * QKS + 2048 + 2 * NE, 1024) for i in range(2)]
        EX0 = QK0 + 2 * QKS
        expS = [a16(EX0 + i * 768, 768) for i in range(2)]
        PT = [a16(EX0 + 1536 + i * 768, 768) for i in range(2)]
        PM0 = EX0 + 3072
        PmT = [[a16(PM0 + i * 2048 + hh * 1024, 1024, parts=16) for hh in range(2)] for i in range(2)]
        assert PM0 + 4096 <= A16N, (PM0, A16N)
        Rb = [a32(i * 512, 512) for i in range(2)]
        attb = [a32(1024 + i * 512, 512) for i in range(2)]
        woutS = a16(0, 16384)
        outb = [a32(i * 1024, 1024) for i in range(2)]
        xres = [a32(2048 + i * 1024, 1024) for i in range(2)]
        btmp = a32(0, 2048)
        btmp2 = a32(2048, 2048)
        tbuild = a16(0, 2048)
        dbuild = a16(4096, 3968)
        xmeta = a32(0, 1024, parts=16)

        S0 = ps("S0", [128, 1024], F32)
        S1 = ps("S1", [128, 1024], F32)
        O0 = ps("O0", [128, 512], F32)
        O1 = ps("O1", [128, 512], F32)
        PJ = ps("PJ", [128, 512], F32)
        TB = ps("TB", [128, 1024], BF16)
        rS0a, rS0b, rS1a, rS1b, rO0, rO1, rPJ, rTB = [Res(n) for n in
                                                    ("S0a", "S0b", "S1a", "S1b", "O0", "O1", "PJ", "TB")]
        rPJa, rPJb = rPJ, Res("PJb")
        banks = [(S0[:, 0:512], rS0a), (S0[:, 512:1024], rS0b), (S1[:, 0:512], rS1a),
                 (S1[:, 512:1024], rS1b), (O0[:, :], rO0), (O1[:, :], rO1), (PJ[:, :], rPJ)]
        Sbuf = [(S0, [rS0a, rS0b]), (S1, [rS1a, rS1b])]
        Obuf = [(O0, rO0), (O1, rO1)]

        esem = {e: sem("s_" + e) for e in Sched.ENGS}
        sch = Sched(nc)
        ginit = Group(sem("g_init"))
        gconst = Group(sem("g_const"))

        def own(r, name):
            r.sem = sem(name)
            return r

        r_xt = [own(Res(f"xt{i}"), f"d_xt{i}") for i in range(2)]
        r_ubf = [Res(f"ubf{i}") for i in range(2)]
        r_w = [own(Res(f"w{i}"), f"d_w{i}") for i in range(WRING)]
        r_mask = [own(Res(f"mask{i}"), f"d_mask{i}") for i in range(8)]
        r_uT = [Res(f"uT{t}") for t in range(NTILE)]
        r_mix = [[Res(f"mix{e}_{g}") for g in range(2)] for e in range(16)]
        r_stat = [Res(f"stat{i}") for i in range(4)]
        r_const = Res("const")
        r_const2 = Res("const2")
        r_scr = Res("scratch")
        r_meta = Res("metabufs")
        r_T = [own(Res(f"T{i}"), f"d_T{i}") for i in range(2)]
        r_diag = [own(Res(f"dg{i}"), f"d_dg{i}") for i in range(2)]
        r_wout = own(Res("woutS"), "d_wout")
        r_xres = [own(Res(f"xres{i}"), f"d_xres{i}") for i in range(2)]
        r_out = [own(Res(f"out{i}"), f"d_out{i}") for i in range(2)]
        r_init = own(Res("initbuf"), "d_initbuf")
        r_tb = own(Res("tb"), "d_tb")
        r_db = own(Res("db"), "d_db")
        r_Tscr = [Res(f"Tscr{i}") for i in range(8)]
        r_dscr = [Res(f"dscr{i}") for i in range(8)]

        def mm(out, lhsT, rhs, start, stop, reads, writes, **kw):
            return sch.add("pe", lambda e: e.matmul(out, lhsT=lhsT, rhs=rhs, start=start, stop=stop, **kw),
                           reads, writes)

        def act(out, in_, func, reads, writes, **kw):
            return sch.add("act", lambda e: e.activation(out=out, in_=in_, func=func, **kw), reads, writes)

        def tt(eng, out, in0, in1, op, reads, writes):
            return sch.add(eng, lambda e: e.tensor_tensor(out=out, in0=in0, in1=in1, op=op), reads, writes)

        def ts(eng, out, in0, s1, s2, op0, op1, reads, writes):
            if s2 is None:
                return sch.add(eng, lambda e: e.tensor_scalar(out=out, in0=in0, scalar1=s1, scalar2=None, op0=op0),
                               reads, writes)
            return sch.add(eng, lambda e: e.tensor_scalar(out=out, in0=in0, scalar1=s1, scalar2=s2, op0=op0, op1=op1),
                           reads, writes)

        def stt(eng, out, in0, scalar, in1, op0, op1, reads, writes):
            return sch.add(eng, lambda e: e.scalar_tensor_tensor(out=out, in0=in0, scalar=scalar, in1=in1,
                                                                 op0=op0, op1=op1), reads, writes)

        def rsqrt(a, r, w, ra, rr, rw):
            ri = r.bitcast(I32)
            ts("dve", ri, a.bitcast(I32), 1, None, ALU.arith_shift_right, None, ra, rr)
            ts("dve", ri, ri, -1, MAGIC, ALU.mult, ALU.add, rr, rr)
            for _ in range(3):
                tt("dve", w, a, r, ALU.mult, ra + rr, rw)
                tt("dve", w, w, r, ALU.mult, rw + rr, rw)
                ts("dve", w, w, -0.5, 1.5, ALU.mult, ALU.add, rw, rw)
                tt("dve", r, r, w, ALU.mult, rr + rw, rr)

        def rsqrt_col(st, n, rr):
            a, r, h, t = st[0:n, 1:2], st[0:n, 2:3], st[0:n, 3:4], st[0:n, 4:5]
            ri = r.bitcast(I32)
            ts("dve", ri, a.bitcast(I32), 1, None, ALU.arith_shift_right, None, rr, rr)
            ts("dve", ri, ri, -1, MAGIC, ALU.mult, ALU.add, rr, rr)
            ts("dve", h, a, -0.5, None, ALU.mult, None, rr, rr)
            for _ in range(3):
                ts("dve", t, h, r, r, ALU.mult, ALU.mult, rr, rr)
                stt("dve", r, t, 1.5, r, ALU.add, ALU.mult, rr, rr)

        def cp(eng, out, in_, reads, writes):
            if eng == "act":
                return sch.add("act", lambda e: e.copy(out=out, in_=in_), reads, writes)
            return sch.add(eng, lambda e: e.tensor_copy(out=out, in_=in_), reads, writes)

        def dma(q, out, in_, reads, writes, own=None, grp=None):
            return sch.dma(q, lambda e: e.dma_start(out=out, in_=in_), reads, writes, own=own, grp=grp)

        def uTv(k, a, b):
            return uT[:, k * NE + a:k * NE + b]

        def mixv(e, a, b):
            return mixT[:, e * NM + a:e * NM + b]

        def wv(slot, k):
            return wring[:, slot * 1024 + k * 128:slot * 1024 + (k + 1) * 128]

        wseq = []
        for s in range(nseg):
            for j in range(8):
                wseq += [8 + j, j]
            for j in range(8):
                wseq += [16 + j]
            while len(wseq) % 4:
                wseq.append(None)
            wseq += [40, 41, 42, 43, 44, 45, 46, 47]
            for hp in range(8):
                wseq += [24 + hp, 32 + hp, 48 + hp]
        wpre = [32 + hp for hp in range(8)] + [40, 41, 42, 43, 44, 45, 46, 47]
        wseq = wpre + wseq
        wstate = {"loaded": 0, "used": 0}

        def w_load_upto(n):
            while wstate["loaded"] < min(n, len(wseq)):
                i = wstate["loaded"]
                e = wseq[i]
                if e is not None:
                    slot = i % WRING
                    dma("sp", wring[:, slot * 1024:(slot + 1) * 1024], win_bf[e],
                        [r_scr], [r_w[slot]], own=r_w[slot])
                wstate["loaded"] += 1

        def w_next(expect):
            i = wstate["used"]
            while wseq[i] is None:
                i += 1
            assert wseq[i] == expect, (i, wseq[i], expect)
            w_load_upto(i + 1)
            wstate["used"] = i + 1
            return i % WRING

        def w_prefetch(k=4):
            assert k <= WRING
            w_load_upto(wstate["used"] + k)

        class StopBuild(Exception):
            pass

        def stage(k):
            if STAGE < k:
                raise StopBuild()

        try:
            for e in range(int(os.environ.get('KNW', '56'))):
                dma("pool", win_bf[e], win[e], [], [r_scr], grp=ginit)
            for q in range(int(os.environ.get('KNO', '16'))):
                dma("pool", wout_bf[:, q * 1024:(q + 1) * 1024], wout[:, q * 1024:(q + 1) * 1024], [], [r_scr], grp=ginit)
            stage(-3)
            dma("sp", gpre_s[:], gpre[:], [], [r_const], grp=gconst)
            dma("sp", gpost_s[:], gpost[:], [], [r_const], grp=gconst)
            dma("sp", chv[:], chvec[:], [], [r_const], grp=gconst)
            dma("sp", cwT[:], convwT[:], [], [r_const], grp=gconst)
            dma("pool", ident_s[:], identf[:], [], [r_const], grp=gconst)
            Mt_s = sb("Mt_s", [128, 1024], F32)
            dma("sp", Mt_s[:], Mt[:], [], [r_const], grp=gconst)
            sch.add("dve", lambda e: e.memset(ones_s[:], 1.0 / 1024.0), [], [r_const2])
            sch.add("dve", lambda e: e.memset(vmA[:], 1.0), [], [r_meta])
            sch.add("dve", lambda e: e.memset(vmB[:], 1.0), [], [r_meta])
            ts("dve", halfg[:], chv[:, 8:16], 0.5, None, ALU.mult, None, [r_const], [r_const2])
            ts("dve", halfb[:], chv[:, 16:24], 0.5, None, ALU.mult, None, [r_const], [r_const2])

            stage(-2)
            for hp in range(8):
                dma("sp", btmp.ap, Bt[:, hp * 2048:(hp + 1) * 2048], [], btmp.res, own=r_init)
                act(btmp2.ap, btmp.ap, AF.Exp, btmp.res, btmp2.res)
                for hh in range(2):
                    tt("dve", tbuild.ap[:, hh * 1024:(hh + 1) * 1024], btmp2.ap[:, hh * 1024:(hh + 1) * 1024],
                       Mt_s[:], ALU.mult, btmp2.res + [r_const], tbuild.res)
                dma("sp", T_bf[hp], tbuild.ap, tbuild.res, [r_Tscr[hp]], own=r_tb)
            stage(-1)
            for j in range(8):
                for s in range(CONVW):
                    eng = "dve" if (s % 2 == 0) else "pool"
                    ts(eng, dbuild.ap[:, s * 128:(s + 1) * 128], ident_s[:],
                       cwT[:, j * CONVW + s:j * CONVW + s + 1], 0.5, ALU.mult, ALU.mult,
                       [r_const], dbuild.res)
                dma("sp", diag_bf[j], dbuild.ap, dbuild.res, [r_dscr[j]], own=r_db)

            cnt = {"x": 0, "st": 0, "cpy": 0, "pj": 0}

            def p0_tile(src_ap, nrows, dst_fn, dst_res, xbuf=None, xres_=None):
                i = cnt["x"] % 2
                cnt["x"] += 1
                si = cnt["st"] % 4
                cnt["st"] += 1
                if xbuf is None:
                    xb, xr = xt[i][0:nrows, :], [r_xt[i]]
                    dma("sp", xb, src_ap, [], xr, own=r_xt[i])
                else:
                    xb, xr = xbuf, xres_
                ub = ubf[i][0:nrows, :]
                st = stat[si]
                act(ub, xb, AF.Square, xr, [r_ubf[i], r_stat[si]], accum_out=st[0:nrows, 0:1])
                ts("dve", st[0:nrows, 1:2], st[0:nrows, 0:1], 1.0 / D, 1e-6, ALU.mult, ALU.add, [r_stat[si]], [r_stat[si]])
                rsqrt_col(st, nrows, [r_stat[si]])
                stt("dve", ub, xb, st[0:nrows, 2:3], gpre_s[0:nrows, :], ALU.mult, ALU.mult,
                    xr + [r_stat[si], r_const], [r_ubf[i]])
                for k in range(8):
                    sch.add("pe", lambda e, k=k: e.transpose(out=TB[:, k * 128:k * 128 + nrows],
                                                               in_=ubf[i][0:nrows, k * 128:(k + 1) * 128],
                                                               identity=ident_s[0:nrows, 0:nrows]),
                            [r_ubf[i], r_const], [rTB])
                ceng = "act" if cnt["cpy"] % 2 == 0 else "dve"
                cnt["cpy"] += 1
                src = TB[:].rearrange("p (k t) -> p k t", t=128)[:, :, 0:nrows]
                cp(ceng, dst_fn(), src, [rTB], dst_res)

            stage(1)
            ssb = sb("ssb", [128, 64], F32)
            r_ssb = Res("ssb")

            def p0_a(sgi):
                for t in range(NTILE):
                    i = cnt["x"] % 2
                    cnt["x"] += 1
                    dma("sp", xt[i][:], xe[sgi, t * 128:(t + 1) * 128, :], [], [r_xt[i]], own=r_xt[i])
                    act(ubf[i][:], xt[i][:], AF.Square, [r_xt[i]], [r_ubf[i], r_ssb], accum_out=ssb[:, t:t + 1])
                a_, r_, h_, t_ = ssb[:, 16:28], ssb[:, 32:44], ssb[:, 48:60], ssb[:, 0:12]
                rr = [r_ssb]
                ts("dve", a_, ssb[:, 0:12], 1.0 / D, 1e-6, ALU.mult, ALU.add, rr, rr)
                ri = r_.bitcast(I32)
                ts("dve", ri, a_.bitcast(I32), 1, None, ALU.arith_shift_right, None, rr, rr)
                ts("dve", ri, ri, -1, MAGIC, ALU.mult, ALU.add, rr, rr)
                ts("dve", h_, a_, -0.5, None, ALU.mult, None, rr, rr)
                for _ in range(3):
                    tt("dve", t_, h_, r_, ALU.mult, rr, rr)
                    tt("dve", t_, t_, r_, ALU.mult, rr, rr)
                    stt("dve", r_, t_, 1.5, r_, ALU.add, ALU.mult, rr, rr)

            def p0_c(sgi, tiles):
                for t in tiles:
                    i = cnt["x"] % 2
                    cnt["x"] += 1
                    dma("sp", xt[i][:], xe[sgi, t * 128:(t + 1) * 128, :], [], [r_xt[i]], own=r_xt[i])
                    stt("dve", ubf[i][:], xt[i][:], ssb[:, 32 + t:33 + t], gpre_s[:], ALU.mult, ALU.mult,
                        [r_xt[i], r_ssb, r_const], [r_ubf[i]])
                    if t % 2 == 0:
                        tbank, tres = TB[:], [rTB]
                    else:
                        tbank, tres = PJ[:].bitcast(BF16), [rPJa, rPJb]
                    for k in range(8):
                        sch.add("pe", lambda e, k=k, i=i, tbank=tbank: e.transpose(out=tbank[:, k * 128:(k + 1) * 128],
                                                                                    in_=ubf[i][:, k * 128:(k + 1) * 128],
                                                                                    identity=ident_s[:, :]),
                                [r_ubf[i], r_const], tres)
                    ceng = "act" if t % 2 == 0 else "dve"
                    src = tbank.rearrange("p (k t) -> p k t", t=128)
                    dst = uT[:].rearrange("p (k n) -> p k n", n=NE)[:, :, t * 128:(t + 1) * 128]
                    cp(ceng, dst, src, tres, [r_uT[t]])

            def p0_batched(sgi):
                p0_a(sgi)
                p0_c(sgi, range(NTILE))

            dma("sp", xmeta.ap, meta[:], [], xmeta.res, own=r_init)
            p0_tile(None, NMETA, lambda: uTm[:].rearrange("p (k t) -> p k t", t=16), [r_meta],
                    xbuf=xmeta.ap, xres_=xmeta.res)
            for hp in range(8):
                slot = w_next(32 + hp)
                for k in range(8):
                    mm(PJ[:, 0:16], wv(slot, k), uTm[:, k * 16:(k + 1) * 16], k == 0, k == 7,
                       [r_w[slot], r_meta], [rPJ])
                cp("dve", kmeta[0:64, hp * 32:hp * 32 + 16], PJ[0:64, 0:16], [rPJ], [r_meta])
                cp("dve", kmeta[0:64, hp * 32 + 16:hp * 32 + 32], PJ[64:128, 0:16], [rPJ], [r_meta])
                w_prefetch(4)
            for g in range(2):
                slots = [w_next(40 + 4 * g + q) for q in range(4)]
                assert slots[0] % 4 == 0
                for k in range(8):
                    rhs = wring[:, slots[0] * 1024 + k * 512:slots[0] * 1024 + (k + 1) * 512]
                    mm(PJ[0:16, 0:512], uTm[:, k * 16:(k + 1) * 16], rhs, k == 0, k == 7,
                       [r_w[s_] for s_ in slots] + [r_meta], [rPJ])
                if g == 0:
                    cp("dve", vmA[0:16, :].rearrange("p (i c) -> p i c", c=128)[:, :, 0:64],
                       PJ[0:16, 0:512].rearrange("p (i c) -> p i c", c=64), [rPJ], [r_meta])
                else:
                    cp("dve", vmB[0:16, :].rearrange("p (i c) -> p i c", c=128)[:, :, 64:128],
                       PJ[0:16, 0:512].rearrange("p (i c) -> p i c", c=64), [rPJ], [r_meta])

            rot = {"pair": 0, "S": 0, "ex": 0, "out": 0, "tmp": 0, "tf": 0, "dg": 0, "ug": 0, "T": 0, "mask": 0}

            for sgi in range(nseg):
                mi = rot["mask"] % 2
                rot["mask"] += 1

                stage(2)
                if sgi == 0:
                    p0_batched(sgi)
                all_uT = list(r_uT)
                if DBG and sgi == nseg - 1:
                    dma("sp", dbg_uT[:, :], uT[:, :], all_uT, [], own=r_init)
                    dma("sp", dbg_w[:, :], win_bf[8], [r_scr], [], own=r_init)

                stage(3)
                ug_groups = [(M0 - PAD, 512), (M0 - PAD + 512, 512), (M0 - PAD + 1024, 2 * PAD)]
                w_prefetch(4)
                for j in range(8):
                    ui = rot["ug"] % 2
                    rot["ug"] += 1
                    sg_ = w_next(8 + j)
                    sv_ = w_next(j)
                    di = rot["dg"] % 2
                    rot["dg"] += 1
                    dma("sp", diag[di].ap, diag_bf[j], [r_dscr[j]], diag[di].res + [r_diag[di]], own=r_diag[di])
                    for gi, (e0, n) in enumerate(ug_groups):
                        bg, rg = banks[(2 * gi) % 4]
                        bv, rv = banks[(2 * gi + 1) % 4]
                        for k in range(8):
                            mm(bg[:, 0:n], wv(sg_, k), uTv(k, e0, e0 + n), k == 0, k == 7, [r_w[sg_]] + all_uT, [rg])
                        for k in range(8):
                            mm(bv[:, 0:n], wv(sv_, k), uTv(k, e0, e0 + n), k == 0, k == 7, [r_w[sv_]] + all_uT, [rv])
                        tb = tmpb[rot["tmp"] % 8]
                        rot["tmp"] += 1
                        act(tb.ap[:, 0:n], bg[:, 0:n], AF.Tanh, [rg], tb.res, scale=0.5)
                        o0 = e0 - (M0 - PAD)
                        stt("dve", ugT[ui].ap[:, o0:o0 + n], tb.ap[:, 0:n], 1.0, bv[:, 0:n], ALU.add, ALU.mult,
                            tb.res + [rv], ugT[ui].res)
                    for g in range(2):
                        by, ry = Obuf[g]
                        for s in range(CONVW):
                            mm(by[:, :], diag[di].ap[:, s * 128:(s + 1) * 128],
                               ugT[ui].ap[:, g * 512 + s:g * 512 + s + 512], s == 0, s == CONVW - 1,
                               diag[di].res + [r_diag[di]] + ugT[ui].res, [ry])
                        act(mixv(j, g * 512, (g + 1) * 512), by[:, :], AF.Identity, [ry, r_const], [r_mix[j][g]],
                            bias=chv[:, j:j + 1])
                    if g == 1:
                        w_prefetch(4)
                stage(4)
                for g in range(2):
                    bm, rm = banks[0]
                    bq, rq = banks[1]
                    for j in range(8):
                        tb = tmpb[rot["tmp"] % 8]
                        rot["tmp"] += 1
                        act(tb.ap, mixv(j, g * 512, (g + 1) * 512), AF.Square, [r_mix[j][g]], tb.res)
                        mm(bm, ones_s[:], mixv(j, g * 512, (g + 1) * 512), j == 0, j == 7, [r_const2, r_mix[j][g]], [rm])
                        mm(bq, ones_s[:], tb.ap, j == 0, j == 7, [r_const2] + tb.res, [rq])
                    A_ = stA.ap[:, g * 512:(g + 1) * 512]
                    B_ = stB.ap[:, g * 512:(g + 1) * 512]
                    t0 = tf[0]
                    act(t0.ap, bm, AF.Square, [rm], t0.res)
                    tt("dve", t0.ap, bq, t0.ap, ALU.subtract, [rq] + t0.res, t0.res)
                    ts("dve", t0.ap, t0.ap, 1e-5, None, ALU.add, None, t0.res, t0.res)
                    rsqrt(t0.ap, A_, tf[1].ap, t0.res, stA.res, tf[1].res)
                    stt("dve", B_, bm, -1.0, A_, ALU.mult, ALU.mult, [rm] + stA.res, stB.res)
                stage(5)
                w_prefetch(4)
                for j in range(8):
                    sc_ = w_next(16 + j)
                    for g in range(2):
                        bc, rc = banks[2 + g]
                        for k in range(8):
                            mm(bc, wv(sc_, k), uTv(k, M0 + g * 512, M0 + (g + 1) * 512), k == 0, k == 7,
                               [r_w[sc_]] + all_uT, [rc])
                        tb = tmpb[rot["tmp"] % 8]
                        rot["tmp"] += 1
                        act(tb.ap, bc, AF.Tanh, [rc], tb.res, scale=0.5)
                        stt("dve", gc2[j].ap[:, g * 512:(g + 1) * 512], tb.ap, 1.0, bc, ALU.add, ALU.mult,
                            tb.res + [rc], gc2[j].res)
                    w_prefetch(4)

                def ln_apply(j, g):
                    A_ = stA.ap[:, g * 512:(g + 1) * 512]
                    B_ = stB.ap[:, g * 512:(g + 1) * 512]
                    t1 = tf[1 + (rot["tf"] % 2)]
                    rot["tf"] += 1
                    mv = mixv(j, g * 512, (g + 1) * 512)
                    tt("pool", t1.ap, mv, A_, ALU.mult, [r_mix[j][g]] + stA.res, t1.res)
                    tt("pool", t1.ap, t1.ap, B_, ALU.add, t1.res + stB.res, t1.res)
                    tb2 = tmpb[rot["tmp"] % 8]
                    rot["tmp"] += 1
                    tb3 = tmpb[rot["tmp"] % 8]
                    rot["tmp"] += 1
                    act(tb2.ap, t1.ap, AF.Tanh, t1.res + [r_const2], tb2.res,
                        scale=halfg[:, j:j + 1], bias=halfb[:, j:j + 1])
                    act(tb3.ap, t1.ap, AF.Identity, t1.res + [r_const], tb3.res,
                        scale=chv[:, 8 + j:9 + j], bias=chv[:, 16 + j:17 + j])
                    stt("dve", t1.ap, tb2.ap, 1.0, tb3.ap, ALU.add, ALU.mult, tb2.res + tb3.res, t1.res)
                    stt("dve", mv, t1.ap, 0.25, gc2[j].ap[:, g * 512:(g + 1) * 512], ALU.mult, ALU.mult,
                        t1.res + gc2[j].res, [r_mix[j][g]])

                stage(6)
                for t in range(NTILE):
                    sch.add("pool", lambda e, t=t: e.memset(
                        Vt[t].ap.rearrange("p (i c) -> p i c", c=192)[:, :, 64:128], 1.0), [], Vt[t].res)
                vsl = [[w_next(40 + 4 * g + q) for q in range(4)] for g in range(2)]
                assert vsl[0][0] % 4 == 0 and vsl[1][0] % 4 == 0
                lnq = [(j, g) for j in range(8) for g in range(2)]
                nv = 0
                for g in range(2):
                    s0 = vsl[g][0]
                    for t in range(NTILE):
                        bv, rv = banks[4 + (t % 2)]
                        for k in range(8):
                            rhs = wring[:, s0 * 1024 + k * 512:s0 * 1024 + (k + 1) * 512]
                            mm(bv, uTv(k, t * 128, (t + 1) * 128), rhs, k == 0, k == 7,
                               [r_w[s_] for s_ in vsl[g]] + [r_uT[t]], [rv])
                        v3 = Vt[t].ap.rearrange("p (i c) -> p i c", c=192)
                        dst = v3[:, :, 0:64] if g == 0 else v3[:, :, 128:192]
                        cp("act", dst, bv.rearrange("p (i c) -> p i c", c=64), [rv], Vt[t].res)
                        nv += 1
                        while lnq and (16 - len(lnq)) * 24 < nv * 16:
                            ln_apply(*lnq.pop(0))
                while lnq:
                    ln_apply(*lnq.pop(0))
                w_prefetch(4)

                O3 = [(O0[:, :], rO0), (O1[:, :], rO1), (TB[:].bitcast(F32), rTB)]

                def pair_tasks(hp):
                    pi = hp % 2
                    ti = hp % 2
                    bp, rp = banks[6]
                    st_ = {"n": 0}
                    tasks = []
                    rotb = [banks[6], banks[1], banks[3]]

                    def nextbank():
                        st_["n"] += 1
                        return rotb[st_["n"] % 3]

                    def setup():
                        st_["q"] = w_next(24 + hp)
                        st_["k"] = w_next(32 + hp)
                        st_["a"] = w_next(48 + hp)
                        dma("sp", Ttab[ti].ap, T_bf[hp], [r_Tscr[hp]], Ttab[ti].res + [r_T[ti]], own=r_T[ti])

                    def qgrp(g):
                        bp, rp = nextbank()
                        sq_ = st_["q"]
                        for k in range(8):
                            mm(bp, wv(sq_, k), uTv(k, M0 + g * 512, M0 + (g + 1) * 512), k == 0, k == 7,
                               [r_w[sq_]] + all_uT, [rp])
                        cp("act", QT[pi][0].ap[0:64, g * 512:(g + 1) * 512], bp[0:64, :], [rp], QT[pi][0].res)
                        cp("act", QT[pi][1].ap[0:64, g * 512:(g + 1) * 512], bp[64:128, :], [rp], QT[pi][1].res)

                    def kgrp(g):
                        bp, rp = nextbank()
                        sk_ = st_["k"]
                        for k in range(8):
                            mm(bp, wv(sk_, k), uTv(k, g * 512, (g + 1) * 512), k == 0, k == 7,
                               [r_w[sk_]] + all_uT, [rp])
                        cp("dve", KT[pi][0].ap[0:64, g * 512:(g + 1) * 512], bp[0:64, :], [rp], KT[pi][0].res)
                        cp("act", KT[pi][1].ap[0:64, g * 512:(g + 1) * 512], bp[64:128, :], [rp], KT[pi][1].res)

                    def agrp(g):
                        bp, rp = nextbank()
                        sa_ = st_["a"]
                        for k in range(8):
                            mm(bp, wv(sa_, k), uTv(k, M0 + g * 512, M0 + (g + 1) * 512), k == 0, k == 7,
                               [r_w[sa_]] + all_uT, [rp])
                        act(AG[pi].ap[:, g * 512:(g + 1) * 512], bp, AF.Tanh, [rp], AG[pi].res, scale=0.5)
                        stt("dve", AG[pi].ap[:, g * 512:(g + 1) * 512], AG[pi].ap[:, g * 512:(g + 1) * 512], 1.0, bp,
                            ALU.add, ALU.mult, AG[pi].res + [rp], AG[pi].res)
                        if g == 1:
                            w_prefetch(4)

                    def mgrp(hh):
                        pm = PmT[pi][hh]
                        for b in range(2):
                            mm(bp[0:16, :], kmeta[0:64, hp * 32 + hh * 16:hp * 32 + hh * 16 + 16],
                               QT[pi][hh].ap[0:64, b * 512:(b + 1) * 512], True, True,
                               [r_meta] + QT[pi][hh].res, [rp])
                            act(pm.ap[:, b * 512:(b + 1) * 512], bp[0:16, :], AF.Exp, [rp], pm.res, scale=0.125)

                    tasks.append(lambda: (setup(), qgrp(0)))
                    tasks.append(lambda: qgrp(1))
                    for g in range(3):
                        tasks.append(lambda g=g: kgrp(g))
                    for g in range(2):
                        tasks.append(lambda g=g: agrp(g))
                    for hh in range(2):
                        tasks.append(lambda hh=hh: mgrp(hh))
                    return tasks

                def pair_proj(hp):
                    for t_ in pair_tasks(hp):
                        t_()

                steps = [(hp, hh, p) for hp in range(8) for hh in range(2) for p in range(NTILE)]

                def step_geom(p):
                    rlo, rhi = tile_rows(p)
                    n = (rhi - rlo + 1) * 64
                    chunks = []
                    c0 = 0
                    while c0 < n:
                        cn = min(512, n - c0)
                        chunks.append((c0, cn))
                        c0 += cn
                    return rlo, rhi, n, chunks

                def emit_qk(idx):
                    hp, hh, p = steps[idx]
                    pi = hp % 2
                    hb = 64 * hh
                    rlo, rhi, n, chunks = step_geom(p)
                    Sb, rS = Sbuf[idx % 2]
                    for _jk in range(NJUNK):
                        mm(Sb[:, 0:512], ident_s[:, :], uTv(0, 0, 512), True, True, [r_const] + all_uT, [rS[0]])
                    for ci, (c0, cn) in enumerate(chunks):
                        q0 = rlo * 64 + c0
                        mm(Sb[:, c0:c0 + cn], KT[pi][hh].ap[0:80, p * 128:(p + 1) * 128],
                           QT[pi][hh].ap[0:80, q0:q0 + cn], True, True,
                           KT[pi][hh].res + QT[pi][hh].res, [rS[ci]])

                def obank(hp, hh, b):
                    return O3[(2 * (2 * hp + hh) + b) % 3]

                def emit_expmult(idx):
                    hp, hh, p = steps[idx]
                    ti = hp % 2
                    rlo, rhi, n, chunks = step_geom(p)
                    Sb, rS = Sbuf[idx % 2]
                    xi = idx % 2
                    rSu = rS[0:len(chunks)]
                    act(expS[xi].ap[:, 0:n], Sb[:, 0:n], AF.Exp, rSu, expS[xi].res, scale=0.125)
                    slot0 = 11 - 2 * p + rlo
                    tab = Ttab[ti].ap[:, hh * 1024 + slot0 * 64:hh * 1024 + slot0 * 64 + n]
                    tt("dve", PT[xi].ap[:, 0:n], expS[xi].ap[:, 0:n], tab, ALU.mult,
                       expS[xi].res + Ttab[ti].res + [r_T[ti]], PT[xi].res)

                def emit_pv(idx):
                    hp, hh, p = steps[idx]
                    pi = hp % 2
                    hb = 64 * hh
                    pm = PmT[pi][hh]
                    rlo, rhi, n, chunks = step_geom(p)
                    xi = idx % 2
                    if p == 0:
                        if hh == 0:
                            vmeta = vmA[0:16, hp * 128:(hp + 1) * 128]
                            pmrows = (0, 16)
                        else:
                            vmeta = vmB[0:16, hp * 128:(hp + 1) * 128]
                            pmrows = (0, 16)
                        for b in range(2):
                            ob, ro = obank(hp, hh, b)
                            mm(ob, vmeta, pm.ap[pmrows[0]:pmrows[1], b * 512:(b + 1) * 512], True, False,
                               [r_meta] + pm.res, [ro], skip_group_check=True)
                    vt = Vt[p].ap
                    lhs = vt[:, 192 * hp + 64 * hh:192 * hp + 64 * hh + 128]
                    for b in range(2):
                        lo = max(rlo, 8 * b)
                        hi = min(rhi, 8 * b + 7)
                        if lo > hi:
                            continue
                        ob, ro = obank(hp, hh, b)
                        mm(ob[:, (lo - 8 * b) * 64:(hi - 8 * b + 1) * 64], lhs,
                           PT[xi].ap[:, (lo - rlo) * 64:(hi - rlo + 1) * 64], False, True,
                           Vt[p].res + PT[xi].res, [ro], skip_group_check=True)
                    for b, plast in ((0, 7), (1, NTILE - 1)):
                        if p != plast:
                            continue
                        ob, ro = obank(hp, hh, b)
                        o_lo, o_hi = hb, hb + 64
                        d_lo, d_hi = 64 - hb, 128 - hb
                        R = Rb[b]
                        at = attb[b]
                        for c4 in range(4):
                            cs = slice(c4 * 128, (c4 + 1) * 128)
                            pending.append(lambda ob=ob, R=R, ro=ro, cs=cs, d_lo=d_lo, d_hi=d_hi, o_lo=o_lo, o_hi=o_hi:
                                           sch.add("dve", lambda e: e.reciprocal(out=R.ap[o_lo:o_hi, cs], in_=ob[d_lo:d_hi, cs]),
                                                   [ro], R.res))

                        def fin(ob=ob, ro=ro, R=R, at=at, o_lo=o_lo, o_hi=o_hi, hp=hp, b=b, pi=pi):
                            act(at.ap[o_lo:o_hi, :], ob[o_lo:o_hi, :], AF.Identity, [ro], at.res, scale=0.5)
                            tt("pool", at.ap[o_lo:o_hi, :], at.ap[o_lo:o_hi, :], R.ap[o_lo:o_hi, :], ALU.mult,
                               at.res + R.res, at.res)
                            tt("pool", mixT[o_lo:o_hi, (8 + hp) * NM + b * 512:(8 + hp) * NM + (b + 1) * 512],
                               at.ap[o_lo:o_hi, :], AG[pi].ap[o_lo:o_hi, b * 512:(b + 1) * 512], ALU.mult,
                               at.res + AG[pi].res, [r_mix[8 + hp][b]])
                        pending.append(fin)

                pending = []
                projq = []
                for pi_ in range(2):
                    for hh_ in range(2):
                        kidx = pi_ * 2 + hh_
                        dma("pool", KT[pi_][hh_].ap[64:80, :], maskA[sgi], [], KT[pi_][hh_].res, own=r_mask[kidx])
                        dma("pool", QT[pi_][hh_].ap[64:80, :], conehot[:], [], QT[pi_][hh_].res, own=r_mask[4 + kidx])
                pair_proj(0)
                emit_qk(0)
                for idx in range(len(steps)):
                    hp, hh, p = steps[idx]
                    if hh == 1 and p == 2 and hp + 1 < 8:
                        pair_proj(hp + 1)
                    if idx + 1 < len(steps):
                        emit_qk(idx + 1)
                    emit_expmult(idx)
                    if pending:
                        pending.pop(0)()
                    if idx >= 1:
                        emit_pv(idx - 1)
                emit_pv(len(steps) - 1)
                while pending:
                    pending.pop(0)()


                if DBG and sgi == nseg - 1:
                    allmix = [r_mix[e_][g_] for e_ in range(16) for g_ in range(2)]
                    dma("sp", dbg_mix[:, :], mixT[:, :], allmix, [], own=r_init)
                dma("sp", woutS.ap, wout_bf[:, :], [r_scr], woutS.res + [r_wout], own=r_wout)
                nxt = sgi + 1 if sgi + 1 < nseg else None
                if nxt is not None:
                    p0_a(nxt)
                for t in range(NM // 128):
                    if nxt is not None:
                        p0_c(nxt, [t])
                    Sb, rS = Sbuf[rot["S"] % 2]
                    rot["S"] += 1
                    oi = rot["out"] % 2
                    rot["out"] += 1
                    g = t // 4
                    dma("sp", xres[oi].ap, xe[sgi, M0 + t * 128:M0 + (t + 1) * 128, :], [],
                        xres[oi].res + [r_xres[oi]], own=r_xres[oi])
                    for half in range(2):
                        for e_ in range(16):
                            mm(Sb[:, half * 512:(half + 1) * 512], mixv(e_, t * 128, (t + 1) * 128),
                               woutS.ap[:, e_ * 1024 + half * 512:e_ * 1024 + (half + 1) * 512], e_ == 0, e_ == 15,
                               [r_mix[e_][g]] + woutS.res + [r_wout], [rS[half]])
                    si = cnt["st"] % 4
                    cnt["st"] += 1
                    st = stat[si]
                    ob = outb[oi]
                    act(ob.ap, Sb[:, :], AF.Square, rS, ob.res + [r_out[oi], r_stat[si]], accum_out=st[:, 0:1])
                    ts("dve", st[:, 1:2], st[:, 0:1], 1.0 / D, 1e-6, ALU.mult, ALU.add, [r_stat[si]], [r_stat[si]])
                    rsqrt_col(st, 128, [r_stat[si]])
                    stt("dve", ob.ap, Sb[:, :], st[:, 2:3], gpost_s[:], ALU.mult, ALU.mult,
                        rS + [r_stat[si], r_const], ob.res + [r_out[oi]])
                    tt("pool", ob.ap, ob.ap, xres[oi].ap, ALU.add, ob.res + [r_out[oi]] + xres[oi].res + [r_xres[oi]],
                       ob.res + [r_out[oi]])
                    dma("sp", ye[sgi, t * 128:(t + 1) * 128, :], ob.ap, ob.res + [r_out[oi]], [], own=r_out[oi])
                if nxt is not None:
                    p0_c(nxt, range(NM // 128, NTILE))

        except StopBuild:
            pass

        def final_waits():
            allr = r_out + r_xt + r_w + r_mask + r_T + r_diag + [r_wout, r_init, r_tb, r_db] + r_xres
            return [(r.sem, r.cnt) for r in allr if r.cnt > 0] + [(g.sem, g.cnt) for g in (ginit, gconst) if g.cnt > 0]

        sch.check_deadlock(esem, final_waits)
        with nc.Block() as block:
            sch.emit(block, esem, final_waits)
    return nc


def _col_tables():
    key = np.arange(GW)
    q = np.arange(GW)
    start = np.clip(q - 8, 0, GW - 16)
    off = key[:, None] - start[None, :]
    valid = (off >= 0) & (off < 16)
    rel = np.clip(key[:, None] - q[None, :] + 15, 0, 30)
    return valid, rel


def host_weights(w_in, w_out, conv_w, conv_b, ln_g, ln_b, rel_bias, pre_g, post_g, meta_tokens):
    w_in = np.asarray(w_in, np.float32)[0]
    cols = np.arange(7168)
    vperm = np.zeros(1024, np.int64)
    for n in range(1024):
        if n < 512:
            head = 2 * (n // 64)
        else:
            head = 2 * ((n - 512) // 64) + 1
        vperm[n] = 5120 + head * 64 + n % 64
    cols[5120:6144] = vperm
    wp = w_in[:, cols]
    win = np.ascontiguousarray(wp.reshape(8, 128, 56, 128).transpose(2, 1, 0, 3)).reshape(56, 128, 1024)
    for g in range(2):
        blk = wp[:, 5120 + g * 512:5120 + (g + 1) * 512].reshape(8, 128, 512).transpose(1, 0, 2)
        win[40 + 4 * g:44 + 4 * g] = blk.reshape(128, 4, 1024).transpose(1, 0, 2)
    wo = np.asarray(w_out, np.float32)[0]
    wout = np.ascontiguousarray(wo.reshape(16, 128, 1024).transpose(1, 0, 2)).reshape(128, 16 * 1024)
    cw = np.asarray(conv_w, np.float32)[0]
    convwT = np.ascontiguousarray(cw.reshape(CONVW, 8, 128).transpose(2, 1, 0)).reshape(128, 8 * CONVW)
    chvec = np.concatenate([np.asarray(v, np.float32)[0].reshape(8, 128).T for v in (conv_b, ln_g, ln_b)], axis=1)
    gpre = np.ascontiguousarray(np.broadcast_to(np.asarray(pre_g, np.float32)[0][None, :], (128, D)))
    gpost = np.ascontiguousarray(np.broadcast_to(np.asarray(post_g, np.float32)[0][None, :], (128, D)))
    rb = np.asarray(rel_bias, np.float32)[0]
    valid, rel = _col_tables()
    Bt = np.zeros((128, 16, 16, 64), np.float32)
    Mt = np.zeros((128, 16, 64), np.float32)
    for half in range(2):
        for slot in range(16):
            d = 7 - slot
            dr = d + half
            if abs(dr) > 7:
                continue
            Bt[half * 64:(half + 1) * 64, :, slot, :] = rb[:, dr + 7, :][:, rel].transpose(1, 0, 2)
            Mt[half * 64:(half + 1) * 64, slot, :] = valid.astype(np.float32)
    conehot = np.zeros((16, NM), np.float32)
    for r in range(16):
        conehot[r, r * 64:(r + 1) * 64] = 1.0
    return {
        "win": win, "wout": wout, "convwT": convwT, "chvec": np.ascontiguousarray(chvec),
        "gpre": gpre, "gpost": gpost, "meta": np.ascontiguousarray(np.asarray(meta_tokens, np.float32)),
        "Bt": Bt.reshape(128, -1), "Mt": Mt.reshape(128, -1), "conehot": conehot,
        "ident": np.eye(128, dtype=np.float32),
    }


def host_segment(xseq, meta_tokens, R0):
    T = xseq.shape[0]
    rows = T // GW
    xe = np.zeros((NE, D), np.float32)
    t0 = R0 * GW - M0
    lo = max(0, t0)
    hi = min(T, t0 + NE)
    xe[lo - t0:hi - t0] = xseq[lo:hi]
    if t0 < 0:
        xe[-t0 - NMETA:-t0] = meta_tokens
    mask = np.full((16, NE), NEG, np.float32)
    wr = min(8, rows)
    for r in range(16):
        R = R0 + r
        sr = int(np.clip(R - wr // 2, 0, rows - wr))
        for kr in range(sr, sr + wr):
            er = kr - R0 + HALO
            if 0 <= er < EXT_ROWS:
                mask[r, er * GW:(er + 1) * GW] = 0.0
    return xe, mask


def run_segments(seg_lists, wts, n_cores):
    nseg = len(seg_lists[0])
    nc = build_program(nseg)
    in_maps = []
    for c in range(n_cores):
        xe = np.zeros((nseg, NE, D), np.float32)
        mk = np.zeros((nseg, 16, NE), np.float32)
        for s, (xseq, R0) in enumerate(seg_lists[c]):
            xe[s], mk[s] = host_segment(xseq, wts["meta"], R0)
        m = dict(wts)
        m["xe"] = xe
        m["maskA"] = mk
        in_maps.append(m)
    res = run_bass_kernel_spmd(nc, in_maps, core_ids=list(range(n_cores)))
    if int(os.environ.get('KDBG', '0')):
        return [r["ye"] for r in res.results], [(r["dbg_mix"], r["dbg_uT"], r["dbg_w"]) for r in res.results]
    return [r["ye"] for r in res.results]


def kernel(x_prompt, x_sample, meta_tokens, pre_norm_g, w_in, conv_w, conv_b, conv_ln_g, conv_ln_b,
           rel_bias, post_norm_g, w_out):
    x_prompt = np.asarray(x_prompt, np.float32)
    x_sample = np.asarray(x_sample, np.float32)
    wts = host_weights(w_in, w_out, conv_w, conv_b, conv_ln_g, conv_ln_b, rel_bias, pre_norm_g, post_norm_g,
                       meta_tokens)
    seg_lists = []
    where = []
    for c in range(NCORES):
        segs = []
        wh = []
        for i in range(4):
            b = 4 * c + i
            for R0 in (0, 16):
                segs.append((x_prompt[b], R0))
                wh.append((0, b, R0))
        sb_ = c // 2
        for R0 in (32 * (c % 2), 32 * (c % 2) + 16):
            segs.append((x_sample[sb_], R0))
            wh.append((1, sb_, R0))
        seg_lists.append(segs)
        where.append(wh)
    outs = run_segments(seg_lists, wts, NCORES)
    y_prompt = np.empty_like(x_prompt)
    y_sample = np.empty_like(x_sample)
    for c in range(NCORES):
        for s, (which, b, R0) in enumerate(where[c]):
            dst = y_prompt if which == 0 else y_sample
            dst[b, R0 * GW:(R0 + SEG_ROWS) * GW] = outs[c][s]
    return (y_prompt, y_sample)
```

```python
import os
import numpy as np
import concourse.bass as bass
import concourse.mybir as mybir
from concourse.bass_utils import run_bass_kernel_spmd

F32 = mybir.dt.float32
BF16 = mybir.dt.bfloat16
I32 = mybir.dt.int32
MAGIC = 0x5f3759df
ALU = mybir.AluOpType
AF = mybir.ActivationFunctionType

D = 1024
NMETA = 16
GW = 64
SEG_ROWS = 16
HALO = 4
EXT_ROWS = SEG_ROWS + 2 * HALO
NE = EXT_ROWS * GW
NM = SEG_ROWS * GW
M0 = HALO * GW
NTILE = NE // 128
CONVW = 31
PAD = 15
NUG = NM + 2 * PAD
NEG = -30000.0
NCORES = 8
WRING = 8


class Res:
    __slots__ = ("name", "lw", "rd", "sem", "cnt")

    def __init__(self, name):
        self.name = name
        self.lw = None
        self.rd = []
        self.sem = None
        self.cnt = 0


class Group:
    def __init__(self, sem):
        self.sem = sem
        self.cnt = 0


class Op:
    __slots__ = ("eng", "fn", "deps", "signal", "val", "dsem", "dval", "grp", "xw")

    def __init__(self, eng, fn):
        self.eng = eng
        self.fn = fn
        self.deps = []
        self.signal = False
        self.val = 0
        self.dsem = None
        self.dval = 0
        self.grp = None
        self.xw = None


class Sched:
    ENGS = ("pe", "act", "dve", "pool", "sp")

    def __init__(self, nc):
        self.nc = nc
        self.ops = {e: [] for e in self.ENGS}
        self.esem = {}
        self.out_ops = []

    def _deps(self, op, reads, writes):
        deps = []
        for r in reads:
            if r.lw is not None:
                deps.append(r.lw)
        for w in writes:
            if w.lw is not None:
                deps.append(w.lw)
            deps.extend(w.rd)
        seen = set()
        for d in deps:
            if id(d) in seen or d is op:
                continue
            seen.add(id(d))
            if d.dsem is None and d.grp is None:
                if d.eng == "pe" and op.eng == "pe":
                    continue
                d.signal = True
            op.deps.append(d)
        for r in reads:
            r.rd.append(op)
        for w in writes:
            w.lw = op
            w.rd = []

    def add(self, eng, fn, reads=(), writes=()):
        op = Op(eng, fn)
        self._deps(op, reads, writes)
        self.ops[eng].append(op)
        return op

    def dma(self, q, fn, reads=(), writes=(), own=None, grp=None):
        op = Op(q, fn)
        self._deps(op, reads, writes)
        if own is not None:
            own.cnt += 16
            op.dsem = own.sem
            op.dval = own.cnt
        else:
            op.deps = [d for d in op.deps if d.grp is not grp]
            grp.cnt += 16
            op.grp = grp
            if q == "pool" and grp.cnt > 16 * 24:
                op.xw = (grp.sem, grp.cnt - 16 * 24)
        self.ops[q].append(op)
        return op

    def check_deadlock(self, sems, final_waits):
        for e in self.ENGS:
            c = 0
            for op in self.ops[e]:
                if op.dsem is None and op.grp is None and op.signal:
                    c += 1
                    op.val = c

        def ev(d):
            if d.dsem is not None:
                return id(d.dsem), d.dval
            if d.grp is not None:
                return id(d.grp.sem), d.grp.cnt
            return id(sems[d.eng]), d.val
        prog = {}
        for e in self.ENGS:
            lst = []
            for op in self.ops[e]:
                waits = [ev(d) for d in op.deps]
                if op.xw is not None:
                    waits.append((id(op.xw[0]), op.xw[1]))
                if op.dsem is not None:
                    inc = (id(op.dsem), 16)
                elif op.grp is not None:
                    inc = (id(op.grp.sem), 16)
                elif op.signal:
                    inc = (id(sems[e]), 1)
                else:
                    inc = None
                lst.append((waits, inc))
            if e == "sp":
                lst.append(([(id(s_), v) for s_, v in final_waits()], None))
            prog[e] = lst
        val = {}
        pc = {e: 0 for e in self.ENGS}
        progress = True
        while progress:
            progress = False
            for e in self.ENGS:
                while pc[e] < len(prog[e]):
                    waits, inc = prog[e][pc[e]]
                    if all(val.get(s_, 0) >= v for s_, v in waits):
                        if inc is not None:
                            val[inc[0]] = val.get(inc[0], 0) + inc[1]
                        pc[e] += 1
                        progress = True
                    else:
                        break
        stuck = {e: (pc[e], len(prog[e])) for e in self.ENGS if pc[e] < len(prog[e])}
        if stuck:
            names = {id(v): k for k, v in sems.items()}
            msg = []
            for e, (p, n) in stuck.items():
                waits, _ = prog[e][p]
                msg.append(f"{e} stuck at {p}/{n} waiting " + str([(names.get(s_, s_), v, val.get(s_, 0)) for s_, v in waits if val.get(s_, 0) < v]))
            raise RuntimeError("DEADLOCK in schedule: " + "; ".join(msg))

    def emit(self, block, sems, final_waits):
        nc = self.nc
        for e in self.ENGS:
            c = 0
            for op in self.ops[e]:
                if op.dsem is None and op.grp is None and op.signal:
                    c += 1
                    op.val = c
        esem = sems
        sched = self

        def ev(d):
            if d.dsem is not None:
                return d.dsem, d.dval
            if d.grp is not None:
                return d.grp.sem, d.grp.cnt
            return esem[d.eng], d.val

        def run(e, eng):
            known = {}
            for op in sched.ops[e]:
                for d in op.deps:
                    s, v = ev(d)
                    key = id(s)
                    if known.get(key, 0) < v:
                        eng.wait_ge(s, v)
                        known[key] = v
                if op.xw is not None:
                    eng.wait_ge(op.xw[0], op.xw[1])
                ins = op.fn(eng)
                if op.dsem is not None:
                    ins.then_inc(op.dsem, 16)
                elif op.grp is not None:
                    ins.then_inc(op.grp.sem, 16)
                elif op.signal:
                    ins.then_inc(esem[e], 1)
            if e == "sp":
                for s, v in final_waits():
                    eng.wait_ge(s, v)

        @block.tensor
        def _(eng):
            run("pe", eng)

        @block.scalar
        def _(eng):
            run("act", eng)

        @block.vector
        def _(eng):
            run("dve", eng)

        @block.gpsimd
        def _(eng):
            run("pool", eng)

        @block.sync
        def _(eng):
            run("sp", eng)


def tile_rows(p):
    lo = max(0, 2 * p - 7)
    hi = min(SEG_ROWS - 1, 2 * p + 1)
    rows = set(range(lo, hi + 1)) if lo <= hi else set()
    if p <= 5:
        rows |= set(range(0, 4))
    if p >= 6:
        rows |= set(range(12, 16))
    rows = sorted(rows)
    assert rows == list(range(rows[0], rows[-1] + 1))
    return rows[0], rows[-1]


def build_program(nseg):
    NJUNK = int(os.environ.get('KJUNK', '0'))
    STAGE = int(os.environ.get('KSTAGE', '9'))
    nc = bass.Bass("TRN2", target_bir_lowering=False)

    def din(name, shape, dt=F32):
        return nc.dram_tensor(name, list(shape), dt, kind="ExternalInput").ap()

    xe = din("xe", [nseg, NE, D])
    maskA = din("maskA", [nseg, 16, NE])
    win = din("win", [56, 128, 1024])
    wout = din("wout", [128, 16 * 1024])
    convwT = din("convwT", [128, 8 * CONVW])
    chvec = din("chvec", [128, 24])
    gpre = din("gpre", [128, D])
    gpost = din("gpost", [128, D])
    meta = din("meta", [NMETA, D])
    Bt = din("Bt", [128, 16 * 16 * 64])
    Mt = din("Mt", [128, 16 * 64])
    conehot = din("conehot", [16, NM])
    identf = din("ident", [128, 128])
    ye = nc.dram_tensor("ye", [nseg, NM, D], F32, kind="ExternalOutput").ap()
    DBG = int(os.environ.get('KDBG', '0'))
    if DBG:
        dbg_mix = nc.dram_tensor("dbg_mix", [128, 16 * NM], BF16, kind="ExternalOutput").ap()
        dbg_uT = nc.dram_tensor("dbg_uT", [128, 8 * NE], BF16, kind="ExternalOutput").ap()
        dbg_w = nc.dram_tensor("dbg_w", [128, 1024], BF16, kind="ExternalOutput").ap()
    win_bf = nc.dram_tensor("win_bf", [56, 128, 1024], BF16, kind="Internal").ap()
    wout_bf = nc.dram_tensor("wout_bf", [128, 16 * 1024], BF16, kind="Internal").ap()
    T_bf = nc.dram_tensor("T_bf", [8, 128, 2048], BF16, kind="Internal").ap()
    diag_bf = nc.dram_tensor("diag_bf", [8, 128, CONVW * 128], BF16, kind="Internal").ap()

    import contextlib
    es = contextlib.ExitStack()
    with es:
        def sb(name, shape, dt):
            return es.enter_context(nc.sbuf_tensor(name, list(shape), dt))

        def ps(name, shape, dt):
            return es.enter_context(nc.psum_tensor(name, list(shape), dt))

        def sem(name):
            return es.enter_context(nc.semaphore(name))

        uT = sb("uT", [128, 8 * NE], BF16)
        mixT = sb("mixT", [128, 16 * NM], BF16)
        wring = sb("wring", [128, WRING * 1024], BF16)
        xt = [sb(f"xt{i}", [128, D], F32) for i in range(2)]
        ubf = [sb(f"ubf{i}", [128, D], BF16) for i in range(2)]
        gpre_s = sb("gpre_s", [128, D], F32)
        gpost_s = sb("gpost_s", [128, D], F32)
        ident_s = sb("ident_s", [128, 128], BF16)
        ones_s = sb("ones_s", [128, 128], BF16)
        kmeta = sb("kmeta", [64, 8 * 2 * 16], BF16)
        vmA = sb("vmA", [16, 8 * 128], BF16)
        vmB = sb("vmB", [16, 8 * 128], BF16)
        uTm = sb("uTm", [128, 8 * 16], BF16)
        chv = sb("chv", [128, 24], F32)
        cwT = sb("cwT", [128, 8 * CONVW], F32)
        halfg = sb("halfg", [128, 8], F32)
        halfb = sb("halfb", [128, 8], F32)
        stat = [sb(f"stat{i}", [128, 8], F32) for i in range(4)]
        A16N = 41984
        A32N = 4096
        ar16 = sb("ar16", [128, A16N], BF16)
        ar32 = sb("ar32", [128, A32N], F32)
        G16 = 512
        G32 = 256
        r16 = [Res(f"a16_{i}") for i in range((A16N + G16 - 1) // G16)]
        r32 = [Res(f"a32_{i}") for i in range((A32N + G32 - 1) // G32)]

        class Buf:
            def __init__(self, ap, res):
                self.ap = ap
                self.res = res

        def a16(off, n, parts=128):
            return Buf(ar16[0:parts, off:off + n], r16[off // G16:(off + n - 1) // G16 + 1])

        def a32(off, n, parts=128):
            return Buf(ar32[0:parts, off:off + n], r32[off // G32:(off + n - 1) // G32 + 1])

        UGP = 1056
        tmpb = [a16(i * 512, 512) for i in range(8)]
        UG0 = 4096 + NTILE * 1536
        gc2 = [a16(UG0 + j * 1024, 1024) for j in range(8)]
        ugT = [a16(UG0 + 8192 + i * UGP, UGP) for i in range(2)]
        DG0 = UG0 + 8192 + 2 * UGP
        diag = [a16(DG0 + i * 3968, 3968) for i in range(2)]
        assert DG0 + 2 * 3968 <= A16N
        stA = a32(0, 1024)
        stB = a32(1024, 1024)
        tf = [a32(2048 + i * 512, 512) for i in range(3)]
        Ttab = [a16(i * 2048, 2048) for i in range(2)]
        VW = 1536
        V0 = 4096
        Vt = [a16(V0 + t * VW, VW) for t in range(NTILE)]
        QK0 = V0 + NTILE * VW
        QKS = 2 * 1024 + 2 * NE + 1024
        QT = [[a16(QK0 + i * QKS + hh * 1024, 1024) for hh in range(2)] for i in range(2)]
        KT = [[a16(QK0 + i * QKS + 2048 + hh * NE, NE) for hh in range(2)] for i in range(2)]
        AG = [a16(QK0 + i * QKS + 2048 + 2 * NE, 1024) for i in range(2)]
        EX0 = QK0 + 2 * QKS
        expS = [a16(EX0 + i * 768, 768) for i in range(2)]
        PT = [a16(EX0 + 1536 + i * 768, 768) for i in range(2)]
        PM0 = EX0 + 3072
        PmT = [[a16(PM0 + i * 2048 + hh * 1024, 1024, parts=16) for hh in range(2)] for i in range(2)]
        assert PM0 + 4096 <= A16N, (PM0, A16N)
        Rb = [a32(i * 512, 512) for i in range(2)]
        attb = [a32(1024 + i * 512, 512) for i in range(2)]
        woutS = a16(0, 16384)
        outb = [a32(i * 1024, 1024) for i in range(2)]
        xres = [a32(2048 + i * 1024, 1024) for i in range(2)]
        btmp = a32(0, 2048)
        btmp2 = a32(2048, 2048)
        tbuild = a16(0, 2048)
        dbuild = a16(4096, 3968)
        xmeta = a32(0, 1024, parts=16)

        S0 = ps("S0", [128, 1024], F32)
        S1 = ps("S1", [128, 1024], F32)
        O0 = ps("O0", [128, 512], F32)
        O1 = ps("O1", [128, 512], F32)
        PJ = ps("PJ", [128, 512], F32)
        TB = ps("TB", [128, 1024], BF16)
        rS0a, rS0b, rS1a, rS1b, rO0, rO1, rPJ, rTB = [Res(n) for n in
                                                    ("S0a", "S0b", "S1a", "S1b", "O0", "O1", "PJ", "TB")]
        rPJa, rPJb = rPJ, Res("PJb")
        banks = [(S0[:, 0:512], rS0a), (S0[:, 512:1024], rS0b), (S1[:, 0:512], rS1a),
                 (S1[:, 512:1024], rS1b), (O0[:, :], rO0), (O1[:, :], rO1), (PJ[:, :], rPJ)]
        Sbuf = [(S0, [rS0a, rS0b]), (S1, [rS1a, rS1b])]
        Obuf = [(O0, rO0), (O1, rO1)]

        esem = {e: sem("s_" + e) for e in Sched.ENGS}
        sch = Sched(nc)
        ginit = Group(sem("g_init"))
        gconst = Group(sem("g_const"))

        def own(r, name):
            r.sem = sem(name)
            return r

        r_xt = [own(Res(f"xt{i}"), f"d_xt{i}") for i in range(2)]
        r_ubf = [Res(f"ubf{i}") for i in range(2)]
        r_w = [own(Res(f"w{i}"), f"d_w{i}") for i in range(WRING)]
        r_mask = [own(Res(f"mask{i}"), f"d_mask{i}") for i in range(8)]
        r_uT = [Res(f"uT{t}") for t in range(NTILE)]
        r_mix = [[Res(f"mix{e}_{g}") for g in range(2)] for e in range(16)]
        r_stat = [Res(f"stat{i}") for i in range(4)]
        r_const = Res("const")
        r_const2 = Res("const2")
        r_scr = Res("scratch")
        r_meta = Res("metabufs")
        r_T = [own(Res(f"T{i}"), f"d_T{i}") for i in range(2)]
        r_diag = [own(Res(f"dg{i}"), f"d_dg{i}") for i in range(2)]
        r_wout = own(Res("woutS"), "d_wout")
        r_xres = [own(Res(f"xres{i}"), f"d_xres{i}") for i in range(2)]
        r_out = [own(Res(f"out{i}"), f"d_out{i}") for i in range(2)]
        r_init = own(Res("initbuf"), "d_initbuf")
        r_tb = own(Res("tb"), "d_tb")
        r_db = own(Res("db"), "d_db")
        r_Tscr = [Res(f"Tscr{i}") for i in range(8)]
        r_dscr = [Res(f"dscr{i}") for i in range(8)]

        def mm(out, lhsT, rhs, start, stop, reads, writes, **kw):
            return sch.add("pe", lambda e: e.matmul(out, lhsT=lhsT, rhs=rhs, start=start, stop=stop, **kw),
                           reads, writes)

        def act(out, in_, func, reads, writes, **kw):
            return sch.add("act", lambda e: e.activation(out=out, in_=in_, func=func, **kw), reads, writes)

        def tt(eng, out, in0, in1, op, reads, writes):
            return sch.add(eng, lambda e: e.tensor_tensor(out=out, in0=in0, in1=in1, op=op), reads, writes)

        def ts(eng, out, in0, s1, s2, op0, op1, reads, writes):
            if s2 is None:
                return sch.add(eng, lambda e: e.tensor_scalar(out=out, in0=in0, scalar1=s1, scalar2=None, op0=op0),
                               reads, writes)
            return sch.add(eng, lambda e: e.tensor_scalar(out=out, in0=in0, scalar1=s1, scalar2=s2, op0=op0, op1=op1),
                           reads, writes)

        def stt(eng, out, in0, scalar, in1, op0, op1, reads, writes):
            return sch.add(eng, lambda e: e.scalar_tensor_tensor(out=out, in0=in0, scalar=scalar, in1=in1,
                                                                 op0=op0, op1=op1), reads, writes)

        def rsqrt(a, r, w, ra, rr, rw):
            ri = r.bitcast(I32)
            ts("dve", ri, a.bitcast(I32), 1, None, ALU.arith_shift_right, None, ra, rr)
            ts("dve", ri, ri, -1, MAGIC, ALU.mult, ALU.add, rr, rr)
            for _ in range(3):
                tt("dve", w, a, r, ALU.mult, ra + rr, rw)
                tt("dve", w, w, r, ALU.mult, rw + rr, rw)
                ts("dve", w, w, -0.5, 1.5, ALU.mult, ALU.add, rw, rw)
                tt("dve", r, r, w, ALU.mult, rr + rw, rr)

        def rsqrt_col(st, n, rr):
            a, r, h, t = st[0:n, 1:2], st[0:n, 2:3], st[0:n, 3:4], st[0:n, 4:5]
            ri = r.bitcast(I32)
            ts("dve", ri, a.bitcast(I32), 1, None, ALU.arith_shift_right, None, rr, rr)
            ts("dve", ri, ri, -1, MAGIC, ALU.mult, ALU.add, rr, rr)
            ts("dve", h, a, -0.5, None, ALU.mult, None, rr, rr)
            for _ in range(3):
                ts("dve", t, h, r, r, ALU.mult, ALU.mult, rr, rr)
                stt("dve", r, t, 1.5, r, ALU.add, ALU.mult, rr, rr)

        def cp(eng, out, in_, reads, writes):
            if eng == "act":
                return sch.add("act", lambda e: e.copy(out=out, in_=in_), reads, writes)
            return sch.add(eng, lambda e: e.tensor_copy(out=out, in_=in_), reads, writes)

        def dma(q, out, in_, reads, writes, own=None, grp=None):
            return sch.dma(q, lambda e: e.dma_start(out=out, in_=in_), reads, writes, own=own, grp=grp)

        def uTv(k, a, b):
            return uT[:, k * NE + a:k * NE + b]

        def mixv(e, a, b):
            return mixT[:, e * NM + a:e * NM + b]

        def wv(slot, k):
            return wring[:, slot * 1024 + k * 128:slot * 1024 + (k + 1) * 128]

        wseq = []
        for s in range(nseg):
            for j in range(8):
                wseq += [8 + j, j]
            for j in range(8):
                wseq += [16 + j]
            while len(wseq) % 4:
                wseq.append(None)
            wseq += [40, 41, 42, 43, 44, 45, 46, 47]
            for hp in range(8):
                wseq += [24 + hp, 32 + hp, 48 + hp]
        wpre = [32 + hp for hp in range(8)] + [40, 41, 42, 43, 44, 45, 46, 47]
        wseq = wpre + wseq
        wstate = {"loaded": 0, "used": 0}

        def w_load_upto(n):
            while wstate["loaded"] < min(n, len(wseq)):
                i = wstate["loaded"]
                e = wseq[i]
                if e is not None:
                    slot = i % WRING
                    dma("sp", wring[:, slot * 1024:(slot + 1) * 1024], win_bf[e],
                        [r_scr], [r_w[slot]], own=r_w[slot])
                wstate["loaded"] += 1

        def w_next(expect):
            i = wstate["used"]
            while wseq[i] is None:
                i += 1
            assert wseq[i] == expect, (i, wseq[i], expect)
            w_load_upto(i + 1)
            wstate["used"] = i + 1
            return i % WRING

        def w_prefetch(k=4):
            assert k <= WRING
            w_load_upto(wstate["used"] + k)

        class StopBuild(Exception):
            pass

        def stage(k):
            if STAGE < k:
                raise StopBuild()

        try:
            for e in range(int(os.environ.get('KNW', '56'))):
                dma("pool", win_bf[e], win[e], [], [r_scr], grp=ginit)
            for q in range(int(os.environ.get('KNO', '16'))):
                dma("pool", wout_bf[:, q * 1024:(q + 1) * 1024], wout[:, q * 1024:(q + 1) * 1024], [], [r_scr], grp=ginit)
            stage(-3)
            dma("sp", gpre_s[:], gpre[:], [], [r_const], grp=gconst)
            dma("sp", gpost_s[:], gpost[:], [], [r_const], grp=gconst)
            dma("sp", chv[:], chvec[:], [], [r_const], grp=gconst)
            dma("sp", cwT[:], convwT[:], [], [r_const], grp=gconst)
            dma("pool", ident_s[:], identf[:], [], [r_const], grp=gconst)
            Mt_s = sb("Mt_s", [128, 1024], F32)
            dma("sp", Mt_s[:], Mt[:], [], [r_const], grp=gconst)
            sch.add("dve", lambda e: e.memset(ones_s[:], 1.0 / 1024.0), [], [r_const2])
            sch.add("dve", lambda e: e.memset(vmA[:], 1.0), [], [r_meta])
            sch.add("dve", lambda e: e.memset(vmB[:], 1.0), [], [r_meta])
            ts("dve", halfg[:], chv[:, 8:16], 0.5, None, ALU.mult, None, [r_const], [r_const2])
            ts("dve", halfb[:], chv[:, 16:24], 0.5, None, ALU.mult, None, [r_const], [r_const2])

            stage(-2)
            for hp in range(8):
                dma("sp", btmp.ap, Bt[:, hp * 2048:(hp + 1) * 2048], [], btmp.res, own=r_init)
                act(btmp2.ap, btmp.ap, AF.Exp, btmp.res, btmp2.res)
                for hh in range(2):
                    tt("dve", tbuild.ap[:, hh * 1024:(hh + 1) * 1024], btmp2.ap[:, hh * 1024:(hh + 1) * 1024],
                       Mt_s[:], ALU.mult, btmp2.res + [r_const], tbuild.res)
                dma("sp", T_bf[hp], tbuild.ap, tbuild.res, [r_Tscr[hp]], own=r_tb)
            stage(-1)
            for j in range(8):
                for s in range(CONVW):
                    eng = "dve" if (s % 2 == 0) else "pool"
                    ts(eng, dbuild.ap[:, s * 128:(s + 1) * 128], ident_s[:],
                       cwT[:, j * CONVW + s:j * CONVW + s + 1], 0.5, ALU.mult, ALU.mult,
                       [r_const], dbuild.res)
                dma("sp", diag_bf[j], dbuild.ap, dbuild.res, [r_dscr[j]], own=r_db)

            cnt = {"x": 0, "st": 0, "cpy": 0, "pj": 0}

            def p0_tile(src_ap, nrows, dst_fn, dst_res, xbuf=None, xres_=None):
                i = cnt["x"] % 2
                cnt["x"] += 1
                si = cnt["st"] % 4
                cnt["st"] += 1
                if xbuf is None:
                    xb, xr = xt[i][0:nrows, :], [r_xt[i]]
                    dma("sp", xb, src_ap, [], xr, own=r_xt[i])
                else:
                    xb, xr = xbuf, xres_
                ub = ubf[i][0:nrows, :]
                st = stat[si]
                act(ub, xb, AF.Square, xr, [r_ubf[i], r_stat[si]], accum_out=st[0:nrows, 0:1])
                ts("dve", st[0:nrows, 1:2], st[0:nrows, 0:1], 1.0 / D, 1e-6, ALU.mult, ALU.add, [r_stat[si]], [r_stat[si]])
                rsqrt_col(st, nrows, [r_stat[si]])
                stt("dve", ub, xb, st[0:nrows, 2:3], gpre_s[0:nrows, :], ALU.mult, ALU.mult,
                    xr + [r_stat[si], r_const], [r_ubf[i]])
                for k in range(8):
                    sch.add("pe", lambda e, k=k: e.transpose(out=TB[:, k * 128:k * 128 + nrows],
                                                               in_=ubf[i][0:nrows, k * 128:(k + 1) * 128],
                                                               identity=ident_s[0:nrows, 0:nrows]),
                            [r_ubf[i], r_const], [rTB])
                ceng = "act" if cnt["cpy"] % 2 == 0 else "dve"
                cnt["cpy"] += 1
                src = TB[:].rearrange("p (k t) -> p k t", t=128)[:, :, 0:nrows]
                cp(ceng, dst_fn(), src, [rTB], dst_res)

            stage(1)
            ssb = sb("ssb", [128, 64], F32)
            r_ssb = Res("ssb")

            def p0_batched(sgi):
                for t in range(NTILE):
                    i = cnt["x"] % 2
                    cnt["x"] += 1
                    dma("sp", xt[i][:], xe[sgi, t * 128:(t + 1) * 128, :], [], [r_xt[i]], own=r_xt[i])
                    act(ubf[i][:], xt[i][:], AF.Square, [r_xt[i]], [r_ubf[i], r_ssb], accum_out=ssb[:, t:t + 1])
                a_, r_, h_, t_ = ssb[:, 16:28], ssb[:, 32:44], ssb[:, 48:60], ssb[:, 0:12]
                rr = [r_ssb]
                ts("dve", a_, ssb[:, 0:12], 1.0 / D, 1e-6, ALU.mult, ALU.add, rr, rr)
                ri = r_.bitcast(I32)
                ts("dve", ri, a_.bitcast(I32), 1, None, ALU.arith_shift_right, None, rr, rr)
                ts("dve", ri, ri, -1, MAGIC, ALU.mult, ALU.add, rr, rr)
                ts("dve", h_, a_, -0.5, None, ALU.mult, None, rr, rr)
                for _ in range(3):
                    tt("dve", t_, h_, r_, ALU.mult, rr, rr)
                    tt("dve", t_, t_, r_, ALU.mult, rr, rr)
                    stt("dve", r_, t_, 1.5, r_, ALU.add, ALU.mult, rr, rr)
                for t in range(NTILE):
                    i = cnt["x"] % 2
                    cnt["x"] += 1
                    dma("sp", xt[i][:], xe[sgi, t * 128:(t + 1) * 128, :], [], [r_xt[i]], own=r_xt[i])
                    stt("dve", ubf[i][:], xt[i][:], ssb[:, 32 + t:33 + t], gpre_s[:], ALU.mult, ALU.mult,
                        [r_xt[i], r_ssb, r_const], [r_ubf[i]])
                    if t % 2 == 0:
                        tbank, tres = TB[:], [rTB]
                    else:
                        tbank, tres = PJ[:].bitcast(BF16), [rPJa, rPJb]
                    for k in range(8):
                        sch.add("pe", lambda e, k=k, i=i, tbank=tbank: e.transpose(out=tbank[:, k * 128:(k + 1) * 128],
                                                                                    in_=ubf[i][:, k * 128:(k + 1) * 128],
                                                                                    identity=ident_s[:, :]),
                                [r_ubf[i], r_const], tres)
                    ceng = "act" if t % 2 == 0 else "dve"
                    src = tbank.rearrange("p (k t) -> p k t", t=128)
                    dst = uT[:].rearrange("p (k n) -> p k n", n=NE)[:, :, t * 128:(t + 1) * 128]
                    cp(ceng, dst, src, tres, [r_uT[t]])

            dma("sp", xmeta.ap, meta[:], [], xmeta.res, own=r_init)
            p0_tile(None, NMETA, lambda: uTm[:].rearrange("p (k t) -> p k t", t=16), [r_meta],
                    xbuf=xmeta.ap, xres_=xmeta.res)
            for hp in range(8):
                slot = w_next(32 + hp)
                for k in range(8):
                    mm(PJ[:, 0:16], wv(slot, k), uTm[:, k * 16:(k + 1) * 16], k == 0, k == 7,
                       [r_w[slot], r_meta], [rPJ])
                cp("dve", kmeta[0:64, hp * 32:hp * 32 + 16], PJ[0:64, 0:16], [rPJ], [r_meta])
                cp("dve", kmeta[0:64, hp * 32 + 16:hp * 32 + 32], PJ[64:128, 0:16], [rPJ], [r_meta])
                w_prefetch(4)
            for g in range(2):
                slots = [w_next(40 + 4 * g + q) for q in range(4)]
                assert slots[0] % 4 == 0
                for k in range(8):
                    rhs = wring[:, slots[0] * 1024 + k * 512:slots[0] * 1024 + (k + 1) * 512]
                    mm(PJ[0:16, 0:512], uTm[:, k * 16:(k + 1) * 16], rhs, k == 0, k == 7,
                       [r_w[s_] for s_ in slots] + [r_meta], [rPJ])
                if g == 0:
                    cp("dve", vmA[0:16, :].rearrange("p (i c) -> p i c", c=128)[:, :, 0:64],
                       PJ[0:16, 0:512].rearrange("p (i c) -> p i c", c=64), [rPJ], [r_meta])
                else:
                    cp("dve", vmB[0:16, :].rearrange("p (i c) -> p i c", c=128)[:, :, 64:128],
                       PJ[0:16, 0:512].rearrange("p (i c) -> p i c", c=64), [rPJ], [r_meta])

            rot = {"pair": 0, "S": 0, "ex": 0, "out": 0, "tmp": 0, "tf": 0, "dg": 0, "ug": 0, "T": 0, "mask": 0}

            for sgi in range(nseg):
                mi = rot["mask"] % 2
                rot["mask"] += 1

                stage(2)
                p0_batched(sgi)
                all_uT = list(r_uT)
                if DBG and sgi == nseg - 1:
                    dma("sp", dbg_uT[:, :], uT[:, :], all_uT, [], own=r_init)
                    dma("sp", dbg_w[:, :], win_bf[8], [r_scr], [], own=r_init)

                stage(3)
                ug_groups = [(M0 - PAD, 512), (M0 - PAD + 512, 512), (M0 - PAD + 1024, 2 * PAD)]
                w_prefetch(4)
                for j in range(8):
                    ui = rot["ug"] % 2
                    rot["ug"] += 1
                    sg_ = w_next(8 + j)
                    sv_ = w_next(j)
                    di = rot["dg"] % 2
                    rot["dg"] += 1
                    dma("sp", diag[di].ap, diag_bf[j], [r_dscr[j]], diag[di].res + [r_diag[di]], own=r_diag[di])
                    for gi, (e0, n) in enumerate(ug_groups):
                        bg, rg = banks[(2 * gi) % 4]
                        bv, rv = banks[(2 * gi + 1) % 4]
                        for k in range(8):
                            mm(bg[:, 0:n], wv(sg_, k), uTv(k, e0, e0 + n), k == 0, k == 7, [r_w[sg_]] + all_uT, [rg])
                        for k in range(8):
                            mm(bv[:, 0:n], wv(sv_, k), uTv(k, e0, e0 + n), k == 0, k == 7, [r_w[sv_]] + all_uT, [rv])
                        tb = tmpb[rot["tmp"] % 8]
                        rot["tmp"] += 1
                        act(tb.ap[:, 0:n], bg[:, 0:n], AF.Tanh, [rg], tb.res, scale=0.5)
                        o0 = e0 - (M0 - PAD)
                        stt("dve", ugT[ui].ap[:, o0:o0 + n], tb.ap[:, 0:n], 1.0, bv[:, 0:n], ALU.add, ALU.mult,
                            tb.res + [rv], ugT[ui].res)
                    for g in range(2):
                        by, ry = Obuf[g]
                        for s in range(CONVW):
                            mm(by[:, :], diag[di].ap[:, s * 128:(s + 1) * 128],
                               ugT[ui].ap[:, g * 512 + s:g * 512 + s + 512], s == 0, s == CONVW - 1,
                               diag[di].res + [r_diag[di]] + ugT[ui].res, [ry])
                        act(mixv(j, g * 512, (g + 1) * 512), by[:, :], AF.Identity, [ry, r_const], [r_mix[j][g]],
                            bias=chv[:, j:j + 1])
                    if g == 1:
                        w_prefetch(4)
                stage(4)
                w_prefetch(4)
                for j in range(8):
                    sc_ = w_next(16 + j)
                    for g in range(2):
                        bc, rc = banks[2 + g]
                        for k in range(8):
                            mm(bc, wv(sc_, k), uTv(k, M0 + g * 512, M0 + (g + 1) * 512), k == 0, k == 7,
                               [r_w[sc_]] + all_uT, [rc])
                        tb = tmpb[rot["tmp"] % 8]
                        rot["tmp"] += 1
                        act(tb.ap, bc, AF.Tanh, [rc], tb.res, scale=0.5)
                        stt("dve", gc2[j].ap[:, g * 512:(g + 1) * 512], tb.ap, 1.0, bc, ALU.add, ALU.mult,
                            tb.res + [rc], gc2[j].res)
                    w_prefetch(4)

                stage(5)
                for g in range(2):
                    bm, rm = banks[0]
                    bq, rq = banks[1]
                    for j in range(8):
                        tb = tmpb[rot["tmp"] % 8]
                        rot["tmp"] += 1
                        act(tb.ap, mixv(j, g * 512, (g + 1) * 512), AF.Square, [r_mix[j][g]], tb.res)
                        mm(bm, ones_s[:], mixv(j, g * 512, (g + 1) * 512), j == 0, j == 7, [r_const2, r_mix[j][g]], [rm])
                        mm(bq, ones_s[:], tb.ap, j == 0, j == 7, [r_const2] + tb.res, [rq])
                    A_ = stA.ap[:, g * 512:(g + 1) * 512]
                    B_ = stB.ap[:, g * 512:(g + 1) * 512]
                    t0 = tf[0]
                    act(t0.ap, bm, AF.Square, [rm], t0.res)
                    tt("dve", t0.ap, bq, t0.ap, ALU.subtract, [rq] + t0.res, t0.res)
                    ts("dve", t0.ap, t0.ap, 1e-5, None, ALU.add, None, t0.res, t0.res)
                    rsqrt(t0.ap, A_, tf[1].ap, t0.res, stA.res, tf[1].res)
                    stt("dve", B_, bm, -1.0, A_, ALU.mult, ALU.mult, [rm] + stA.res, stB.res)
                def ln_apply(j, g):
                    A_ = stA.ap[:, g * 512:(g + 1) * 512]
                    B_ = stB.ap[:, g * 512:(g + 1) * 512]
                    t1 = tf[1 + (rot["tf"] % 2)]
                    rot["tf"] += 1
                    mv = mixv(j, g * 512, (g + 1) * 512)
                    tt("pool", t1.ap, mv, A_, ALU.mult, [r_mix[j][g]] + stA.res, t1.res)
                    tt("pool", t1.ap, t1.ap, B_, ALU.add, t1.res + stB.res, t1.res)
                    tb2 = tmpb[rot["tmp"] % 8]
                    rot["tmp"] += 1
                    tb3 = tmpb[rot["tmp"] % 8]
                    rot["tmp"] += 1
                    act(tb2.ap, t1.ap, AF.Tanh, t1.res + [r_const2], tb2.res,
                        scale=halfg[:, j:j + 1], bias=halfb[:, j:j + 1])
                    act(tb3.ap, t1.ap, AF.Identity, t1.res + [r_const], tb3.res,
                        scale=chv[:, 8 + j:9 + j], bias=chv[:, 16 + j:17 + j])
                    stt("dve", t1.ap, tb2.ap, 1.0, tb3.ap, ALU.add, ALU.mult, tb2.res + tb3.res, t1.res)
                    stt("dve", mv, t1.ap, 0.25, gc2[j].ap[:, g * 512:(g + 1) * 512], ALU.mult, ALU.mult,
                        t1.res + gc2[j].res, [r_mix[j][g]])

                stage(6)
                for t in range(NTILE):
                    sch.add("pool", lambda e, t=t: e.memset(
                        Vt[t].ap.rearrange("p (i c) -> p i c", c=192)[:, :, 64:128], 1.0), [], Vt[t].res)
                vsl = [[w_next(40 + 4 * g + q) for q in range(4)] for g in range(2)]
                assert vsl[0][0] % 4 == 0 and vsl[1][0] % 4 == 0
                lnq = [(j, g) for j in range(8) for g in range(2)]
                nv = 0
                for g in range(2):
                    s0 = vsl[g][0]
                    for t in range(NTILE):
                        bv, rv = banks[4 + (t % 2)]
                        for k in range(8):
                            rhs = wring[:, s0 * 1024 + k * 512:s0 * 1024 + (k + 1) * 512]
                            mm(bv, uTv(k, t * 128, (t + 1) * 128), rhs, k == 0, k == 7,
                               [r_w[s_] for s_ in vsl[g]] + [r_uT[t]], [rv])
                        v3 = Vt[t].ap.rearrange("p (i c) -> p i c", c=192)
                        dst = v3[:, :, 0:64] if g == 0 else v3[:, :, 128:192]
                        cp("act", dst, bv.rearrange("p (i c) -> p i c", c=64), [rv], Vt[t].res)
                        nv += 1
                        while lnq and (16 - len(lnq)) * 24 < nv * 16:
                            ln_apply(*lnq.pop(0))
                while lnq:
                    ln_apply(*lnq.pop(0))
                w_prefetch(4)

                O3 = [(O0[:, :], rO0), (O1[:, :], rO1), (TB[:].bitcast(F32), rTB)]

                def pair_tasks(hp):
                    pi = hp % 2
                    ti = hp % 2
                    bp, rp = banks[6]
                    st_ = {"n": 0}
                    tasks = []
                    rotb = [banks[6], banks[1], banks[3]]

                    def nextbank():
                        st_["n"] += 1
                        return rotb[st_["n"] % 3]

                    def setup():
                        st_["q"] = w_next(24 + hp)
                        st_["k"] = w_next(32 + hp)
                        st_["a"] = w_next(48 + hp)
                        dma("sp", Ttab[ti].ap, T_bf[hp], [r_Tscr[hp]], Ttab[ti].res + [r_T[ti]], own=r_T[ti])

                    def qgrp(g):
                        bp, rp = nextbank()
                        sq_ = st_["q"]
                        for k in range(8):
                            mm(bp, wv(sq_, k), uTv(k, M0 + g * 512, M0 + (g + 1) * 512), k == 0, k == 7,
                               [r_w[sq_]] + all_uT, [rp])
                        cp("act", QT[pi][0].ap[0:64, g * 512:(g + 1) * 512], bp[0:64, :], [rp], QT[pi][0].res)
                        cp("act", QT[pi][1].ap[0:64, g * 512:(g + 1) * 512], bp[64:128, :], [rp], QT[pi][1].res)

                    def kgrp(g):
                        bp, rp = nextbank()
                        sk_ = st_["k"]
                        for k in range(8):
                            mm(bp, wv(sk_, k), uTv(k, g * 512, (g + 1) * 512), k == 0, k == 7,
                               [r_w[sk_]] + all_uT, [rp])
                        cp("dve", KT[pi][0].ap[0:64, g * 512:(g + 1) * 512], bp[0:64, :], [rp], KT[pi][0].res)
                        cp("act", KT[pi][1].ap[0:64, g * 512:(g + 1) * 512], bp[64:128, :], [rp], KT[pi][1].res)

                    def agrp(g):
                        bp, rp = nextbank()
                        sa_ = st_["a"]
                        for k in range(8):
                            mm(bp, wv(sa_, k), uTv(k, M0 + g * 512, M0 + (g + 1) * 512), k == 0, k == 7,
                               [r_w[sa_]] + all_uT, [rp])
                        act(AG[pi].ap[:, g * 512:(g + 1) * 512], bp, AF.Tanh, [rp], AG[pi].res, scale=0.5)
                        stt("dve", AG[pi].ap[:, g * 512:(g + 1) * 512], AG[pi].ap[:, g * 512:(g + 1) * 512], 1.0, bp,
                            ALU.add, ALU.mult, AG[pi].res + [rp], AG[pi].res)
                        if g == 1:
                            w_prefetch(4)

                    def mgrp(hh):
                        pm = PmT[pi][hh]
                        for b in range(2):
                            mm(bp[0:16, :], kmeta[0:64, hp * 32 + hh * 16:hp * 32 + hh * 16 + 16],
                               QT[pi][hh].ap[0:64, b * 512:(b + 1) * 512], True, True,
                               [r_meta] + QT[pi][hh].res, [rp])
                            act(pm.ap[:, b * 512:(b + 1) * 512], bp[0:16, :], AF.Exp, [rp], pm.res, scale=0.125)

                    tasks.append(lambda: (setup(), qgrp(0)))
                    tasks.append(lambda: qgrp(1))
                    for g in range(3):
                        tasks.append(lambda g=g: kgrp(g))
                    for g in range(2):
                        tasks.append(lambda g=g: agrp(g))
                    for hh in range(2):
                        tasks.append(lambda hh=hh: mgrp(hh))
                    return tasks

                def pair_proj(hp):
                    for t_ in pair_tasks(hp):
                        t_()

                steps = [(hp, hh, p) for hp in range(8) for hh in range(2) for p in range(NTILE)]

                def step_geom(p):
                    rlo, rhi = tile_rows(p)
                    n = (rhi - rlo + 1) * 64
                    chunks = []
                    c0 = 0
                    while c0 < n:
                        cn = min(512, n - c0)
                        chunks.append((c0, cn))
                        c0 += cn
                    return rlo, rhi, n, chunks

                def emit_qk(idx):
                    hp, hh, p = steps[idx]
                    pi = hp % 2
                    hb = 64 * hh
                    rlo, rhi, n, chunks = step_geom(p)
                    Sb, rS = Sbuf[idx % 2]
                    for _jk in range(NJUNK):
                        mm(Sb[:, 0:512], ident_s[:, :], uTv(0, 0, 512), True, True, [r_const] + all_uT, [rS[0]])
                    for ci, (c0, cn) in enumerate(chunks):
                        q0 = rlo * 64 + c0
                        mm(Sb[:, c0:c0 + cn], KT[pi][hh].ap[0:80, p * 128:(p + 1) * 128],
                           QT[pi][hh].ap[0:80, q0:q0 + cn], True, True,
                           KT[pi][hh].res + QT[pi][hh].res, [rS[ci]])

                def obank(hp, hh, b):
                    return O3[(2 * (2 * hp + hh) + b) % 3]

                def emit_expmult(idx):
                    hp, hh, p = steps[idx]
                    ti = hp % 2
                    rlo, rhi, n, chunks = step_geom(p)
                    Sb, rS = Sbuf[idx % 2]
                    xi = idx % 2
                    rSu = rS[0:len(chunks)]
                    act(expS[xi].ap[:, 0:n], Sb[:, 0:n], AF.Exp, rSu, expS[xi].res, scale=0.125)
                    slot0 = 11 - 2 * p + rlo
                    tab = Ttab[ti].ap[:, hh * 1024 + slot0 * 64:hh * 1024 + slot0 * 64 + n]
                    tt("dve", PT[xi].ap[:, 0:n], expS[xi].ap[:, 0:n], tab, ALU.mult,
                       expS[xi].res + Ttab[ti].res + [r_T[ti]], PT[xi].res)

                def emit_pv(idx):
                    hp, hh, p = steps[idx]
                    pi = hp % 2
                    hb = 64 * hh
                    pm = PmT[pi][hh]
                    rlo, rhi, n, chunks = step_geom(p)
                    xi = idx % 2
                    if p == 0:
                        if hh == 0:
                            vmeta = vmA[0:16, hp * 128:(hp + 1) * 128]
                            pmrows = (0, 16)
                        else:
                            vmeta = vmB[0:16, hp * 128:(hp + 1) * 128]
                            pmrows = (0, 16)
                        for b in range(2):
                            ob, ro = obank(hp, hh, b)
                            mm(ob, vmeta, pm.ap[pmrows[0]:pmrows[1], b * 512:(b + 1) * 512], True, False,
                               [r_meta] + pm.res, [ro], skip_group_check=True)
                    vt = Vt[p].ap
                    lhs = vt[:, 192 * hp + 64 * hh:192 * hp + 64 * hh + 128]
                    for b in range(2):
                        lo = max(rlo, 8 * b)
                        hi = min(rhi, 8 * b + 7)
                        if lo > hi:
                            continue
                        ob, ro = obank(hp, hh, b)
                        mm(ob[:, (lo - 8 * b) * 64:(hi - 8 * b + 1) * 64], lhs,
                           PT[xi].ap[:, (lo - rlo) * 64:(hi - rlo + 1) * 64], False, True,
                           Vt[p].res + PT[xi].res, [ro], skip_group_check=True)
                    for b, plast in ((0, 7), (1, NTILE - 1)):
                        if p != plast:
                            continue
                        ob, ro = obank(hp, hh, b)
                        o_lo, o_hi = hb, hb + 64
                        d_lo, d_hi = 64 - hb, 128 - hb
                        R = Rb[b]
                        at = attb[b]
                        for c4 in range(4):
                            cs = slice(c4 * 128, (c4 + 1) * 128)
                            pending.append(lambda ob=ob, R=R, ro=ro, cs=cs, d_lo=d_lo, d_hi=d_hi, o_lo=o_lo, o_hi=o_hi:
                                           sch.add("dve", lambda e: e.reciprocal(out=R.ap[o_lo:o_hi, cs], in_=ob[d_lo:d_hi, cs]),
                                                   [ro], R.res))

                        def fin(ob=ob, ro=ro, R=R, at=at, o_lo=o_lo, o_hi=o_hi, hp=hp, b=b, pi=pi):
                            act(at.ap[o_lo:o_hi, :], ob[o_lo:o_hi, :], AF.Identity, [ro], at.res, scale=0.5)
                            tt("pool", at.ap[o_lo:o_hi, :], at.ap[o_lo:o_hi, :], R.ap[o_lo:o_hi, :], ALU.mult,
                               at.res + R.res, at.res)
                            tt("pool", mixT[o_lo:o_hi, (8 + hp) * NM + b * 512:(8 + hp) * NM + (b + 1) * 512],
                               at.ap[o_lo:o_hi, :], AG[pi].ap[o_lo:o_hi, b * 512:(b + 1) * 512], ALU.mult,
                               at.res + AG[pi].res, [r_mix[8 + hp][b]])
                        pending.append(fin)

                pending = []
                projq = []
                for pi_ in range(2):
                    for hh_ in range(2):
                        kidx = pi_ * 2 + hh_
                        dma("pool", KT[pi_][hh_].ap[64:80, :], maskA[sgi], [], KT[pi_][hh_].res, own=r_mask[kidx])
                        dma("pool", QT[pi_][hh_].ap[64:80, :], conehot[:], [], QT[pi_][hh_].res, own=r_mask[4 + kidx])
                pair_proj(0)
                emit_qk(0)
                for idx in range(len(steps)):
                    hp, hh, p = steps[idx]
                    if hh == 1 and p == 2 and hp + 1 < 8:
                        pair_proj(hp + 1)
                    if idx + 1 < len(steps):
                        emit_qk(idx + 1)
                    emit_expmult(idx)
                    if pending:
                        pending.pop(0)()
                    if idx >= 1:
                        emit_pv(idx - 1)
                emit_pv(len(steps) - 1)
                while pending:
                    pending.pop(0)()


                if DBG and sgi == nseg - 1:
                    allmix = [r_mix[e_][g_] for e_ in range(16) for g_ in range(2)]
                    dma("sp", dbg_mix[:, :], mixT[:, :], allmix, [], own=r_init)
                dma("sp", woutS.ap, wout_bf[:, :], [r_scr], woutS.res + [r_wout], own=r_wout)
                for t in range(NM // 128):
                    Sb, rS = Sbuf[rot["S"] % 2]
                    rot["S"] += 1
                    oi = rot["out"] % 2
                    rot["out"] += 1
                    g = t // 4
                    dma("sp", xres[oi].ap, xe[sgi, M0 + t * 128:M0 + (t + 1) * 128, :], [],
                        xres[oi].res + [r_xres[oi]], own=r_xres[oi])
                    for half in range(2):
                        for e_ in range(16):
                            mm(Sb[:, half * 512:(half + 1) * 512], mixv(e_, t * 128, (t + 1) * 128),
                               woutS.ap[:, e_ * 1024 + half * 512:e_ * 1024 + (half + 1) * 512], e_ == 0, e_ == 15,
                               [r_mix[e_][g]] + woutS.res + [r_wout], [rS[half]])
                    si = cnt["st"] % 4
                    cnt["st"] += 1
                    st = stat[si]
                    ob = outb[oi]
                    act(ob.ap, Sb[:, :], AF.Square, rS, ob.res + [r_out[oi], r_stat[si]], accum_out=st[:, 0:1])
                    ts("dve", st[:, 1:2], st[:, 0:1], 1.0 / D, 1e-6, ALU.mult, ALU.add, [r_stat[si]], [r_stat[si]])
                    rsqrt_col(st, 128, [r_stat[si]])
                    stt("dve", ob.ap, Sb[:, :], st[:, 2:3], gpost_s[:], ALU.mult, ALU.mult,
                        rS + [r_stat[si], r_const], ob.res + [r_out[oi]])
                    tt("pool", ob.ap, ob.ap, xres[oi].ap, ALU.add, ob.res + [r_out[oi]] + xres[oi].res + [r_xres[oi]],
                       ob.res + [r_out[oi]])
                    dma("sp", ye[sgi, t * 128:(t + 1) * 128, :], ob.ap, ob.res + [r_out[oi]], [], own=r_out[oi])

        except StopBuild:
            pass

        def final_waits():
            allr = r_out + r_xt + r_w + r_mask + r_T + r_diag + [r_wout, r_init, r_tb, r_db] + r_xres
            return [(r.sem, r.cnt) for r in allr if r.cnt > 0] + [(g.sem, g.cnt) for g in (ginit, gconst) if g.cnt > 0]

        sch.check_deadlock(esem, final_waits)
        with nc.Block() as block:
            sch.emit(block, esem, final_waits)
    return nc


def _col_tables():
    key = np.arange(GW)
    q = np.arange(GW)
    start = np.clip(q - 8, 0, GW - 16)
    off = key[:, None] - start[None, :]
    valid = (off >= 0) & (off < 16)
    rel = np.clip(key[:, None] - q[None, :] + 15, 0, 30)
    return valid, rel


def host_weights(w_in, w_out, conv_w, conv_b, ln_g, ln_b, rel_bias, pre_g, post_g, meta_tokens):
    w_in = np.asarray(w_in, np.float32)[0]
    cols = np.arange(7168)
    vperm = np.zeros(1024, np.int64)
    for n in range(1024):
        if n < 512:
            head = 2 * (n // 64)
        else:
            head = 2 * ((n - 512) // 64) + 1
        vperm[n] = 5120 + head * 64 + n % 64
    cols[5120:6144] = vperm
    wp = w_in[:, cols]
    win = np.ascontiguousarray(wp.reshape(8, 128, 56, 128).transpose(2, 1, 0, 3)).reshape(56, 128, 1024)
    for g in range(2):
        blk = wp[:, 5120 + g * 512:5120 + (g + 1) * 512].reshape(8, 128, 512).transpose(1, 0, 2)
        win[40 + 4 * g:44 + 4 * g] = blk.reshape(128, 4, 1024).transpose(1, 0, 2)
    wo = np.asarray(w_out, np.float32)[0]
    wout = np.ascontiguousarray(wo.reshape(16, 128, 1024).transpose(1, 0, 2)).reshape(128, 16 * 1024)
    cw = np.asarray(conv_w, np.float32)[0]
    convwT = np.ascontiguousarray(cw.reshape(CONVW, 8, 128).transpose(2, 1, 0)).reshape(128, 8 * CONVW)
    chvec = np.concatenate([np.asarray(v, np.float32)[0].reshape(8, 128).T for v in (conv_b, ln_g, ln_b)], axis=1)
    gpre = np.ascontiguousarray(np.broadcast_to(np.asarray(pre_g, np.float32)[0][None, :], (128, D)))
    gpost = np.ascontiguousarray(np.broadcast_to(np.asarray(post_g, np.float32)[0][None, :], (128, D)))
    rb = np.asarray(rel_bias, np.float32)[0]
    valid, rel = _col_tables()
    Bt = np.zeros((128, 16, 16, 64), np.float32)
    Mt = np.zeros((128, 16, 64), np.float32)
    for half in range(2):
        for slot in range(16):
            d = 7 - slot
            dr = d + half
            if abs(dr) > 7:
                continue
            Bt[half * 64:(half + 1) * 64, :, slot, :] = rb[:, dr + 7, :][:, rel].transpose(1, 0, 2)
            Mt[half * 64:(half + 1) * 64, slot, :] = valid.astype(np.float32)
    conehot = np.zeros((16, NM), np.float32)
    for r in range(16):
        conehot[r, r * 64:(r + 1) * 64] = 1.0
    return {
        "win": win, "wout": wout, "convwT": convwT, "chvec": np.ascontiguousarray(chvec),
        "gpre": gpre, "gpost": gpost, "meta": np.ascontiguousarray(np.asarray(meta_tokens, np.float32)),
        "Bt": Bt.reshape(128, -1), "Mt": Mt.reshape(128, -1), "conehot": conehot,
        "ident": np.eye(128, dtype=np.float32),
    }


def host_segment(xseq, meta_tokens, R0):
    T = xseq.shape[0]
    rows = T // GW
    xe = np.zeros((NE, D), np.float32)
    t0 = R0 * GW - M0
    lo = max(0, t0)
    hi = min(T, t0 + NE)
    xe[lo - t0:hi - t0] = xseq[lo:hi]
    if t0 < 0:
        xe[-t0 - NMETA:-t0] = meta_tokens
    mask = np.full((16, NE), NEG, np.float32)
    wr = min(8, rows)
    for r in range(16):
        R = R0 + r
        sr = int(np.clip(R - wr // 2, 0, rows - wr))
        for kr in range(sr, sr + wr):
            er = kr - R0 + HALO
            if 0 <= er < EXT_ROWS:
                mask[r, er * GW:(er + 1) * GW] = 0.0
    return xe, mask


def run_segments(seg_lists, wts, n_cores):
    nseg = len(seg_lists[0])
    nc = build_program(nseg)
    in_maps = []
    for c in range(n_cores):
        xe = np.zeros((nseg, NE, D), np.float32)
        mk = np.zeros((nseg, 16, NE), np.float32)
        for s, (xseq, R0) in enumerate(seg_lists[c]):
            xe[s], mk[s] = host_segment(xseq, wts["meta"], R0)
        m = dict(wts)
        m["xe"] = xe
        m["maskA"] = mk
        in_maps.append(m)
    res = run_bass_kernel_spmd(nc, in_maps, core_ids=list(range(n_cores)))
    if int(os.environ.get('KDBG', '0')):
        return [r["ye"] for r in res.results], [(r["dbg_mix"], r["dbg_uT"], r["dbg_w"]) for r in res.results]
    return [r["ye"] for r in res.results]


def kernel(x_prompt, x_sample, meta_tokens, pre_norm_g, w_in, conv_w, conv_b, conv_ln_g, conv_ln_b,
           rel_bias, post_norm_g, w_out):
    x_prompt = np.asarray(x_prompt, np.float32)
    x_sample = np.asarray(x_sample, np.float32)
    wts = host_weights(w_in, w_out, conv_w, conv_b, conv_ln_g, conv_ln_b, rel_bias, pre_norm_g, post_norm_g,
                       meta_tokens)
    seg_lists = []
    where = []
    for c in range(NCORES):
        segs = []
        wh = []
        for i in range(4):
            b = 4 * c + i
            for R0 in (0, 16):
                segs.append((x_prompt[b], R0))
                wh.append((0, b, R0))
        sb_ = c // 2
        for R0 in (32 * (c % 2), 32 * (c % 2) + 16):
            segs.append((x_sample[sb_], R0))
            wh.append((1, sb_, R0))
        seg_lists.append(segs)
        where.append(wh)
    outs = run_segments(seg_lists, wts, NCORES)
    y_prompt = np.empty_like(x_prompt)
    y_sample = np.empty_like(x_sample)
    for c in range(NCORES):
        for s, (which, b, R0) in enumerate(where[c]):
            dst = y_prompt if which == 0 else y_sample
            dst[b, R0 * GW:(R0 + SEG_ROWS) * GW] = outs[c][s]
    return (y_prompt, y_sample)
```

```python
import os
import numpy as np
import concourse.bass as bass
import concourse.mybir as mybir
from concourse.bass_utils import run_bass_kernel_spmd

F32 = mybir.dt.float32
BF16 = mybir.dt.bfloat16
I32 = mybir.dt.int32
MAGIC = 0x5f3759df
ALU = mybir.AluOpType
AF = mybir.ActivationFunctionType

D = 1024
NMETA = 16
GW = 64
SEG_ROWS = 16
HALO = 4
EXT_ROWS = SEG_ROWS + 2 * HALO
NE = EXT_ROWS * GW
NM = SEG_ROWS * GW
M0 = HALO * GW
NTILE = NE // 128
CONVW = 31
PAD = 15
NUG = NM + 2 * PAD
NEG = -30000.0
NCORES = 8
WRING = 8


class Res:
    __slots__ = ("name", "lw", "rd", "sem", "cnt")

    def __init__(self, name):
        self.name = name
        self.lw = None
        self.rd = []
        self.sem = None
        self.cnt = 0


class Group:
    def __init__(self, sem):
        self.sem = sem
        self.cnt = 0


class Op:
    __slots__ = ("eng", "fn", "deps", "signal", "val", "dsem", "dval", "grp", "xw")

    def __init__(self, eng, fn):
        self.eng = eng
        self.fn = fn
        self.deps = []
        self.signal = False
        self.val = 0
        self.dsem = None
        self.dval = 0
        self.grp = None
        self.xw = None


class Sched:
    ENGS = ("pe", "act", "dve", "pool", "sp")

    def __init__(self, nc):
        self.nc = nc
        self.ops = {e: [] for e in self.ENGS}
        self.esem = {}
        self.out_ops = []

    def _deps(self, op, reads, writes):
        deps = []
        for r in reads:
            if r.lw is not None:
                deps.append(r.lw)
        for w in writes:
            if w.lw is not None:
                deps.append(w.lw)
            deps.extend(w.rd)
        seen = set()
        for d in deps:
            if id(d) in seen or d is op:
                continue
            seen.add(id(d))
            if d.dsem is None and d.grp is None:
                if d.eng == "pe" and op.eng == "pe":
                    continue
                d.signal = True
            op.deps.append(d)
        for r in reads:
            r.rd.append(op)
        for w in writes:
            w.lw = op
            w.rd = []

    def add(self, eng, fn, reads=(), writes=()):
        op = Op(eng, fn)
        self._deps(op, reads, writes)
        self.ops[eng].append(op)
        return op

    def dma(self, q, fn, reads=(), writes=(), own=None, grp=None):
        op = Op(q, fn)
        self._deps(op, reads, writes)
        if own is not None:
            own.cnt += 16
            op.dsem = own.sem
            op.dval = own.cnt
        else:
            op.deps = [d for d in op.deps if d.grp is not grp]
            grp.cnt += 16
            op.grp = grp
            if q == "pool" and grp.cnt > 16 * 24:
                op.xw = (grp.sem, grp.cnt - 16 * 24)
        self.ops[q].append(op)
        return op

    def check_deadlock(self, sems, final_waits):
        for e in self.ENGS:
            c = 0
            for op in self.ops[e]:
                if op.dsem is None and op.grp is None and op.signal:
                    c += 1
                    op.val = c

        def ev(d):
            if d.dsem is not None:
                return id(d.dsem), d.dval
            if d.grp is not None:
                return id(d.grp.sem), d.grp.cnt
            return id(sems[d.eng]), d.val
        prog = {}
        for e in self.ENGS:
            lst = []
            for op in self.ops[e]:
                waits = [ev(d) for d in op.deps]
                if op.xw is not None:
                    waits.append((id(op.xw[0]), op.xw[1]))
                if op.dsem is not None:
                    inc = (id(op.dsem), 16)
                elif op.grp is not None:
                    inc = (id(op.grp.sem), 16)
                elif op.signal:
                    inc = (id(sems[e]), 1)
                else:
                    inc = None
                lst.append((waits, inc))
            if e == "sp":
                lst.append(([(id(s_), v) for s_, v in final_waits()], None))
            prog[e] = lst
        val = {}
        pc = {e: 0 for e in self.ENGS}
        progress = True
        while progress:
            progress = False
            for e in self.ENGS:
                while pc[e] < len(prog[e]):
                    waits, inc = prog[e][pc[e]]
                    if all(val.get(s_, 0) >= v for s_, v in waits):
                        if inc is not None:
                            val[inc[0]] = val.get(inc[0], 0) + inc[1]
                        pc[e] += 1
                        progress = True
                    else:
                        break
        stuck = {e: (pc[e], len(prog[e])) for e in self.ENGS if pc[e] < len(prog[e])}
        if stuck:
            names = {id(v): k for k, v in sems.items()}
            msg = []
            for e, (p, n) in stuck.items():
                waits, _ = prog[e][p]
                msg.append(f"{e} stuck at {p}/{n} waiting " + str([(names.get(s_, s_), v, val.get(s_, 0)) for s_, v in waits if val.get(s_, 0) < v]))
            raise RuntimeError("DEADLOCK in schedule: " + "; ".join(msg))

    def emit(self, block, sems, final_waits):
        nc = self.nc
        for e in self.ENGS:
            c = 0
            for op in self.ops[e]:
                if op.dsem is None and op.grp is None and op.signal:
                    c += 1
                    op.val = c
        esem = sems
        sched = self

        def ev(d):
            if d.dsem is not None:
                return d.dsem, d.dval
            if d.grp is not None:
                return d.grp.sem, d.grp.cnt
            return esem[d.eng], d.val

        def run(e, eng):
            known = {}
            for op in sched.ops[e]:
                for d in op.deps:
                    s, v = ev(d)
                    key = id(s)
                    if known.get(key, 0) < v:
                        eng.wait_ge(s, v)
                        known[key] = v
                if op.xw is not None:
                    eng.wait_ge(op.xw[0], op.xw[1])
                ins = op.fn(eng)
                if op.dsem is not None:
                    ins.then_inc(op.dsem, 16)
                elif op.grp is not None:
                    ins.then_inc(op.grp.sem, 16)
                elif op.signal:
                    ins.then_inc(esem[e], 1)
            if e == "sp":
                for s, v in final_waits():
                    eng.wait_ge(s, v)

        @block.tensor
        def _(eng):
            run("pe", eng)

        @block.scalar
        def _(eng):
            run("act", eng)

        @block.vector
        def _(eng):
            run("dve", eng)

        @block.gpsimd
        def _(eng):
            run("pool", eng)

        @block.sync
        def _(eng):
            run("sp", eng)


def tile_rows(p):
    lo = max(0, 2 * p - 7)
    hi = min(SEG_ROWS - 1, 2 * p + 1)
    rows = set(range(lo, hi + 1)) if lo <= hi else set()
    if p <= 5:
        rows |= set(range(0, 4))
    if p >= 6:
        rows |= set(range(12, 16))
    rows = sorted(rows)
    assert rows == list(range(rows[0], rows[-1] + 1))
    return rows[0], rows[-1]


def build_program(nseg):
    NJUNK = int(os.environ.get('KJUNK', '0'))
    STAGE = int(os.environ.get('KSTAGE', '9'))
    nc = bass.Bass("TRN2", target_bir_lowering=False)

    def din(name, shape, dt=F32):
        return nc.dram_tensor(name, list(shape), dt, kind="ExternalInput").ap()

    xe = din("xe", [nseg, NE, D])
    maskA = din("maskA", [nseg, 16, NE])
    win = din("win", [56, 128, 1024])
    wout = din("wout", [128, 16 * 1024])
    convwT = din("convwT", [128, 8 * CONVW])
    chvec = din("chvec", [128, 24])
    gpre = din("gpre", [128, D])
    gpost = din("gpost", [128, D])
    meta = din("meta", [NMETA, D])
    Bt = din("Bt", [128, 16 * 16 * 64])
    Mt = din("Mt", [128, 16 * 64])
    conehot = din("conehot", [16, NM])
    identf = din("ident", [128, 128])
    ye = nc.dram_tensor("ye", [nseg, NM, D], F32, kind="ExternalOutput").ap()
    DBG = int(os.environ.get('KDBG', '0'))
    if DBG:
        dbg_mix = nc.dram_tensor("dbg_mix", [128, 16 * NM], BF16, kind="ExternalOutput").ap()
        dbg_uT = nc.dram_tensor("dbg_uT", [128, 8 * NE], BF16, kind="ExternalOutput").ap()
        dbg_w = nc.dram_tensor("dbg_w", [128, 1024], BF16, kind="ExternalOutput").ap()
    win_bf = nc.dram_tensor("win_bf", [56, 128, 1024], BF16, kind="Internal").ap()
    wout_bf = nc.dram_tensor("wout_bf", [128, 16 * 1024], BF16, kind="Internal").ap()
    T_bf = nc.dram_tensor("T_bf", [8, 128, 2048], BF16, kind="Internal").ap()
    diag_bf = nc.dram_tensor("diag_bf", [8, 128, CONVW * 128], BF16, kind="Internal").ap()

    import contextlib
    es = contextlib.ExitStack()
    with es:
        def sb(name, shape, dt):
            return es.enter_context(nc.sbuf_tensor(name, list(shape), dt))

        def ps(name, shape, dt):
            return es.enter_context(nc.psum_tensor(name, list(shape), dt))

        def sem(name):
            return es.enter_context(nc.semaphore(name))

        uT = sb("uT", [128, 8 * NE], BF16)
        mixT = sb("mixT", [128, 16 * NM], BF16)
        wring = sb("wring", [128, WRING * 1024], BF16)
        xt = [sb(f"xt{i}", [128, D], F32) for i in range(2)]
        ubf = [sb(f"ubf{i}", [128, D], BF16) for i in range(2)]
        gpre_s = sb("gpre_s", [128, D], F32)
        gpost_s = sb("gpost_s", [128, D], F32)
        ident_s = sb("ident_s", [128, 128], BF16)
        ones_s = sb("ones_s", [128, 128], BF16)
        kmeta = sb("kmeta", [64, 8 * 2 * 16], BF16)
        vmA = sb("vmA", [16, 8 * 128], BF16)
        vmB = sb("vmB", [16, 8 * 128], BF16)
        uTm = sb("uTm", [128, 8 * 16], BF16)
        chv = sb("chv", [128, 24], F32)
        cwT = sb("cwT", [128, 8 * CONVW], F32)
        halfg = sb("halfg", [128, 8], F32)
        halfb = sb("halfb", [128, 8], F32)
        stat = [sb(f"stat{i}", [128, 8], F32) for i in range(4)]
        A16N = 41984
        A32N = 4096
        ar16 = sb("ar16", [128, A16N], BF16)
        ar32 = sb("ar32", [128, A32N], F32)
        G16 = 512
        G32 = 256
        r16 = [Res(f"a16_{i}") for i in range((A16N + G16 - 1) // G16)]
        r32 = [Res(f"a32_{i}") for i in range((A32N + G32 - 1) // G32)]

        class Buf:
            def __init__(self, ap, res):
                self.ap = ap
                self.res = res

        def a16(off, n, parts=128):
            return Buf(ar16[0:parts, off:off + n], r16[off // G16:(off + n - 1) // G16 + 1])

        def a32(off, n, parts=128):
            return Buf(ar32[0:parts, off:off + n], r32[off // G32:(off + n - 1) // G32 + 1])

        UGP = 1056
        tmpb = [a16(i * 512, 512) for i in range(8)]
        UG0 = 4096 + NTILE * 1536
        gc2 = [a16(UG0 + j * 1024, 1024) for j in range(8)]
        ugT = [a16(UG0 + 8192 + i * UGP, UGP) for i in range(2)]
        DG0 = UG0 + 8192 + 2 * UGP
        diag = [a16(DG0 + i * 3968, 3968) for i in range(2)]
        assert DG0 + 2 * 3968 <= A16N
        stA = a32(0, 1024)
        stB = a32(1024, 1024)
        tf = [a32(2048 + i * 512, 512) for i in range(3)]
        Ttab = [a16(i * 2048, 2048) for i in range(2)]
        VW = 1536
        V0 = 4096
        Vt = [a16(V0 + t * VW, VW) for t in range(NTILE)]
        QK0 = V0 + NTILE * VW
        QKS = 2 * 1024 + 2 * NE + 1024
        QT = [[a16(QK0 + i * QKS + hh * 1024, 1024) for hh in range(2)] for i in range(2)]
        KT = [[a16(QK0 + i * QKS + 2048 + hh * NE, NE) for hh in range(2)] for i in range(2)]
        AG = [a16(QK0 + i * QKS + 2048 + 2 * NE, 1024) for i in range(2)]
        EX0 = QK0 + 2 * QKS
        expS = [a16(EX0 + i * 768, 768) for i in range(2)]
        PT = [a16(EX0 + 1536 + i * 768, 768) for i in range(2)]
        PM0 = EX0 + 3072
        PmT = [[a16(PM0 + i * 2048 + hh * 1024, 1024, parts=16) for hh in range(2)] for i in range(2)]
        assert PM0 + 4096 <= A16N, (PM0, A16N)
        Rb = [a32(i * 512, 512) for i in range(2)]
        attb = [a32(1024 + i * 512, 512) for i in range(2)]
        woutS = a16(0, 16384)
        outb = [a32(i * 1024, 1024) for i in range(2)]
        xres = [a32(2048 + i * 1024, 1024) for i in range(2)]
        btmp = a32(0, 2048)
        btmp2 = a32(2048, 2048)
        tbuild = a16(0, 2048)
        dbuild = a16(4096, 3968)
        xmeta = a32(0, 1024, parts=16)

        S0 = ps("S0", [128, 1024], F32)
        S1 = ps("S1", [128, 1024], F32)
        O0 = ps("O0", [128, 512], F32)
        O1 = ps("O1", [128, 512], F32)
        PJ = ps("PJ", [128, 512], F32)
        TB = ps("TB", [128, 1024], BF16)
        rS0a, rS0b, rS1a, rS1b, rO0, rO1, rPJ, rTB = [Res(n) for n in
                                                    ("S0a", "S0b", "S1a", "S1b", "O0", "O1", "PJ", "TB")]
        rPJa, rPJb = rPJ, Res("PJb")
        banks = [(S0[:, 0:512], rS0a), (S0[:, 512:1024], rS0b), (S1[:, 0:512], rS1a),
                 (S1[:, 512:1024], rS1b), (O0[:, :], rO0), (O1[:, :], rO1), (PJ[:, :], rPJ)]
        Sbuf = [(S0, [rS0a, rS0b]), (S1, [rS1a, rS1b])]
        Obuf = [(O0, rO0), (O1, rO1)]

        esem = {e: sem("s_" + e) for e in Sched.ENGS}
        sch = Sched(nc)
        ginit = Group(sem("g_init"))
        gconst = Group(sem("g_const"))

        def own(r, name):
            r.sem = sem(name)
            return r

        r_xt = [own(Res(f"xt{i}"), f"d_xt{i}") for i in range(2)]
        r_ubf = [Res(f"ubf{i}") for i in range(2)]
        r_w = [own(Res(f"w{i}"), f"d_w{i}") for i in range(WRING)]
        r_mask = [own(Res(f"mask{i}"), f"d_mask{i}") for i in range(8)]
        r_uT = [Res(f"uT{t}") for t in range(NTILE)]
        r_mix = [[Res(f"mix{e}_{g}") for g in range(2)] for e in range(16)]
        r_stat = [Res(f"stat{i}") for i in range(4)]
        r_const = Res("const")
        r_const2 = Res("const2")
        r_scr = Res("scratch")
        r_meta = Res("metabufs")
        r_T = [own(Res(f"T{i}"), f"d_T{i}") for i in range(2)]
        r_diag = [own(Res(f"dg{i}"), f"d_dg{i}") for i in range(2)]
        r_wout = own(Res("woutS"), "d_wout")
        r_xres = [own(Res(f"xres{i}"), f"d_xres{i}") for i in range(2)]
        r_out = [own(Res(f"out{i}"), f"d_out{i}") for i in range(2)]
        r_init = own(Res("initbuf"), "d_initbuf")
        r_tb = own(Res("tb"), "d_tb")
        r_db = own(Res("db"), "d_db")
        r_Tscr = [Res(f"Tscr{i}") for i in range(8)]
        r_dscr = [Res(f"dscr{i}") for i in range(8)]

        def mm(out, lhsT, rhs, start, stop, reads, writes, **kw):
            return sch.add("pe", lambda e: e.matmul(out, lhsT=lhsT, rhs=rhs, start=start, stop=stop, **kw),
                           reads, writes)

        def act(out, in_, func, reads, writes, **kw):
            return sch.add("act", lambda e: e.activation(out=out, in_=in_, func=func, **kw), reads, writes)

        def tt(eng, out, in0, in1, op, reads, writes):
            return sch.add(eng, lambda e: e.tensor_tensor(out=out, in0=in0, in1=in1, op=op), reads, writes)

        def ts(eng, out, in0, s1, s2, op0, op1, reads, writes):
            if s2 is None:
                return sch.add(eng, lambda e: e.tensor_scalar(out=out, in0=in0, scalar1=s1, scalar2=None, op0=op0),
                               reads, writes)
            return sch.add(eng, lambda e: e.tensor_scalar(out=out, in0=in0, scalar1=s1, scalar2=s2, op0=op0, op1=op1),
                           reads, writes)

        def stt(eng, out, in0, scalar, in1, op0, op1, reads, writes):
            return sch.add(eng, lambda e: e.scalar_tensor_tensor(out=out, in0=in0, scalar=scalar, in1=in1,
                                                                 op0=op0, op1=op1), reads, writes)

        def rsqrt(a, r, w, ra, rr, rw):
            ri = r.bitcast(I32)
            ts("dve", ri, a.bitcast(I32), 1, None, ALU.arith_shift_right, None, ra, rr)
            ts("dve", ri, ri, -1, MAGIC, ALU.mult, ALU.add, rr, rr)
            for _ in range(3):
                tt("dve", w, a, r, ALU.mult, ra + rr, rw)
                tt("dve", w, w, r, ALU.mult, rw + rr, rw)
                ts("dve", w, w, -0.5, 1.5, ALU.mult, ALU.add, rw, rw)
                tt("dve", r, r, w, ALU.mult, rr + rw, rr)

        def rsqrt_col(st, n, rr):
            a, r, h, t = st[0:n, 1:2], st[0:n, 2:3], st[0:n, 3:4], st[0:n, 4:5]
            ri = r.bitcast(I32)
            ts("dve", ri, a.bitcast(I32), 1, None, ALU.arith_shift_right, None, rr, rr)
            ts("dve", ri, ri, -1, MAGIC, ALU.mult, ALU.add, rr, rr)
            ts("dve", h, a, -0.5, None, ALU.mult, None, rr, rr)
            for _ in range(3):
                ts("dve", t, h, r, r, ALU.mult, ALU.mult, rr, rr)
                stt("dve", r, t, 1.5, r, ALU.add, ALU.mult, rr, rr)

        def cp(eng, out, in_, reads, writes):
            if eng == "act":
                return sch.add("act", lambda e: e.copy(out=out, in_=in_), reads, writes)
            return sch.add(eng, lambda e: e.tensor_copy(out=out, in_=in_), reads, writes)

        def dma(q, out, in_, reads, writes, own=None, grp=None):
            return sch.dma(q, lambda e: e.dma_start(out=out, in_=in_), reads, writes, own=own, grp=grp)

        def uTv(k, a, b):
            return uT[:, k * NE + a:k * NE + b]

        def mixv(e, a, b):
            return mixT[:, e * NM + a:e * NM + b]

        def wv(slot, k):
            return wring[:, slot * 1024 + k * 128:slot * 1024 + (k + 1) * 128]

        wseq = []
        for s in range(nseg):
            for j in range(8):
                wseq += [8 + j, j]
            for j in range(8):
                wseq += [16 + j]
            while len(wseq) % 4:
                wseq.append(None)
            wseq += [40, 41, 42, 43, 44, 45, 46, 47]
            for hp in range(8):
                wseq += [24 + hp, 32 + hp, 48 + hp]
        wpre = [32 + hp for hp in range(8)] + [40, 41, 42, 43, 44, 45, 46, 47]
        wseq = wpre + wseq
        wstate = {"loaded": 0, "used": 0}

        def w_load_upto(n):
            while wstate["loaded"] < min(n, len(wseq)):
                i = wstate["loaded"]
                e = wseq[i]
                if e is not None:
                    slot = i % WRING
                    dma("sp", wring[:, slot * 1024:(slot + 1) * 1024], win_bf[e],
                        [r_scr], [r_w[slot]], own=r_w[slot])
                wstate["loaded"] += 1

        def w_next(expect):
            i = wstate["used"]
            while wseq[i] is None:
                i += 1
            assert wseq[i] == expect, (i, wseq[i], expect)
            w_load_upto(i + 1)
            wstate["used"] = i + 1
            return i % WRING

        def w_prefetch(k=4):
            assert k <= WRING
            w_load_upto(wstate["used"] + k)

        class StopBuild(Exception):
            pass

        def stage(k):
            if STAGE < k:
                raise StopBuild()

        try:
            for e in range(int(os.environ.get('KNW', '56'))):
                dma("pool", win_bf[e], win[e], [], [r_scr], grp=ginit)
            for q in range(int(os.environ.get('KNO', '16'))):
                dma("pool", wout_bf[:, q * 1024:(q + 1) * 1024], wout[:, q * 1024:(q + 1) * 1024], [], [r_scr], grp=ginit)
            stage(-3)
            dma("sp", gpre_s[:], gpre[:], [], [r_const], grp=gconst)
            dma("sp", gpost_s[:], gpost[:], [], [r_const], grp=gconst)
            dma("sp", chv[:], chvec[:], [], [r_const], grp=gconst)
            dma("sp", cwT[:], convwT[:], [], [r_const], grp=gconst)
            dma("pool", ident_s[:], identf[:], [], [r_const], grp=gconst)
            Mt_s = sb("Mt_s", [128, 1024], F32)
            dma("sp", Mt_s[:], Mt[:], [], [r_const], grp=gconst)
            sch.add("dve", lambda e: e.memset(ones_s[:], 1.0 / 1024.0), [], [r_const2])
            sch.add("dve", lambda e: e.memset(vmA[:], 1.0), [], [r_meta])
            sch.add("dve", lambda e: e.memset(vmB[:], 1.0), [], [r_meta])
            ts("dve", halfg[:], chv[:, 8:16], 0.5, None, ALU.mult, None, [r_const], [r_const2])
            ts("dve", halfb[:], chv[:, 16:24], 0.5, None, ALU.mult, None, [r_const], [r_const2])

            stage(-2)
            for hp in range(8):
                dma("sp", btmp.ap, Bt[:, hp * 2048:(hp + 1) * 2048], [], btmp.res, own=r_init)
                act(btmp2.ap, btmp.ap, AF.Exp, btmp.res, btmp2.res)
                for hh in range(2):
                    tt("dve", tbuild.ap[:, hh * 1024:(hh + 1) * 1024], btmp2.ap[:, hh * 1024:(hh + 1) * 1024],
                       Mt_s[:], ALU.mult, btmp2.res + [r_const], tbuild.res)
                dma("sp", T_bf[hp], tbuild.ap, tbuild.res, [r_Tscr[hp]], own=r_tb)
            stage(-1)
            for j in range(8):
                for s in range(CONVW):
                    eng = "dve" if (s % 2 == 0) else "pool"
                    ts(eng, dbuild.ap[:, s * 128:(s + 1) * 128], ident_s[:],
                       cwT[:, j * CONVW + s:j * CONVW + s + 1], 0.5, ALU.mult, ALU.mult,
                       [r_const], dbuild.res)
                dma("sp", diag_bf[j], dbuild.ap, dbuild.res, [r_dscr[j]], own=r_db)

            cnt = {"x": 0, "st": 0, "cpy": 0, "pj": 0}

            def p0_tile(src_ap, nrows, dst_fn, dst_res, xbuf=None, xres_=None):
                i = cnt["x"] % 2
                cnt["x"] += 1
                si = cnt["st"] % 4
                cnt["st"] += 1
                if xbuf is None:
                    xb, xr = xt[i][0:nrows, :], [r_xt[i]]
                    dma("sp", xb, src_ap, [], xr, own=r_xt[i])
                else:
                    xb, xr = xbuf, xres_
                ub = ubf[i][0:nrows, :]
                st = stat[si]
                act(ub, xb, AF.Square, xr, [r_ubf[i], r_stat[si]], accum_out=st[0:nrows, 0:1])
                ts("dve", st[0:nrows, 1:2], st[0:nrows, 0:1], 1.0 / D, 1e-6, ALU.mult, ALU.add, [r_stat[si]], [r_stat[si]])
                rsqrt_col(st, nrows, [r_stat[si]])
                stt("dve", ub, xb, st[0:nrows, 2:3], gpre_s[0:nrows, :], ALU.mult, ALU.mult,
                    xr + [r_stat[si], r_const], [r_ubf[i]])
                for k in range(8):
                    sch.add("pe", lambda e, k=k: e.transpose(out=TB[:, k * 128:k * 128 + nrows],
                                                               in_=ubf[i][0:nrows, k * 128:(k + 1) * 128],
                                                               identity=ident_s[0:nrows, 0:nrows]),
                            [r_ubf[i], r_const], [rTB])
                ceng = "act" if cnt["cpy"] % 2 == 0 else "dve"
                cnt["cpy"] += 1
                src = TB[:].rearrange("p (k t) -> p k t", t=128)[:, :, 0:nrows]
                cp(ceng, dst_fn(), src, [rTB], dst_res)

            stage(1)
            ssb = sb("ssb", [128, 64], F32)
            r_ssb = Res("ssb")

            def p0_a_tiles(sgi, tiles):
                for t in tiles:
                    i = cnt["x"] % 2
                    cnt["x"] += 1
                    dma("sp", xt[i][:], xe[sgi, t * 128:(t + 1) * 128, :], [], [r_xt[i]], own=r_xt[i])
                    act(ubf[i][:], xt[i][:], AF.Square, [r_xt[i]], [r_ubf[i], r_ssb], accum_out=ssb[:, t:t + 1])

            def p0_chain():
                a_, r_, h_, t_ = ssb[:, 16:28], ssb[:, 32:44], ssb[:, 48:60], ssb[:, 0:12]
                rr = [r_ssb]
                ts("dve", a_, ssb[:, 0:12], 1.0 / D, 1e-6, ALU.mult, ALU.add, rr, rr)
                ri = r_.bitcast(I32)
                ts("dve", ri, a_.bitcast(I32), 1, None, ALU.arith_shift_right, None, rr, rr)
                ts("dve", ri, ri, -1, MAGIC, ALU.mult, ALU.add, rr, rr)
                ts("dve", h_, a_, -0.5, None, ALU.mult, None, rr, rr)
                for _ in range(3):
                    tt("dve", t_, h_, r_, ALU.mult, rr, rr)
                    tt("dve", t_, t_, r_, ALU.mult, rr, rr)
                    stt("dve", r_, t_, 1.5, r_, ALU.add, ALU.mult, rr, rr)

            def p0_c(sgi, tiles):
                for t in tiles:
                    i = cnt["x"] % 2
                    cnt["x"] += 1
                    dma("sp", xt[i][:], xe[sgi, t * 128:(t + 1) * 128, :], [], [r_xt[i]], own=r_xt[i])
                    stt("dve", ubf[i][:], xt[i][:], ssb[:, 32 + t:33 + t], gpre_s[:], ALU.mult, ALU.mult,
                        [r_xt[i], r_ssb, r_const], [r_ubf[i]])
                    if t % 2 == 0:
                        tbank, tres = TB[:], [rTB]
                    else:
                        tbank, tres = PJ[:].bitcast(BF16), [rPJa, rPJb]
                    for k in range(8):
                        sch.add("pe", lambda e, k=k, i=i, tbank=tbank: e.transpose(out=tbank[:, k * 128:(k + 1) * 128],
                                                                                    in_=ubf[i][:, k * 128:(k + 1) * 128],
                                                                                    identity=ident_s[:, :]),
                                [r_ubf[i], r_const], tres)
                    ceng = "act" if t % 2 == 0 else "dve"
                    src = tbank.rearrange("p (k t) -> p k t", t=128)
                    dst = uT[:].rearrange("p (k n) -> p k n", n=NE)[:, :, t * 128:(t + 1) * 128]
                    cp(ceng, dst, src, tres, [r_uT[t]])

            def p0_batched(sgi):
                p0_a_tiles(sgi, range(NTILE))
                p0_chain()
                p0_c(sgi, range(NTILE))

            dma("sp", xmeta.ap, meta[:], [], xmeta.res, own=r_init)
            p0_tile(None, NMETA, lambda: uTm[:].rearrange("p (k t) -> p k t", t=16), [r_meta],
                    xbuf=xmeta.ap, xres_=xmeta.res)
            for hp in range(8):
                slot = w_next(32 + hp)
                for k in range(8):
                    mm(PJ[:, 0:16], wv(slot, k), uTm[:, k * 16:(k + 1) * 16], k == 0, k == 7,
                       [r_w[slot], r_meta], [rPJ])
                cp("dve", kmeta[0:64, hp * 32:hp * 32 + 16], PJ[0:64, 0:16], [rPJ], [r_meta])
                cp("dve", kmeta[0:64, hp * 32 + 16:hp * 32 + 32], PJ[64:128, 0:16], [rPJ], [r_meta])
                w_prefetch(4)
            for g in range(2):
                slots = [w_next(40 + 4 * g + q) for q in range(4)]
                assert slots[0] % 4 == 0
                for k in range(8):
                    rhs = wring[:, slots[0] * 1024 + k * 512:slots[0] * 1024 + (k + 1) * 512]
                    mm(PJ[0:16, 0:512], uTm[:, k * 16:(k + 1) * 16], rhs, k == 0, k == 7,
                       [r_w[s_] for s_ in slots] + [r_meta], [rPJ])
                if g == 0:
                    cp("dve", vmA[0:16, :].rearrange("p (i c) -> p i c", c=128)[:, :, 0:64],
                       PJ[0:16, 0:512].rearrange("p (i c) -> p i c", c=64), [rPJ], [r_meta])
                else:
                    cp("dve", vmB[0:16, :].rearrange("p (i c) -> p i c", c=128)[:, :, 64:128],
                       PJ[0:16, 0:512].rearrange("p (i c) -> p i c", c=64), [rPJ], [r_meta])

            rot = {"pair": 0, "S": 0, "ex": 0, "out": 0, "tmp": 0, "tf": 0, "dg": 0, "ug": 0, "T": 0, "mask": 0}

            for sgi in range(nseg):
                mi = rot["mask"] % 2
                rot["mask"] += 1

                stage(2)
                if sgi == 0:
                    p0_batched(sgi)
                else:
                    p0_c(sgi, range(NTILE))
                nxt = sgi + 1 if sgi + 1 < nseg else None
                all_uT = list(r_uT)
                if DBG and sgi == nseg - 1:
                    dma("sp", dbg_uT[:, :], uT[:, :], all_uT, [], own=r_init)
                    dma("sp", dbg_w[:, :], win_bf[8], [r_scr], [], own=r_init)

                stage(3)
                ug_groups = [(M0 - PAD, 512), (M0 - PAD + 512, 512), (M0 - PAD + 1024, 2 * PAD)]
                w_prefetch(4)
                for j in range(8):
                    ui = rot["ug"] % 2
                    rot["ug"] += 1
                    sg_ = w_next(8 + j)
                    sv_ = w_next(j)
                    di = rot["dg"] % 2
                    rot["dg"] += 1
                    dma("sp", diag[di].ap, diag_bf[j], [r_dscr[j]], diag[di].res + [r_diag[di]], own=r_diag[di])
                    for gi, (e0, n) in enumerate(ug_groups):
                        bg, rg = banks[(2 * gi) % 4]
                        bv, rv = banks[(2 * gi + 1) % 4]
                        for k in range(8):
                            mm(bg[:, 0:n], wv(sg_, k), uTv(k, e0, e0 + n), k == 0, k == 7, [r_w[sg_]] + all_uT, [rg])
                        for k in range(8):
                            mm(bv[:, 0:n], wv(sv_, k), uTv(k, e0, e0 + n), k == 0, k == 7, [r_w[sv_]] + all_uT, [rv])
                        tb = tmpb[rot["tmp"] % 8]
                        rot["tmp"] += 1
                        act(tb.ap[:, 0:n], bg[:, 0:n], AF.Tanh, [rg], tb.res, scale=0.5)
                        o0 = e0 - (M0 - PAD)
                        stt("dve", ugT[ui].ap[:, o0:o0 + n], tb.ap[:, 0:n], 1.0, bv[:, 0:n], ALU.add, ALU.mult,
                            tb.res + [rv], ugT[ui].res)
                    for g in range(2):
                        by, ry = Obuf[g]
                        for s in range(CONVW):
                            mm(by[:, :], diag[di].ap[:, s * 128:(s + 1) * 128],
                               ugT[ui].ap[:, g * 512 + s:g * 512 + s + 512], s == 0, s == CONVW - 1,
                               diag[di].res + [r_diag[di]] + ugT[ui].res, [ry])
                        act(mixv(j, g * 512, (g + 1) * 512), by[:, :], AF.Identity, [ry, r_const], [r_mix[j][g]],
                            bias=chv[:, j:j + 1])
                    if g == 1:
                        w_prefetch(4)
                        if nxt is not None and j < 6:
                            p0_a_tiles(nxt, [2 * j, 2 * j + 1])
                stage(4)
                for g in range(2):
                    bm, rm = banks[0]
                    bq, rq = banks[1]
                    for j in range(8):
                        tb = tmpb[rot["tmp"] % 8]
                        rot["tmp"] += 1
                        act(tb.ap, mixv(j, g * 512, (g + 1) * 512), AF.Square, [r_mix[j][g]], tb.res)
                        mm(bm, ones_s[:], mixv(j, g * 512, (g + 1) * 512), j == 0, j == 7, [r_const2, r_mix[j][g]], [rm])
                        mm(bq, ones_s[:], tb.ap, j == 0, j == 7, [r_const2] + tb.res, [rq])
                    A_ = stA.ap[:, g * 512:(g + 1) * 512]
                    B_ = stB.ap[:, g * 512:(g + 1) * 512]
                    t0 = tf[0]
                    act(t0.ap, bm, AF.Square, [rm], t0.res)
                    tt("dve", t0.ap, bq, t0.ap, ALU.subtract, [rq] + t0.res, t0.res)
                    ts("dve", t0.ap, t0.ap, 1e-5, None, ALU.add, None, t0.res, t0.res)
                    rsqrt(t0.ap, A_, tf[1].ap, t0.res, stA.res, tf[1].res)
                    stt("dve", B_, bm, -1.0, A_, ALU.mult, ALU.mult, [rm] + stA.res, stB.res)
                stage(5)
                if nxt is not None:
                    p0_chain()
                w_prefetch(4)
                for j in range(8):
                    sc_ = w_next(16 + j)
                    for g in range(2):
                        bc, rc = banks[2 + g]
                        for k in range(8):
                            mm(bc, wv(sc_, k), uTv(k, M0 + g * 512, M0 + (g + 1) * 512), k == 0, k == 7,
                               [r_w[sc_]] + all_uT, [rc])
                        tb = tmpb[rot["tmp"] % 8]
                        rot["tmp"] += 1
                        act(tb.ap, bc, AF.Tanh, [rc], tb.res, scale=0.5)
                        stt("dve", gc2[j].ap[:, g * 512:(g + 1) * 512], tb.ap, 1.0, bc, ALU.add, ALU.mult,
                            tb.res + [rc], gc2[j].res)
                    w_prefetch(4)

                def ln_apply(j, g):
                    A_ = stA.ap[:, g * 512:(g + 1) * 512]
                    B_ = stB.ap[:, g * 512:(g + 1) * 512]
                    t1 = tf[1 + (rot["tf"] % 2)]
                    rot["tf"] += 1
                    mv = mixv(j, g * 512, (g + 1) * 512)
                    tt("pool", t1.ap, mv, A_, ALU.mult, [r_mix[j][g]] + stA.res, t1.res)
                    tt("pool", t1.ap, t1.ap, B_, ALU.add, t1.res + stB.res, t1.res)
                    tb2 = tmpb[rot["tmp"] % 8]
                    rot["tmp"] += 1
                    tb3 = tmpb[rot["tmp"] % 8]
                    rot["tmp"] += 1
                    act(tb2.ap, t1.ap, AF.Tanh, t1.res + [r_const2], tb2.res,
                        scale=halfg[:, j:j + 1], bias=halfb[:, j:j + 1])
                    act(tb3.ap, t1.ap, AF.Identity, t1.res + [r_const], tb3.res,
                        scale=chv[:, 8 + j:9 + j], bias=chv[:, 16 + j:17 + j])
                    stt("dve", t1.ap, tb2.ap, 1.0, tb3.ap, ALU.add, ALU.mult, tb2.res + tb3.res, t1.res)
                    stt("dve", mv, t1.ap, 0.25, gc2[j].ap[:, g * 512:(g + 1) * 512], ALU.mult, ALU.mult,
                        t1.res + gc2[j].res, [r_mix[j][g]])

                stage(6)
                for t in range(NTILE):
                    sch.add("pool", lambda e, t=t: e.memset(
                        Vt[t].ap.rearrange("p (i c) -> p i c", c=192)[:, :, 64:128], 1.0), [], Vt[t].res)
                vsl = [[w_next(40 + 4 * g + q) for q in range(4)] for g in range(2)]
                assert vsl[0][0] % 4 == 0 and vsl[1][0] % 4 == 0
                lnq = [(j, g) for j in range(8) for g in range(2)]
                nv = 0
                for g in range(2):
                    s0 = vsl[g][0]
                    for t in range(NTILE):
                        bv, rv = banks[4 + (t % 2)]
                        for k in range(8):
                            rhs = wring[:, s0 * 1024 + k * 512:s0 * 1024 + (k + 1) * 512]
                            mm(bv, uTv(k, t * 128, (t + 1) * 128), rhs, k == 0, k == 7,
                               [r_w[s_] for s_ in vsl[g]] + [r_uT[t]], [rv])
                        v3 = Vt[t].ap.rearrange("p (i c) -> p i c", c=192)
                        dst = v3[:, :, 0:64] if g == 0 else v3[:, :, 128:192]
                        cp("act", dst, bv.rearrange("p (i c) -> p i c", c=64), [rv], Vt[t].res)
                        nv += 1
                        while lnq and (16 - len(lnq)) * 24 < nv * 16:
                            ln_apply(*lnq.pop(0))
                while lnq:
                    ln_apply(*lnq.pop(0))
                w_prefetch(4)

                O3 = [(O0[:, :], rO0), (O1[:, :], rO1), (TB[:].bitcast(F32), rTB)]

                def pair_tasks(hp):
                    pi = hp % 2
                    ti = hp % 2
                    bp, rp = banks[6]
                    st_ = {"n": 0}
                    tasks = []
                    rotb = [banks[6], banks[1], banks[3]]

                    def nextbank():
                        st_["n"] += 1
                        return rotb[st_["n"] % 3]

                    def setup():
                        st_["q"] = w_next(24 + hp)
                        st_["k"] = w_next(32 + hp)
                        st_["a"] = w_next(48 + hp)
                        dma("sp", Ttab[ti].ap, T_bf[hp], [r_Tscr[hp]], Ttab[ti].res + [r_T[ti]], own=r_T[ti])

                    def qgrp(g):
                        bp, rp = nextbank()
                        sq_ = st_["q"]
                        for k in range(8):
                            mm(bp, wv(sq_, k), uTv(k, M0 + g * 512, M0 + (g + 1) * 512), k == 0, k == 7,
                               [r_w[sq_]] + all_uT, [rp])
                        cp("act", QT[pi][0].ap[0:64, g * 512:(g + 1) * 512], bp[0:64, :], [rp], QT[pi][0].res)
                        cp("act", QT[pi][1].ap[0:64, g * 512:(g + 1) * 512], bp[64:128, :], [rp], QT[pi][1].res)

                    def kgrp(g):
                        bp, rp = nextbank()
                        sk_ = st_["k"]
                        for k in range(8):
                            mm(bp, wv(sk_, k), uTv(k, g * 512, (g + 1) * 512), k == 0, k == 7,
                               [r_w[sk_]] + all_uT, [rp])
                        cp("dve", KT[pi][0].ap[0:64, g * 512:(g + 1) * 512], bp[0:64, :], [rp], KT[pi][0].res)
                        cp("act", KT[pi][1].ap[0:64, g * 512:(g + 1) * 512], bp[64:128, :], [rp], KT[pi][1].res)

                    def agrp(g):
                        bp, rp = nextbank()
                        sa_ = st_["a"]
                        for k in range(8):
                            mm(bp, wv(sa_, k), uTv(k, M0 + g * 512, M0 + (g + 1) * 512), k == 0, k == 7,
                               [r_w[sa_]] + all_uT, [rp])
                        act(AG[pi].ap[:, g * 512:(g + 1) * 512], bp, AF.Tanh, [rp], AG[pi].res, scale=0.5)
                        stt("dve", AG[pi].ap[:, g * 512:(g + 1) * 512], AG[pi].ap[:, g * 512:(g + 1) * 512], 1.0, bp,
                            ALU.add, ALU.mult, AG[pi].res + [rp], AG[pi].res)
                        if g == 1:
                            w_prefetch(4)

                    def mgrp(hh):
                        pm = PmT[pi][hh]
                        for b in range(2):
                            mm(bp[0:16, :], kmeta[0:64, hp * 32 + hh * 16:hp * 32 + hh * 16 + 16],
                               QT[pi][hh].ap[0:64, b * 512:(b + 1) * 512], True, True,
                               [r_meta] + QT[pi][hh].res, [rp])
                            act(pm.ap[:, b * 512:(b + 1) * 512], bp[0:16, :], AF.Exp, [rp], pm.res, scale=0.125)

                    tasks.append(lambda: (setup(), qgrp(0)))
                    tasks.append(lambda: qgrp(1))
                    for g in range(3):
                        tasks.append(lambda g=g: kgrp(g))
                    for g in range(2):
                        tasks.append(lambda g=g: agrp(g))
                    for hh in range(2):
                        tasks.append(lambda hh=hh: mgrp(hh))
                    return tasks

                def pair_proj(hp):
                    for t_ in pair_tasks(hp):
                        t_()

                steps = [(hp, hh, p) for hp in range(8) for hh in range(2) for p in range(NTILE)]

                def step_geom(p):
                    rlo, rhi = tile_rows(p)
                    n = (rhi - rlo + 1) * 64
                    chunks = []
                    c0 = 0
                    while c0 < n:
                        cn = min(512, n - c0)
                        chunks.append((c0, cn))
                        c0 += cn
                    return rlo, rhi, n, chunks

                def emit_qk(idx):
                    hp, hh, p = steps[idx]
                    pi = hp % 2
                    hb = 64 * hh
                    rlo, rhi, n, chunks = step_geom(p)
                    Sb, rS = Sbuf[idx % 2]
                    for _jk in range(NJUNK):
                        mm(Sb[:, 0:512], ident_s[:, :], uTv(0, 0, 512), True, True, [r_const] + all_uT, [rS[0]])
                    for ci, (c0, cn) in enumerate(chunks):
                        q0 = rlo * 64 + c0
                        mm(Sb[:, c0:c0 + cn], KT[pi][hh].ap[0:80, p * 128:(p + 1) * 128],
                           QT[pi][hh].ap[0:80, q0:q0 + cn], True, True,
                           KT[pi][hh].res + QT[pi][hh].res, [rS[ci]])

                def obank(hp, hh, b):
                    return O3[(2 * (2 * hp + hh) + b) % 3]

                def emit_expmult(idx):
                    hp, hh, p = steps[idx]
                    ti = hp % 2
                    rlo, rhi, n, chunks = step_geom(p)
                    Sb, rS = Sbuf[idx % 2]
                    xi = idx % 2
                    rSu = rS[0:len(chunks)]
                    act(expS[xi].ap[:, 0:n], Sb[:, 0:n], AF.Exp, rSu, expS[xi].res, scale=0.125)
                    slot0 = 11 - 2 * p + rlo
                    tab = Ttab[ti].ap[:, hh * 1024 + slot0 * 64:hh * 1024 + slot0 * 64 + n]
                    tt("dve", PT[xi].ap[:, 0:n], expS[xi].ap[:, 0:n], tab, ALU.mult,
                       expS[xi].res + Ttab[ti].res + [r_T[ti]], PT[xi].res)

                def emit_pv(idx):
                    hp, hh, p = steps[idx]
                    pi = hp % 2
                    hb = 64 * hh
                    pm = PmT[pi][hh]
                    rlo, rhi, n, chunks = step_geom(p)
                    xi = idx % 2
                    if p == 0:
                        if hh == 0:
                            vmeta = vmA[0:16, hp * 128:(hp + 1) * 128]
                            pmrows = (0, 16)
                        else:
                            vmeta = vmB[0:16, hp * 128:(hp + 1) * 128]
                            pmrows = (0, 16)
                        for b in range(2):
                            ob, ro = obank(hp, hh, b)
                            mm(ob, vmeta, pm.ap[pmrows[0]:pmrows[1], b * 512:(b + 1) * 512], True, False,
                               [r_meta] + pm.res, [ro], skip_group_check=True)
                    vt = Vt[p].ap
                    lhs = vt[:, 192 * hp + 64 * hh:192 * hp + 64 * hh + 128]
                    for b in range(2):
                        lo = max(rlo, 8 * b)
                        hi = min(rhi, 8 * b + 7)
                        if lo > hi:
                            continue
                        ob, ro = obank(hp, hh, b)
                        mm(ob[:, (lo - 8 * b) * 64:(hi - 8 * b + 1) * 64], lhs,
                           PT[xi].ap[:, (lo - rlo) * 64:(hi - rlo + 1) * 64], False, True,
                           Vt[p].res + PT[xi].res, [ro], skip_group_check=True)
                    for b, plast in ((0, 7), (1, NTILE - 1)):
                        if p != plast:
                            continue
                        ob, ro = obank(hp, hh, b)
                        o_lo, o_hi = hb, hb + 64
                        d_lo, d_hi = 64 - hb, 128 - hb
                        R = Rb[b]
                        at = attb[b]
                        for c4 in range(4):
                            cs = slice(c4 * 128, (c4 + 1) * 128)
                            pending.append(lambda ob=ob, R=R, ro=ro, cs=cs, d_lo=d_lo, d_hi=d_hi, o_lo=o_lo, o_hi=o_hi:
                                           sch.add("dve", lambda e: e.reciprocal(out=R.ap[o_lo:o_hi, cs], in_=ob[d_lo:d_hi, cs]),
                                                   [ro], R.res))

                        def fin(ob=ob, ro=ro, R=R, at=at, o_lo=o_lo, o_hi=o_hi, hp=hp, b=b, pi=pi):
                            act(at.ap[o_lo:o_hi, :], ob[o_lo:o_hi, :], AF.Identity, [ro], at.res, scale=0.5)
                            tt("pool", at.ap[o_lo:o_hi, :], at.ap[o_lo:o_hi, :], R.ap[o_lo:o_hi, :], ALU.mult,
                               at.res + R.res, at.res)
                            tt("pool", mixT[o_lo:o_hi, (8 + hp) * NM + b * 512:(8 + hp) * NM + (b + 1) * 512],
                               at.ap[o_lo:o_hi, :], AG[pi].ap[o_lo:o_hi, b * 512:(b + 1) * 512], ALU.mult,
                               at.res + AG[pi].res, [r_mix[8 + hp][b]])
                        pending.append(fin)

                pending = []
                projq = []
                for pi_ in range(2):
                    for hh_ in range(2):
                        kidx = pi_ * 2 + hh_
                        dma("pool", KT[pi_][hh_].ap[64:80, :], maskA[sgi], [], KT[pi_][hh_].res, own=r_mask[kidx])
                        dma("pool", QT[pi_][hh_].ap[64:80, :], conehot[:], [], QT[pi_][hh_].res, own=r_mask[4 + kidx])
                pair_proj(0)
                emit_qk(0)
                for idx in range(len(steps)):
                    hp, hh, p = steps[idx]
                    if hh == 1 and p == 2 and hp + 1 < 8:
                        pair_proj(hp + 1)
                    if idx + 1 < len(steps):
                        emit_qk(idx + 1)
                    emit_expmult(idx)
                    if pending:
                        pending.pop(0)()
                    if idx >= 1:
                        emit_pv(idx - 1)
                emit_pv(len(steps) - 1)
                while pending:
                    pending.pop(0)()


                if DBG and sgi == nseg - 1:
                    allmix = [r_mix[e_][g_] for e_ in range(16) for g_ in range(2)]
                    dma("sp", dbg_mix[:, :], mixT[:, :], allmix, [], own=r_init)
                dma("sp", woutS.ap, wout_bf[:, :], [r_scr], woutS.res + [r_wout], own=r_wout)
                for t in range(NM // 128):
                    Sb, rS = Sbuf[rot["S"] % 2]
                    rot["S"] += 1
                    oi = rot["out"] % 2
                    rot["out"] += 1
                    g = t // 4
                    dma("sp", xres[oi].ap, xe[sgi, M0 + t * 128:M0 + (t + 1) * 128, :], [],
                        xres[oi].res + [r_xres[oi]], own=r_xres[oi])
                    for half in range(2):
                        for e_ in range(16):
                            mm(Sb[:, half * 512:(half + 1) * 512], mixv(e_, t * 128, (t + 1) * 128),
                               woutS.ap[:, e_ * 1024 + half * 512:e_ * 1024 + (half + 1) * 512], e_ == 0, e_ == 15,
                               [r_mix[e_][g]] + woutS.res + [r_wout], [rS[half]])
                    si = cnt["st"] % 4
                    cnt["st"] += 1
                    st = stat[si]
                    ob = outb[oi]
                    act(ob.ap, Sb[:, :], AF.Square, rS, ob.res + [r_out[oi], r_stat[si]], accum_out=st[:, 0:1])
                    ts("dve", st[:, 1:2], st[:, 0:1], 1.0 / D, 1e-6, ALU.mult, ALU.add, [r_stat[si]], [r_stat[si]])
                    rsqrt_col(st, 128, [r_stat[si]])
                    stt("dve", ob.ap, Sb[:, :], st[:, 2:3], gpost_s[:], ALU.mult, ALU.mult,
                        rS + [r_stat[si], r_const], ob.res + [r_out[oi]])
                    tt("pool", ob.ap, ob.ap, xres[oi].ap, ALU.add, ob.res + [r_out[oi]] + xres[oi].res + [r_xres[oi]],
                       ob.res + [r_out[oi]])
                    dma("sp", ye[sgi, t * 128:(t + 1) * 128, :], ob.ap, ob.res + [r_out[oi]], [], own=r_out[oi])

        except StopBuild:
            pass

        def final_waits():
            allr = r_out + r_xt + r_w + r_mask + r_T + r_diag + [r_wout, r_init, r_tb, r_db] + r_xres
            return [(r.sem, r.cnt) for r in allr if r.cnt > 0] + [(g.sem, g.cnt) for g in (ginit, gconst) if g.cnt > 0]

        sch.check_deadlock(esem, final_waits)
        with nc.Block() as block:
            sch.emit(block, esem, final_waits)
    return nc


def _col_tables():
    key = np.arange(GW)
    q = np.arange(GW)
    start = np.clip(q - 8, 0, GW - 16)
    off = key[:, None] - start[None, :]
    valid = (off >= 0) & (off < 16)
    rel = np.clip(key[:, None] - q[None, :] + 15, 0, 30)
    return valid, rel


def host_weights(w_in, w_out, conv_w, conv_b, ln_g, ln_b, rel_bias, pre_g, post_g, meta_tokens):
    w_in = np.asarray(w_in, np.float32)[0]
    cols = np.arange(7168)
    vperm = np.zeros(1024, np.int64)
    for n in range(1024):
        if n < 512:
            head = 2 * (n // 64)
        else:
            head = 2 * ((n - 512) // 64) + 1
        vperm[n] = 5120 + head * 64 + n % 64
    cols[5120:6144] = vperm
    wp = w_in[:, cols]
    win = np.ascontiguousarray(wp.reshape(8, 128, 56, 128).transpose(2, 1, 0, 3)).reshape(56, 128, 1024)
    for g in range(2):
        blk = wp[:, 5120 + g * 512:5120 + (g + 1) * 512].reshape(8, 128, 512).transpose(1, 0, 2)
        win[40 + 4 * g:44 + 4 * g] = blk.reshape(128, 4, 1024).transpose(1, 0, 2)
    wo = np.asarray(w_out, np.float32)[0]
    wout = np.ascontiguousarray(wo.reshape(16, 128, 1024).transpose(1, 0, 2)).reshape(128, 16 * 1024)
    cw = np.asarray(conv_w, np.float32)[0]
    convwT = np.ascontiguousarray(cw.reshape(CONVW, 8, 128).transpose(2, 1, 0)).reshape(128, 8 * CONVW)
    chvec = np.concatenate([np.asarray(v, np.float32)[0].reshape(8, 128).T for v in (conv_b, ln_g, ln_b)], axis=1)
    gpre = np.ascontiguousarray(np.broadcast_to(np.asarray(pre_g, np.float32)[0][None, :], (128, D)))
    gpost = np.ascontiguousarray(np.broadcast_to(np.asarray(post_g, np.float32)[0][None, :], (128, D)))
    rb = np.asarray(rel_bias, np.float32)[0]
    valid, rel = _col_tables()
    Bt = np.zeros((128, 16, 16, 64), np.float32)
    Mt = np.zeros((128, 16, 64), np.float32)
    for half in range(2):
        for slot in range(16):
            d = 7 - slot
            dr = d + half
            if abs(dr) > 7:
                continue
            Bt[half * 64:(half + 1) * 64, :, slot, :] = rb[:, dr + 7, :][:, rel].transpose(1, 0, 2)
            Mt[half * 64:(half + 1) * 64, slot, :] = valid.astype(np.float32)
    conehot = np.zeros((16, NM), np.float32)
    for r in range(16):
        conehot[r, r * 64:(r + 1) * 64] = 1.0
    return {
        "win": win, "wout": wout, "convwT": convwT, "chvec": np.ascontiguousarray(chvec),
        "gpre": gpre, "gpost": gpost, "meta": np.ascontiguousarray(np.asarray(meta_tokens, np.float32)),
        "Bt": Bt.reshape(128, -1), "Mt": Mt.reshape(128, -1), "conehot": conehot,
        "ident": np.eye(128, dtype=np.float32),
    }


def host_segment(xseq, meta_tokens, R0):
    T = xseq.shape[0]
    rows = T // GW
    xe = np.zeros((NE, D), np.float32)
    t0 = R0 * GW - M0
    lo = max(0, t0)
    hi = min(T, t0 + NE)
    xe[lo - t0:hi - t0] = xseq[lo:hi]
    if t0 < 0:
        xe[-t0 - NMETA:-t0] = meta_tokens
    mask = np.full((16, NE), NEG, np.float32)
    wr = min(8, rows)
    for r in range(16):
        R = R0 + r
        sr = int(np.clip(R - wr // 2, 0, rows - wr))
        for kr in range(sr, sr + wr):
            er = kr - R0 + HALO
            if 0 <= er < EXT_ROWS:
                mask[r, er * GW:(er + 1) * GW] = 0.0
    return xe, mask


def run_segments(seg_lists, wts, n_cores):
    nseg = len(seg_lists[0])
    nc = build_program(nseg)
    in_maps = []
    for c in range(n_cores):
        xe = np.zeros((nseg, NE, D), np.float32)
        mk = np.zeros((nseg, 16, NE), np.float32)
        for s, (xseq, R0) in enumerate(seg_lists[c]):
            xe[s], mk[s] = host_segment(xseq, wts["meta"], R0)
        m = dict(wts)
        m["xe"] = xe
        m["maskA"] = mk
        in_maps.append(m)
    res = run_bass_kernel_spmd(nc, in_maps, core_ids=list(range(n_cores)))
    if int(os.environ.get('KDBG', '0')):
        return [r["ye"] for r in res.results], [(r["dbg_mix"], r["dbg_uT"], r["dbg_w"]) for r in res.results]
    return [r["ye"] for r in res.results]


def kernel(x_prompt, x_sample, meta_tokens, pre_norm_g, w_in, conv_w, conv_b, conv_ln_g, conv_ln_b,
           rel_bias, post_norm_g, w_out):
    x_prompt = np.asarray(x_prompt, np.float32)
    x_sample = np.asarray(x_sample, np.float32)
    wts = host_weights(w_in, w_out, conv_w, conv_b, conv_ln_g, conv_ln_b, rel_bias, pre_norm_g, post_norm_g,
                       meta_tokens)
    seg_lists = []
    where = []
    for c in range(NCORES):
        segs = []
        wh = []
        for i in range(4):
            b = 4 * c + i
            for R0 in (0, 16):
                segs.append((x_prompt[b], R0))
                wh.append((0, b, R0))
        sb_ = c // 2
        for R0 in (32 * (c % 2), 32 * (c % 2) + 16):
            segs.append((x_sample[sb_], R0))
            wh.append((1, sb_, R0))
        seg_lists.append(segs)
        where.append(wh)
    outs = run_segments(seg_lists, wts, NCORES)
    y_prompt = np.empty_like(x_prompt)
    y_sample = np.empty_like(x_sample)
    for c in range(NCORES):
        for s, (which, b, R0) in enumerate(where[c]):
            dst = y_prompt if which == 0 else y_sample
            dst[b, R0 * GW:(R0 + SEG_ROWS) * GW] = outs[c][s]
    return (y_prompt, y_sample)
```

```python
import os
import numpy as np
import concourse.bass as bass
import concourse.mybir as mybir
from concourse.bass_utils import run_bass_kernel_spmd

F32 = mybir.dt.float32
BF16 = mybir.dt.bfloat16
I32 = mybir.dt.int32
MAGIC = 0x5f3759df
ALU = mybir.AluOpType
AF = mybir.ActivationFunctionType

D = 1024
NMETA = 16
GW = 64
SEG_ROWS = 16
HALO = 4
EXT_ROWS = SEG_ROWS + 2 * HALO
NE = EXT_ROWS * GW
NM = SEG_ROWS * GW
M0 = HALO * GW
NTILE = NE // 128
CONVW = 31
PAD = 15
NUG = NM + 2 * PAD
NEG = -30000.0
NCORES = 8
WRING = 8


class Res:
    __slots__ = ("name", "lw", "rd", "sem", "cnt")

    def __init__(self, name):
        self.name = name
        self.lw = None
        self.rd = []
        self.sem = None
        self.cnt = 0


class Group:
    def __init__(self, sem):
        self.sem = sem
        self.cnt = 0


class Op:
    __slots__ = ("eng", "fn", "deps", "signal", "val", "dsem", "dval", "grp", "xw")

    def __init__(self, eng, fn):
        self.eng = eng
        self.fn = fn
        self.deps = []
        self.signal = False
        self.val = 0
        self.dsem = None
        self.dval = 0
        self.grp = None
        self.xw = None


class Sched:
    ENGS = ("pe", "act", "dve", "pool", "sp")

    def __init__(self, nc):
        self.nc = nc
        self.ops = {e: [] for e in self.ENGS}
        self.esem = {}
        self.out_ops = []

    def _deps(self, op, reads, writes):
        deps = []
        for r in reads:
            if r.lw is not None:
                deps.append(r.lw)
        for w in writes:
            if w.lw is not None:
                deps.append(w.lw)
            deps.extend(w.rd)
        seen = set()
        for d in deps:
            if id(d) in seen or d is op:
                continue
            seen.add(id(d))
            if d.dsem is None and d.grp is None:
                if d.eng == "pe" and op.eng == "pe":
                    continue
                d.signal = True
            op.deps.append(d)
        for r in reads:
            r.rd.append(op)
        for w in writes:
            w.lw = op
            w.rd = []

    def add(self, eng, fn, reads=(), writes=()):
        op = Op(eng, fn)
        self._deps(op, reads, writes)
        self.ops[eng].append(op)
        return op

    def dma(self, q, fn, reads=(), writes=(), own=None, grp=None):
        op = Op(q, fn)
        self._deps(op, reads, writes)
        if own is not None:
            own.cnt += 16
            op.dsem = own.sem
            op.dval = own.cnt
        else:
            op.deps = [d for d in op.deps if d.grp is not grp]
            grp.cnt += 16
            op.grp = grp
            if q == "pool" and grp.cnt > 16 * 24:
                op.xw = (grp.sem, grp.cnt - 16 * 24)
        self.ops[q].append(op)
        return op

    def check_deadlock(self, sems, final_waits):
        for e in self.ENGS:
            c = 0
            for op in self.ops[e]:
                if op.dsem is None and op.grp is None and op.signal:
                    c += 1
                    op.val = c

        def ev(d):
            if d.dsem is not None:
                return id(d.dsem), d.dval
            if d.grp is not None:
                return id(d.grp.sem), d.grp.cnt
            return id(sems[d.eng]), d.val
        prog = {}
        for e in self.ENGS:
            lst = []
            for op in self.ops[e]:
                waits = [ev(d) for d in op.deps]
                if op.xw is not None:
                    waits.append((id(op.xw[0]), op.xw[1]))
                if op.dsem is not None:
                    inc = (id(op.dsem), 16)
                elif op.grp is not None:
                    inc = (id(op.grp.sem), 16)
                elif op.signal:
                    inc = (id(sems[e]), 1)
                else:
                    inc = None
                lst.append((waits, inc))
            if e == "sp":
                lst.append(([(id(s_), v) for s_, v in final_waits()], None))
            prog[e] = lst
        val = {}
        pc = {e: 0 for e in self.ENGS}
        progress = True
        while progress:
            progress = False
            for e in self.ENGS:
                while pc[e] < len(prog[e]):
                    waits, inc = prog[e][pc[e]]
                    if all(val.get(s_, 0) >= v for s_, v in waits):
                        if inc is not None:
                            val[inc[0]] = val.get(inc[0], 0) + inc[1]
                        pc[e] += 1
                        progress = True
                    else:
                        break
        stuck = {e: (pc[e], len(prog[e])) for e in self.ENGS if pc[e] < len(prog[e])}
        if stuck:
            names = {id(v): k for k, v in sems.items()}
            msg = []
            for e, (p, n) in stuck.items():
                waits, _ = prog[e][p]
                msg.append(f"{e} stuck at {p}/{n} waiting " + str([(names.get(s_, s_), v, val.get(s_, 0)) for s_, v in waits if val.get(s_, 0) < v]))
            raise RuntimeError("DEADLOCK in schedule: " + "; ".join(msg))

    def emit(self, block, sems, final_waits):
        nc = self.nc
        for e in self.ENGS:
            c = 0
            for op in self.ops[e]:
                if op.dsem is None and op.grp is None and op.signal:
                    c += 1
                    op.val = c
        esem = sems
        sched = self

        def ev(d):
            if d.dsem is not None:
                return d.dsem, d.dval
            if d.grp is not None:
                return d.grp.sem, d.grp.cnt
            return esem[d.eng], d.val

        def run(e, eng):
            known = {}
            for op in sched.ops[e]:
                for d in op.deps:
                    s, v = ev(d)
                    key = id(s)
                    if known.get(key, 0) < v:
                        eng.wait_ge(s, v)
                        known[key] = v
                if op.xw is not None:
                    eng.wait_ge(op.xw[0], op.xw[1])
                ins = op.fn(eng)
                if op.dsem is not None:
                    ins.then_inc(op.dsem, 16)
                elif op.grp is not None:
                    ins.then_inc(op.grp.sem, 16)
                elif op.signal:
                    ins.then_inc(esem[e], 1)
            if e == "sp":
                for s, v in final_waits():
                    eng.wait_ge(s, v)

        @block.tensor
        def _(eng):
            run("pe", eng)

        @block.scalar
        def _(eng):
            run("act", eng)

        @block.vector
        def _(eng):
            run("dve", eng)

        @block.gpsimd
        def _(eng):
            run("pool", eng)

        @block.sync
        def _(eng):
            run("sp", eng)


def tile_rows(p):
    lo = max(0, 2 * p - 7)
    hi = min(SEG_ROWS - 1, 2 * p + 1)
    rows = set(range(lo, hi + 1)) if lo <= hi else set()
    if p <= 5:
        rows |= set(range(0, 4))
    if p >= 6:
        rows |= set(range(12, 16))
    rows = sorted(rows)
    assert rows == list(range(rows[0], rows[-1] + 1))
    return rows[0], rows[-1]


def build_program(nseg):
    NJUNK = int(os.environ.get('KJUNK', '0'))
    STAGE = int(os.environ.get('KSTAGE', '9'))
    nc = bass.Bass("TRN2", target_bir_lowering=False)

    def din(name, shape, dt=F32):
        return nc.dram_tensor(name, list(shape), dt, kind="ExternalInput").ap()

    xe = din("xe", [nseg, NE, D])
    maskA = din("maskA", [nseg, 16, NE])
    win = din("win", [56, 128, 1024])
    wout = din("wout", [128, 16 * 1024])
    convwT = din("convwT", [128, 8 * CONVW])
    chvec = din("chvec", [128, 24])
    gpre = din("gpre", [128, D])
    gpost = din("gpost", [128, D])
    meta = din("meta", [NMETA, D])
    Bt = din("Bt", [128, 16 * 16 * 64])
    Mt = din("Mt", [128, 16 * 64])
    conehot = din("conehot", [16, NM])
    identf = din("ident", [128, 128])
    ye = nc.dram_tensor("ye", [nseg, NM, D], F32, kind="ExternalOutput").ap()
    DBG = int(os.environ.get('KDBG', '0'))
    if DBG:
        dbg_mix = nc.dram_tensor("dbg_mix", [128, 16 * NM], BF16, kind="ExternalOutput").ap()
        dbg_uT = nc.dram_tensor("dbg_uT", [128, 8 * NE], BF16, kind="ExternalOutput").ap()
        dbg_w = nc.dram_tensor("dbg_w", [128, 1024], BF16, kind="ExternalOutput").ap()
    win_bf = nc.dram_tensor("win_bf", [56, 128, 1024], BF16, kind="Internal").ap()
    wout_bf = nc.dram_tensor("wout_bf", [128, 16 * 1024], BF16, kind="Internal").ap()
    T_bf = nc.dram_tensor("T_bf", [8, 128, 2048], BF16, kind="Internal").ap()
    diag_bf = nc.dram_tensor("diag_bf", [8, 128, CONVW * 128], BF16, kind="Internal").ap()

    import contextlib
    es = contextlib.ExitStack()
    with es:
        def sb(name, shape, dt):
            return es.enter_context(nc.sbuf_tensor(name, list(shape), dt))

        def ps(name, shape, dt):
            return es.enter_context(nc.psum_tensor(name, list(shape), dt))

        def sem(name):
            return es.enter_context(nc.semaphore(name))

        uT = sb("uT", [128, 8 * NE], BF16)
        mixT = sb("mixT", [128, 16 * NM], BF16)
        wring = sb("wring", [128, WRING * 1024], BF16)
        xt = [sb(f"xt{i}", [128, D], F32) for i in range(2)]
        ubf = [sb(f"ubf{i}", [128, D], BF16) for i in range(2)]
        gpre_s = sb("gpre_s", [128, D], F32)
        gpost_s = sb("gpost_s", [128, D], F32)
        ident_s = sb("ident_s", [128, 128], BF16)
        ones_s = sb("ones_s", [128, 128], BF16)
        kmeta = sb("kmeta", [64, 8 * 2 * 16], BF16)
        vmA = sb("vmA", [16, 8 * 128], BF16)
        vmB = sb("vmB", [16, 8 * 128], BF16)
        uTm = sb("uTm", [128, 8 * 16], BF16)
        chv = sb("chv", [128, 24], F32)
        cwT = sb("cwT", [128, 8 * CONVW], F32)
        halfg = sb("halfg", [128, 8], F32)
        halfb = sb("halfb", [128, 8], F32)
        stat = [sb(f"stat{i}", [128, 8], F32) for i in range(4)]
        A16N = 41984
        A32N = 4096
        ar16 = sb("ar16", [128, A16N], BF16)
        ar32 = sb("ar32", [128, A32N], F32)
        G16 = 512
        G32 = 256
        r16 = [Res(f"a16_{i}") for i in range((A16N + G16 - 1) // G16)]
        r32 = [Res(f"a32_{i}") for i in range((A32N + G32 - 1) // G32)]

        class Buf:
            def __init__(self, ap, res):
                self.ap = ap
                self.res = res

        def a16(off, n, parts=128):
            return Buf(ar16[0:parts, off:off + n], r16[off // G16:(off + n - 1) // G16 + 1])

        def a32(off, n, parts=128):
            return Buf(ar32[0:parts, off:off + n], r32[off // G32:(off + n - 1) // G32 + 1])

        UGP = 1056
        tmpb = [a16(i * 512, 512) for i in range(8)]
        UG0 = 4096 + NTILE * 1536
        gc2 = [a16(UG0 + j * 1024, 1024) for j in range(8)]
        ugT = [a16(UG0 + 8192 + i * UGP, UGP) for i in range(2)]
        DG0 = UG0 + 8192 + 2 * UGP
        diag = [a16(DG0 + i * 3968, 3968) for i in range(2)]
        assert DG0 + 2 * 3968 <= A16N
        stA = a32(0, 1024)
        stB = a32(1024, 1024)
        tf = [a32(2048 + i * 512, 512) for i in range(3)]
        Ttab = [a16(i * 2048, 2048) for i in range(2)]
        VW = 1536
        V0 = 4096
        Vt = [a16(V0 + t * VW, VW) for t in range(NTILE)]
        QK0 = V0 + NTILE * VW
        QKS = 2 * 1024 + 2 * NE + 1024
        QT = [[a16(QK0 + i * QKS + hh * 1024, 1024) for hh in range(2)] for i in range(2)]
        KT = [[a16(QK0 + i * QKS + 2048 + hh * NE, NE) for hh in range(2)] for i in range(2)]
        AG = [a16(QK0 + i * QKS + 2048 + 2 * NE, 1024) for i in range(2)]
        EX0 = QK0 + 2 * QKS
        expS = [a16(EX0 + i * 768, 768) for i in range(2)]
        PT = [a16(EX0 + 1536 + i * 768, 768) for i in range(2)]
        PM0 = EX0 + 3072
        PmT = [[a16(PM0 + i * 2048 + hh * 1024, 1024, parts=16) for hh in range(2)] for i in range(2)]
        assert PM0 + 4096 <= A16N, (PM0, A16N)
        Rb = [a32(i * 512, 512) for i in range(2)]
        attb = [a32(1024 + i * 512, 512) for i in range(2)]
        woutS = a16(0, 16384)
        outb = [a32(i * 1024, 1024) for i in range(2)]
        xres = [a32(2048 + i * 1024, 1024) for i in range(2)]
        btmp = a32(0, 2048)
        btmp2 = a32(2048, 2048)
        tbuild = a16(0, 2048)
        dbuild = a16(4096, 3968)
        xmeta = a32(0, 1024, parts=16)

        S0 = ps("S0", [128, 1024], F32)
        S1 = ps("S1", [128, 1024], F32)
        O0 = ps("O0", [128, 512], F32)
        O1 = ps("O1", [128, 512], F32)
        PJ = ps("PJ", [128, 512], F32)
        TB = ps("TB", [128, 1024], BF16)
        rS0a, rS0b, rS1a, rS1b, rO0, rO1, rPJ, rTB = [Res(n) for n in
                                                    ("S0a", "S0b", "S1a", "S1b", "O0", "O1", "PJ", "TB")]
        rPJa, rPJb = rPJ, Res("PJb")
        banks = [(S0[:, 0:512], rS0a), (S0[:, 512:1024], rS0b), (S1[:, 0:512], rS1a),
                 (S1[:, 512:1024], rS1b), (O0[:, :], rO0), (O1[:, :], rO1), (PJ[:, :], rPJ)]
        Sbuf = [(S0, [rS0a, rS0b]), (S1, [rS1a, rS1b])]
        Obuf = [(O0, rO0), (O1, rO1)]

        esem = {e: sem("s_" + e) for e in Sched.ENGS}
        sch = Sched(nc)
        ginit = Group(sem("g_init"))
        gconst = Group(sem("g_const"))

        def own(r, name):
            r.sem = sem(name)
            return r

        r_xt = [own(Res(f"xt{i}"), f"d_xt{i}") for i in range(2)]
        r_ubf = [Res(f"ubf{i}") for i in range(2)]
        r_w = [own(Res(f"w{i}"), f"d_w{i}") for i in range(WRING)]
        r_mask = [own(Res(f"mask{i}"), f"d_mask{i}") for i in range(8)]
        r_uT = [Res(f"uT{t}") for t in range(NTILE)]
        r_mix = [[Res(f"mix{e}_{g}") for g in range(2)] for e in range(16)]
        r_stat = [Res(f"stat{i}") for i in range(4)]
        r_const = Res("const")
        r_const2 = Res("const2")
        r_scr = Res("scratch")
        r_meta = Res("metabufs")
        r_T = [own(Res(f"T{i}"), f"d_T{i}") for i in range(2)]
        r_diag = [own(Res(f"dg{i}"), f"d_dg{i}") for i in range(2)]
        r_wout = own(Res("woutS"), "d_wout")
        r_xres = [own(Res(f"xres{i}"), f"d_xres{i}") for i in range(2)]
        r_out = [own(Res(f"out{i}"), f"d_out{i}") for i in range(2)]
        r_init = own(Res("initbuf"), "d_initbuf")
        r_tb = own(Res("tb"), "d_tb")
        r_db = own(Res("db"), "d_db")
        r_Tscr = [Res(f"Tscr{i}") for i in range(8)]
        r_dscr = [Res(f"dscr{i}") for i in range(8)]

        def mm(out, lhsT, rhs, start, stop, reads, writes, **kw):
            return sch.add("pe", lambda e: e.matmul(out, lhsT=lhsT, rhs=rhs, start=start, stop=stop, **kw),
                           reads, writes)

        def act(out, in_, func, reads, writes, **kw):
            return sch.add("act", lambda e: e.activation(out=out, in_=in_, func=func, **kw), reads, writes)

        def tt(eng, out, in0, in1, op, reads, writes):
            return sch.add(eng, lambda e: e.tensor_tensor(out=out, in0=in0, in1=in1, op=op), reads, writes)

        def ts(eng, out, in0, s1, s2, op0, op1, reads, writes):
            if s2 is None:
                return sch.add(eng, lambda e: e.tensor_scalar(out=out, in0=in0, scalar1=s1, scalar2=None, op0=op0),
                               reads, writes)
            return sch.add(eng, lambda e: e.tensor_scalar(out=out, in0=in0, scalar1=s1, scalar2=s2, op0=op0, op1=op1),
                           reads, writes)

        def stt(eng, out, in0, scalar, in1, op0, op1, reads, writes):
            return sch.add(eng, lambda e: e.scalar_tensor_tensor(out=out, in0=in0, scalar=scalar, in1=in1,
                                                                 op0=op0, op1=op1), reads, writes)

        def rsqrt(a, r, w, ra, rr, rw):
            ri = r.bitcast(I32)
            ts("dve", ri, a.bitcast(I32), 1, None, ALU.arith_shift_right, None, ra, rr)
            ts("dve", ri, ri, -1, MAGIC, ALU.mult, ALU.add, rr, rr)
            for _ in range(3):
                tt("dve", w, a, r, ALU.mult, ra + rr, rw)
                tt("dve", w, w, r, ALU.mult, rw + rr, rw)
                ts("dve", w, w, -0.5, 1.5, ALU.mult, ALU.add, rw, rw)
                tt("dve", r, r, w, ALU.mult, rr + rw, rr)

        def rsqrt_col(st, n, rr):
            a, r, h, t = st[0:n, 1:2], st[0:n, 2:3], st[0:n, 3:4], st[0:n, 4:5]
            ri = r.bitcast(I32)
            ts("dve", ri, a.bitcast(I32), 1, None, ALU.arith_shift_right, None, rr, rr)
            ts("dve", ri, ri, -1, MAGIC, ALU.mult, ALU.add, rr, rr)
            ts("dve", h, a, -0.5, None, ALU.mult, None, rr, rr)
            for _ in range(3):
                ts("dve", t, h, r, r, ALU.mult, ALU.mult, rr, rr)
                stt("dve", r, t, 1.5, r, ALU.add, ALU.mult, rr, rr)

        def cp(eng, out, in_, reads, writes):
            if eng == "act":
                return sch.add("act", lambda e: e.copy(out=out, in_=in_), reads, writes)
            return sch.add(eng, lambda e: e.tensor_copy(out=out, in_=in_), reads, writes)

        def dma(q, out, in_, reads, writes, own=None, grp=None):
            return sch.dma(q, lambda e: e.dma_start(out=out, in_=in_), reads, writes, own=own, grp=grp)

        def uTv(k, a, b):
            return uT[:, k * NE + a:k * NE + b]

        def mixv(e, a, b):
            return mixT[:, e * NM + a:e * NM + b]

        def wv(slot, k):
            return wring[:, slot * 1024 + k * 128:slot * 1024 + (k + 1) * 128]

        wseq = []
        for s in range(nseg):
            for j in range(8):
                wseq += [8 + j, j]
            for j in range(8):
                wseq += [16 + j]
            while len(wseq) % 4:
                wseq.append(None)
            wseq += [40, 41, 42, 43, 44, 45, 46, 47]
            for hp in range(8):
                wseq += [24 + hp, 32 + hp, 48 + hp]
        wpre = [32 + hp for hp in range(8)] + [40, 41, 42, 43, 44, 45, 46, 47]
        wseq = wpre + wseq
        wstate = {"loaded": 0, "used": 0}

        def w_load_upto(n):
            while wstate["loaded"] < min(n, len(wseq)):
                i = wstate["loaded"]
                e = wseq[i]
                if e is not None:
                    slot = i % WRING
                    dma("sp", wring[:, slot * 1024:(slot + 1) * 1024], win_bf[e],
                        [r_scr], [r_w[slot]], own=r_w[slot])
                wstate["loaded"] += 1

        def w_next(expect):
            i = wstate["used"]
            while wseq[i] is None:
                i += 1
            assert wseq[i] == expect, (i, wseq[i], expect)
            w_load_upto(i + 1)
            wstate["used"] = i + 1
            return i % WRING

        def w_prefetch(k=4):
            assert k <= WRING
            w_load_upto(wstate["used"] + k)

        class StopBuild(Exception):
            pass

        def stage(k):
            if STAGE < k:
                raise StopBuild()

        try:
            for e in range(int(os.environ.get('KNW', '56'))):
                dma("pool", win_bf[e], win[e], [], [r_scr], grp=ginit)
            for q in range(int(os.environ.get('KNO', '16'))):
                dma("pool", wout_bf[:, q * 1024:(q + 1) * 1024], wout[:, q * 1024:(q + 1) * 1024], [], [r_scr], grp=ginit)
            stage(-3)
            dma("sp", gpre_s[:], gpre[:], [], [r_const], grp=gconst)
            dma("sp", gpost_s[:], gpost[:], [], [r_const], grp=gconst)
            dma("sp", chv[:], chvec[:], [], [r_const], grp=gconst)
            dma("sp", cwT[:], convwT[:], [], [r_const], grp=gconst)
            dma("pool", ident_s[:], identf[:], [], [r_const], grp=gconst)
            Mt_s = sb("Mt_s", [128, 1024], F32)
            dma("sp", Mt_s[:], Mt[:], [], [r_const], grp=gconst)
            sch.add("dve", lambda e: e.memset(ones_s[:], 1.0 / 1024.0), [], [r_const2])
            sch.add("dve", lambda e: e.memset(vmA[:], 1.0), [], [r_meta])
            sch.add("dve", lambda e: e.memset(vmB[:], 1.0), [], [r_meta])
            ts("dve", halfg[:], chv[:, 8:16], 0.5, None, ALU.mult, None, [r_const], [r_const2])
            ts("dve", halfb[:], chv[:, 16:24], 0.5, None, ALU.mult, None, [r_const], [r_const2])

            stage(-2)
            for hp in range(8):
                dma("sp", btmp.ap, Bt[:, hp * 2048:(hp + 1) * 2048], [], btmp.res, own=r_init)
                act(btmp2.ap, btmp.ap, AF.Exp, btmp.res, btmp2.res)
                for hh in range(2):
                    tt("dve", tbuild.ap[:, hh * 1024:(hh + 1) * 1024], btmp2.ap[:, hh * 1024:(hh + 1) * 1024],
                       Mt_s[:], ALU.mult, btmp2.res + [r_const], tbuild.res)
                dma("sp", T_bf[hp], tbuild.ap, tbuild.res, [r_Tscr[hp]], own=r_tb)
            stage(-1)
            for j in range(8):
                for s in range(CONVW):
                    eng = "dve" if (s % 2 == 0) else "pool"
                    ts(eng, dbuild.ap[:, s * 128:(s + 1) * 128], ident_s[:],
                       cwT[:, j * CONVW + s:j * CONVW + s + 1], 0.5, ALU.mult, ALU.mult,
                       [r_const], dbuild.res)
                dma("sp", diag_bf[j], dbuild.ap, dbuild.res, [r_dscr[j]], own=r_db)

            cnt = {"x": 0, "st": 0, "cpy": 0, "pj": 0}

            def p0_tile(src_ap, nrows, dst_fn, dst_res, xbuf=None, xres_=None):
                i = cnt["x"] % 2
                cnt["x"] += 1
                si = cnt["st"] % 4
                cnt["st"] += 1
                if xbuf is None:
                    xb, xr = xt[i][0:nrows, :], [r_xt[i]]
                    dma("sp", xb, src_ap, [], xr, own=r_xt[i])
                else:
                    xb, xr = xbuf, xres_
                ub = ubf[i][0:nrows, :]
                st = stat[si]
                act(ub, xb, AF.Square, xr, [r_ubf[i], r_stat[si]], accum_out=st[0:nrows, 0:1])
                ts("dve", st[0:nrows, 1:2], st[0:nrows, 0:1], 1.0 / D, 1e-6, ALU.mult, ALU.add, [r_stat[si]], [r_stat[si]])
                rsqrt_col(st, nrows, [r_stat[si]])
                stt("dve", ub, xb, st[0:nrows, 2:3], gpre_s[0:nrows, :], ALU.mult, ALU.mult,
                    xr + [r_stat[si], r_const], [r_ubf[i]])
                for k in range(8):
                    sch.add("pe", lambda e, k=k: e.transpose(out=TB[:, k * 128:k * 128 + nrows],
                                                               in_=ubf[i][0:nrows, k * 128:(k + 1) * 128],
                                                               identity=ident_s[0:nrows, 0:nrows]),
                            [r_ubf[i], r_const], [rTB])
                ceng = "act" if cnt["cpy"] % 2 == 0 else "dve"
                cnt["cpy"] += 1
                src = TB[:].rearrange("p (k t) -> p k t", t=128)[:, :, 0:nrows]
                cp(ceng, dst_fn(), src, [rTB], dst_res)

            stage(1)
            ssb = sb("ssb", [128, 64], F32)
            r_ssb = Res("ssb")

            def p0_a_tiles(sgi, tiles):
                for t in tiles:
                    i = cnt["x"] % 2
                    cnt["x"] += 1
                    dma("sp", xt[i][:], xe[sgi, t * 128:(t + 1) * 128, :], [], [r_xt[i]], own=r_xt[i])
                    act(ubf[i][:], xt[i][:], AF.Square, [r_xt[i]], [r_ubf[i], r_ssb], accum_out=ssb[:, t:t + 1])

            def p0_chain():
                a_, r_, h_, t_ = ssb[:, 16:28], ssb[:, 32:44], ssb[:, 48:60], ssb[:, 0:12]
                rr = [r_ssb]
                ts("dve", a_, ssb[:, 0:12], 1.0 / D, 1e-6, ALU.mult, ALU.add, rr, rr)
                ri = r_.bitcast(I32)
                ts("dve", ri, a_.bitcast(I32), 1, None, ALU.arith_shift_right, None, rr, rr)
                ts("dve", ri, ri, -1, MAGIC, ALU.mult, ALU.add, rr, rr)
                ts("dve", h_, a_, -0.5, None, ALU.mult, None, rr, rr)
                for _ in range(3):
                    tt("dve", t_, h_, r_, ALU.mult, rr, rr)
                    tt("dve", t_, t_, r_, ALU.mult, rr, rr)
                    stt("dve", r_, t_, 1.5, r_, ALU.add, ALU.mult, rr, rr)

            def p0_c(sgi, tiles, only_pj=False):
                for t in tiles:
                    i = cnt["x"] % 2
                    cnt["x"] += 1
                    dma("sp", xt[i][:], xe[sgi, t * 128:(t + 1) * 128, :], [], [r_xt[i]], own=r_xt[i])
                    stt("dve", ubf[i][:], xt[i][:], ssb[:, 32 + t:33 + t], gpre_s[:], ALU.mult, ALU.mult,
                        [r_xt[i], r_ssb, r_const], [r_ubf[i]])
                    if t % 2 == 0 and not only_pj:
                        tbank, tres = TB[:], [rTB]
                    else:
                        tbank, tres = PJ[:].bitcast(BF16), [rPJa, rPJb]
                    for k in range(8):
                        sch.add("pe", lambda e, k=k, i=i, tbank=tbank: e.transpose(out=tbank[:, k * 128:(k + 1) * 128],
                                                                                    in_=ubf[i][:, k * 128:(k + 1) * 128],
                                                                                    identity=ident_s[:, :]),
                                [r_ubf[i], r_const], tres)
                    ceng = "act" if t % 2 == 0 else "dve"
                    src = tbank.rearrange("p (k t) -> p k t", t=128)
                    dst = uT[:].rearrange("p (k n) -> p k n", n=NE)[:, :, t * 128:(t + 1) * 128]
                    cp(ceng, dst, src, tres, [r_uT[t]])

            def p0_batched(sgi):
                p0_a_tiles(sgi, range(NTILE))
                p0_chain()
                p0_c(sgi, range(NTILE))

            dma("sp", xmeta.ap, meta[:], [], xmeta.res, own=r_init)
            p0_tile(None, NMETA, lambda: uTm[:].rearrange("p (k t) -> p k t", t=16), [r_meta],
                    xbuf=xmeta.ap, xres_=xmeta.res)
            for hp in range(8):
                slot = w_next(32 + hp)
                for k in range(8):
                    mm(PJ[:, 0:16], wv(slot, k), uTm[:, k * 16:(k + 1) * 16], k == 0, k == 7,
                       [r_w[slot], r_meta], [rPJ])
                cp("dve", kmeta[0:64, hp * 32:hp * 32 + 16], PJ[0:64, 0:16], [rPJ], [r_meta])
                cp("dve", kmeta[0:64, hp * 32 + 16:hp * 32 + 32], PJ[64:128, 0:16], [rPJ], [r_meta])
                w_prefetch(4)
            for g in range(2):
                slots = [w_next(40 + 4 * g + q) for q in range(4)]
                assert slots[0] % 4 == 0
                for k in range(8):
                    rhs = wring[:, slots[0] * 1024 + k * 512:slots[0] * 1024 + (k + 1) * 512]
                    mm(PJ[0:16, 0:512], uTm[:, k * 16:(k + 1) * 16], rhs, k == 0, k == 7,
                       [r_w[s_] for s_ in slots] + [r_meta], [rPJ])
                if g == 0:
                    cp("dve", vmA[0:16, :].rearrange("p (i c) -> p i c", c=128)[:, :, 0:64],
                       PJ[0:16, 0:512].rearrange("p (i c) -> p i c", c=64), [rPJ], [r_meta])
                else:
                    cp("dve", vmB[0:16, :].rearrange("p (i c) -> p i c", c=128)[:, :, 64:128],
                       PJ[0:16, 0:512].rearrange("p (i c) -> p i c", c=64), [rPJ], [r_meta])

            rot = {"pair": 0, "S": 0, "ex": 0, "out": 0, "tmp": 0, "tf": 0, "dg": 0, "ug": 0, "T": 0, "mask": 0}

            p0c_done = {}
            for sgi in range(nseg):
                mi = rot["mask"] % 2
                rot["mask"] += 1

                stage(2)
                if sgi == 0:
                    p0_batched(sgi)
                elif not p0c_done.get(sgi, False):
                    p0_c(sgi, range(NTILE))
                nxt = sgi + 1 if sgi + 1 < nseg else None
                all_uT = list(r_uT)
                if DBG and sgi == nseg - 1:
                    dma("sp", dbg_uT[:, :], uT[:, :], all_uT, [], own=r_init)
                    dma("sp", dbg_w[:, :], win_bf[8], [r_scr], [], own=r_init)

                stage(3)
                ug_groups = [(M0 - PAD, 512), (M0 - PAD + 512, 512), (M0 - PAD + 1024, 2 * PAD)]
                w_prefetch(4)
                for j in range(8):
                    ui = rot["ug"] % 2
                    rot["ug"] += 1
                    sg_ = w_next(8 + j)
                    sv_ = w_next(j)
                    di = rot["dg"] % 2
                    rot["dg"] += 1
                    dma("sp", diag[di].ap, diag_bf[j], [r_dscr[j]], diag[di].res + [r_diag[di]], own=r_diag[di])
                    for gi, (e0, n) in enumerate(ug_groups):
                        bg, rg = banks[(2 * gi) % 4]
                        bv, rv = banks[(2 * gi + 1) % 4]
                        for k in range(8):
                            mm(bg[:, 0:n], wv(sg_, k), uTv(k, e0, e0 + n), k == 0, k == 7, [r_w[sg_]] + all_uT, [rg])
                        for k in range(8):
                            mm(bv[:, 0:n], wv(sv_, k), uTv(k, e0, e0 + n), k == 0, k == 7, [r_w[sv_]] + all_uT, [rv])
                        tb = tmpb[rot["tmp"] % 8]
                        rot["tmp"] += 1
                        act(tb.ap[:, 0:n], bg[:, 0:n], AF.Tanh, [rg], tb.res, scale=0.5)
                        o0 = e0 - (M0 - PAD)
                        stt("dve", ugT[ui].ap[:, o0:o0 + n], tb.ap[:, 0:n], 1.0, bv[:, 0:n], ALU.add, ALU.mult,
                            tb.res + [rv], ugT[ui].res)
                    for g in range(2):
                        by, ry = Obuf[g]
                        for s in range(CONVW):
                            mm(by[:, :], diag[di].ap[:, s * 128:(s + 1) * 128],
                               ugT[ui].ap[:, g * 512 + s:g * 512 + s + 512], s == 0, s == CONVW - 1,
                               diag[di].res + [r_diag[di]] + ugT[ui].res, [ry])
                        act(mixv(j, g * 512, (g + 1) * 512), by[:, :], AF.Identity, [ry, r_const], [r_mix[j][g]],
                            bias=chv[:, j:j + 1])
                    if g == 1:
                        w_prefetch(4)
                        if nxt is not None and j < 6:
                            p0_a_tiles(nxt, [2 * j, 2 * j + 1])
                stage(4)
                for g in range(2):
                    bm, rm = banks[0]
                    bq, rq = banks[1]
                    for j in range(8):
                        tb = tmpb[rot["tmp"] % 8]
                        rot["tmp"] += 1
                        act(tb.ap, mixv(j, g * 512, (g + 1) * 512), AF.Square, [r_mix[j][g]], tb.res)
                        mm(bm, ones_s[:], mixv(j, g * 512, (g + 1) * 512), j == 0, j == 7, [r_const2, r_mix[j][g]], [rm])
                        mm(bq, ones_s[:], tb.ap, j == 0, j == 7, [r_const2] + tb.res, [rq])
                    A_ = stA.ap[:, g * 512:(g + 1) * 512]
                    B_ = stB.ap[:, g * 512:(g + 1) * 512]
                    t0 = tf[0]
                    act(t0.ap, bm, AF.Square, [rm], t0.res)
                    tt("dve", t0.ap, bq, t0.ap, ALU.subtract, [rq] + t0.res, t0.res)
                    ts("dve", t0.ap, t0.ap, 1e-5, None, ALU.add, None, t0.res, t0.res)
                    rsqrt(t0.ap, A_, tf[1].ap, t0.res, stA.res, tf[1].res)
                    stt("dve", B_, bm, -1.0, A_, ALU.mult, ALU.mult, [rm] + stA.res, stB.res)
                stage(5)
                if nxt is not None:
                    p0_chain()
                w_prefetch(4)
                for j in range(8):
                    sc_ = w_next(16 + j)
                    for g in range(2):
                        bc, rc = banks[2 + g]
                        for k in range(8):
                            mm(bc, wv(sc_, k), uTv(k, M0 + g * 512, M0 + (g + 1) * 512), k == 0, k == 7,
                               [r_w[sc_]] + all_uT, [rc])
                        tb = tmpb[rot["tmp"] % 8]
                        rot["tmp"] += 1
                        act(tb.ap, bc, AF.Tanh, [rc], tb.res, scale=0.5)
                        stt("dve", gc2[j].ap[:, g * 512:(g + 1) * 512], tb.ap, 1.0, bc, ALU.add, ALU.mult,
                            tb.res + [rc], gc2[j].res)
                    w_prefetch(4)

                def ln_apply(j, g):
                    A_ = stA.ap[:, g * 512:(g + 1) * 512]
                    B_ = stB.ap[:, g * 512:(g + 1) * 512]
                    t1 = tf[1 + (rot["tf"] % 2)]
                    rot["tf"] += 1
                    mv = mixv(j, g * 512, (g + 1) * 512)
                    tt("pool", t1.ap, mv, A_, ALU.mult, [r_mix[j][g]] + stA.res, t1.res)
                    tt("pool", t1.ap, t1.ap, B_, ALU.add, t1.res + stB.res, t1.res)
                    tb2 = tmpb[rot["tmp"] % 8]
                    rot["tmp"] += 1
                    tb3 = tmpb[rot["tmp"] % 8]
                    rot["tmp"] += 1
                    act(tb2.ap, t1.ap, AF.Tanh, t1.res + [r_const2], tb2.res,
                        scale=halfg[:, j:j + 1], bias=halfb[:, j:j + 1])
                    act(tb3.ap, t1.ap, AF.Identity, t1.res + [r_const], tb3.res,
                        scale=chv[:, 8 + j:9 + j], bias=chv[:, 16 + j:17 + j])
                    stt("dve", t1.ap, tb2.ap, 1.0, tb3.ap, ALU.add, ALU.mult, tb2.res + tb3.res, t1.res)
                    stt("dve", mv, t1.ap, 0.25, gc2[j].ap[:, g * 512:(g + 1) * 512], ALU.mult, ALU.mult,
                        t1.res + gc2[j].res, [r_mix[j][g]])

                stage(6)
                for t in range(NTILE):
                    sch.add("pool", lambda e, t=t: e.memset(
                        Vt[t].ap.rearrange("p (i c) -> p i c", c=192)[:, :, 64:128], 1.0), [], Vt[t].res)
                vsl = [[w_next(40 + 4 * g + q) for q in range(4)] for g in range(2)]
                assert vsl[0][0] % 4 == 0 and vsl[1][0] % 4 == 0
                lnq = [(j, g) for j in range(8) for g in range(2)]
                nv = 0
                for g in range(2):
                    s0 = vsl[g][0]
                    for t in range(NTILE):
                        bv, rv = banks[4 + (t % 2)]
                        for k in range(8):
                            rhs = wring[:, s0 * 1024 + k * 512:s0 * 1024 + (k + 1) * 512]
                            mm(bv, uTv(k, t * 128, (t + 1) * 128), rhs, k == 0, k == 7,
                               [r_w[s_] for s_ in vsl[g]] + [r_uT[t]], [rv])
                        v3 = Vt[t].ap.rearrange("p (i c) -> p i c", c=192)
                        dst = v3[:, :, 0:64] if g == 0 else v3[:, :, 128:192]
                        cp("act", dst, bv.rearrange("p (i c) -> p i c", c=64), [rv], Vt[t].res)
                        nv += 1
                        while lnq and (16 - len(lnq)) * 24 < nv * 16:
                            ln_apply(*lnq.pop(0))
                while lnq:
                    ln_apply(*lnq.pop(0))
                w_prefetch(4)

                O3 = [(O0[:, :], rO0), (O1[:, :], rO1), (TB[:].bitcast(F32), rTB)]

                def pair_tasks(hp):
                    pi = hp % 2
                    ti = hp % 2
                    bp, rp = banks[6]
                    st_ = {"n": 0}
                    tasks = []
                    rotb = [banks[6], banks[1], banks[3]]

                    def nextbank():
                        st_["n"] += 1
                        return rotb[st_["n"] % 3]

                    def setup():
                        st_["q"] = w_next(24 + hp)
                        st_["k"] = w_next(32 + hp)
                        st_["a"] = w_next(48 + hp)
                        dma("sp", Ttab[ti].ap, T_bf[hp], [r_Tscr[hp]], Ttab[ti].res + [r_T[ti]], own=r_T[ti])

                    def qgrp(g):
                        bp, rp = nextbank()
                        sq_ = st_["q"]
                        for k in range(8):
                            mm(bp, wv(sq_, k), uTv(k, M0 + g * 512, M0 + (g + 1) * 512), k == 0, k == 7,
                               [r_w[sq_]] + all_uT, [rp])
                        cp("act", QT[pi][0].ap[0:64, g * 512:(g + 1) * 512], bp[0:64, :], [rp], QT[pi][0].res)
                        cp("act", QT[pi][1].ap[0:64, g * 512:(g + 1) * 512], bp[64:128, :], [rp], QT[pi][1].res)

                    def kgrp(g):
                        bp, rp = nextbank()
                        sk_ = st_["k"]
                        for k in range(8):
                            mm(bp, wv(sk_, k), uTv(k, g * 512, (g + 1) * 512), k == 0, k == 7,
                               [r_w[sk_]] + all_uT, [rp])
                        cp("dve", KT[pi][0].ap[0:64, g * 512:(g + 1) * 512], bp[0:64, :], [rp], KT[pi][0].res)
                        cp("act", KT[pi][1].ap[0:64, g * 512:(g + 1) * 512], bp[64:128, :], [rp], KT[pi][1].res)

                    def agrp(g):
                        bp, rp = nextbank()
                        sa_ = st_["a"]
                        for k in range(8):
                            mm(bp, wv(sa_, k), uTv(k, M0 + g * 512, M0 + (g + 1) * 512), k == 0, k == 7,
                               [r_w[sa_]] + all_uT, [rp])
                        act(AG[pi].ap[:, g * 512:(g + 1) * 512], bp, AF.Tanh, [rp], AG[pi].res, scale=0.5)
                        stt("dve", AG[pi].ap[:, g * 512:(g + 1) * 512], AG[pi].ap[:, g * 512:(g + 1) * 512], 1.0, bp,
                            ALU.add, ALU.mult, AG[pi].res + [rp], AG[pi].res)
                        if g == 1:
                            w_prefetch(4)

                    def mgrp(hh):
                        pm = PmT[pi][hh]
                        for b in range(2):
                            mm(bp[0:16, :], kmeta[0:64, hp * 32 + hh * 16:hp * 32 + hh * 16 + 16],
                               QT[pi][hh].ap[0:64, b * 512:(b + 1) * 512], True, True,
                               [r_meta] + QT[pi][hh].res, [rp])
                            act(pm.ap[:, b * 512:(b + 1) * 512], bp[0:16, :], AF.Exp, [rp], pm.res, scale=0.125)

                    tasks.append(lambda: (setup(), qgrp(0)))
                    tasks.append(lambda: qgrp(1))
                    for g in range(3):
                        tasks.append(lambda g=g: kgrp(g))
                    for g in range(2):
                        tasks.append(lambda g=g: agrp(g))
                    for hh in range(2):
                        tasks.append(lambda hh=hh: mgrp(hh))
                    return tasks

                def pair_proj(hp):
                    for t_ in pair_tasks(hp):
                        t_()

                steps = [(hp, hh, p) for hp in range(8) for hh in range(2) for p in range(NTILE)]

                def step_geom(p):
                    rlo, rhi = tile_rows(p)
                    n = (rhi - rlo + 1) * 64
                    chunks = []
                    c0 = 0
                    while c0 < n:
                        cn = min(512, n - c0)
                        chunks.append((c0, cn))
                        c0 += cn
                    return rlo, rhi, n, chunks

                def emit_qk(idx):
                    hp, hh, p = steps[idx]
                    pi = hp % 2
                    hb = 64 * hh
                    rlo, rhi, n, chunks = step_geom(p)
                    Sb, rS = Sbuf[idx % 2]
                    for _jk in range(NJUNK):
                        mm(Sb[:, 0:512], ident_s[:, :], uTv(0, 0, 512), True, True, [r_const] + all_uT, [rS[0]])
                    for ci, (c0, cn) in enumerate(chunks):
                        q0 = rlo * 64 + c0
                        mm(Sb[:, c0:c0 + cn], KT[pi][hh].ap[0:80, p * 128:(p + 1) * 128],
                           QT[pi][hh].ap[0:80, q0:q0 + cn], True, True,
                           KT[pi][hh].res + QT[pi][hh].res, [rS[ci]])

                def obank(hp, hh, b):
                    return O3[(2 * (2 * hp + hh) + b) % 3]

                def emit_expmult(idx):
                    hp, hh, p = steps[idx]
                    ti = hp % 2
                    rlo, rhi, n, chunks = step_geom(p)
                    Sb, rS = Sbuf[idx % 2]
                    xi = idx % 2
                    rSu = rS[0:len(chunks)]
                    act(expS[xi].ap[:, 0:n], Sb[:, 0:n], AF.Exp, rSu, expS[xi].res, scale=0.125)
                    slot0 = 11 - 2 * p + rlo
                    tab = Ttab[ti].ap[:, hh * 1024 + slot0 * 64:hh * 1024 + slot0 * 64 + n]
                    tt("dve", PT[xi].ap[:, 0:n], expS[xi].ap[:, 0:n], tab, ALU.mult,
                       expS[xi].res + Ttab[ti].res + [r_T[ti]], PT[xi].res)

                def emit_pv(idx):
                    hp, hh, p = steps[idx]
                    pi = hp % 2
                    hb = 64 * hh
                    pm = PmT[pi][hh]
                    rlo, rhi, n, chunks = step_geom(p)
                    xi = idx % 2
                    if p == 0:
                        if hh == 0:
                            vmeta = vmA[0:16, hp * 128:(hp + 1) * 128]
                            pmrows = (0, 16)
                        else:
                            vmeta = vmB[0:16, hp * 128:(hp + 1) * 128]
                            pmrows = (0, 16)
                        for b in range(2):
                            ob, ro = obank(hp, hh, b)
                            mm(ob, vmeta, pm.ap[pmrows[0]:pmrows[1], b * 512:(b + 1) * 512], True, False,
                               [r_meta] + pm.res, [ro], skip_group_check=True)
                    vt = Vt[p].ap
                    lhs = vt[:, 192 * hp + 64 * hh:192 * hp + 64 * hh + 128]
                    for b in range(2):
                        lo = max(rlo, 8 * b)
                        hi = min(rhi, 8 * b + 7)
                        if lo > hi:
                            continue
                        ob, ro = obank(hp, hh, b)
                        mm(ob[:, (lo - 8 * b) * 64:(hi - 8 * b + 1) * 64], lhs,
                           PT[xi].ap[:, (lo - rlo) * 64:(hi - rlo + 1) * 64], False, True,
                           Vt[p].res + PT[xi].res, [ro], skip_group_check=True)
                    for b, plast in ((0, 7), (1, NTILE - 1)):
                        if p != plast:
                            continue
                        ob, ro = obank(hp, hh, b)
                        o_lo, o_hi = hb, hb + 64
                        d_lo, d_hi = 64 - hb, 128 - hb
                        R = Rb[b]
                        at = attb[b]
                        for c4 in range(4):
                            cs = slice(c4 * 128, (c4 + 1) * 128)
                            pending.append(lambda ob=ob, R=R, ro=ro, cs=cs, d_lo=d_lo, d_hi=d_hi, o_lo=o_lo, o_hi=o_hi:
                                           sch.add("dve", lambda e: e.reciprocal(out=R.ap[o_lo:o_hi, cs], in_=ob[d_lo:d_hi, cs]),
                                                   [ro], R.res))

                        def fin(ob=ob, ro=ro, R=R, at=at, o_lo=o_lo, o_hi=o_hi, hp=hp, b=b, pi=pi):
                            act(at.ap[o_lo:o_hi, :], ob[o_lo:o_hi, :], AF.Identity, [ro], at.res, scale=0.5)
                            tt("pool", at.ap[o_lo:o_hi, :], at.ap[o_lo:o_hi, :], R.ap[o_lo:o_hi, :], ALU.mult,
                               at.res + R.res, at.res)
                            tt("pool", mixT[o_lo:o_hi, (8 + hp) * NM + b * 512:(8 + hp) * NM + (b + 1) * 512],
                               at.ap[o_lo:o_hi, :], AG[pi].ap[o_lo:o_hi, b * 512:(b + 1) * 512], ALU.mult,
                               at.res + AG[pi].res, [r_mix[8 + hp][b]])
                        pending.append(fin)

                pending = []
                projq = []
                for pi_ in range(2):
                    for hh_ in range(2):
                        kidx = pi_ * 2 + hh_
                        dma("pool", KT[pi_][hh_].ap[64:80, :], maskA[sgi], [], KT[pi_][hh_].res, own=r_mask[kidx])
                        dma("pool", QT[pi_][hh_].ap[64:80, :], conehot[:], [], QT[pi_][hh_].res, own=r_mask[4 + kidx])
                pair_proj(0)
                emit_qk(0)
                for idx in range(len(steps)):
                    hp, hh, p = steps[idx]
                    if hh == 1 and p == 2 and hp + 1 < 8:
                        pair_proj(hp + 1)
                    if idx + 1 < len(steps):
                        emit_qk(idx + 1)
                    if hp == 7 and nxt is not None and (idx % 2 == 0):
                        tl = (idx - 7 * 24) // 2
                        if 0 <= tl < NTILE:
                            p0_c(nxt, [tl], only_pj=True)
                            p0c_done[nxt] = True
                    emit_expmult(idx)
                    if pending:
                        pending.pop(0)()
                    if idx >= 1:
                        emit_pv(idx - 1)
                emit_pv(len(steps) - 1)
                while pending:
                    pending.pop(0)()


                if DBG and sgi == nseg - 1:
                    allmix = [r_mix[e_][g_] for e_ in range(16) for g_ in range(2)]
                    dma("sp", dbg_mix[:, :], mixT[:, :], allmix, [], own=r_init)
                dma("sp", woutS.ap, wout_bf[:, :], [r_scr], woutS.res + [r_wout], own=r_wout)
                for t in range(NM // 128):
                    Sb, rS = Sbuf[rot["S"] % 2]
                    rot["S"] += 1
                    oi = rot["out"] % 2
                    rot["out"] += 1
                    g = t // 4
                    dma("sp", xres[oi].ap, xe[sgi, M0 + t * 128:M0 + (t + 1) * 128, :], [],
                        xres[oi].res + [r_xres[oi]], own=r_xres[oi])
                    for half in range(2):
                        for e_ in range(16):
                            mm(Sb[:, half * 512:(half + 1) * 512], mixv(e_, t * 128, (t + 1) * 128),
                               woutS.ap[:, e_ * 1024 + half * 512:e_ * 1024 + (half + 1) * 512], e_ == 0, e_ == 15,
                               [r_mix[e_][g]] + woutS.res + [r_wout], [rS[half]])
                    si = cnt["st"] % 4
                    cnt["st"] += 1
                    st = stat[si]
                    ob = outb[oi]
                    act(ob.ap, Sb[:, :], AF.Square, rS, ob.res + [r_out[oi], r_stat[si]], accum_out=st[:, 0:1])
                    ts("dve", st[:, 1:2], st[:, 0:1], 1.0 / D, 1e-6, ALU.mult, ALU.add, [r_stat[si]], [r_stat[si]])
                    rsqrt_col(st, 128, [r_stat[si]])
                    stt("dve", ob.ap, Sb[:, :], st[:, 2:3], gpost_s[:], ALU.mult, ALU.mult,
                        rS + [r_stat[si], r_const], ob.res + [r_out[oi]])
                    tt("pool", ob.ap, ob.ap, xres[oi].ap, ALU.add, ob.res + [r_out[oi]] + xres[oi].res + [r_xres[oi]],
                       ob.res + [r_out[oi]])
                    dma("sp", ye[sgi, t * 128:(t + 1) * 128, :], ob.ap, ob.res + [r_out[oi]], [], own=r_out[oi])

        except StopBuild:
            pass

        def final_waits():
            allr = r_out + r_xt + r_w + r_mask + r_T + r_diag + [r_wout, r_init, r_tb, r_db] + r_xres
            return [(r.sem, r.cnt) for r in allr if r.cnt > 0] + [(g.sem, g.cnt) for g in (ginit, gconst) if g.cnt > 0]

        sch.check_deadlock(esem, final_waits)
        with nc.Block() as block:
            sch.emit(block, esem, final_waits)
    return nc


def _col_tables():
    key = np.arange(GW)
    q = np.arange(GW)
    start = np.clip(q - 8, 0, GW - 16)
    off = key[:, None] - start[None, :]
    valid = (off >= 0) & (off < 16)
    rel = np.clip(key[:, None] - q[None, :] + 15, 0, 30)
    return valid, rel


def host_weights(w_in, w_out, conv_w, conv_b, ln_g, ln_b, rel_bias, pre_g, post_g, meta_tokens):
    w_in = np.asarray(w_in, np.float32)[0]
    cols = np.arange(7168)
    vperm = np.zeros(1024, np.int64)
    for n in range(1024):
        if n < 512:
            head = 2 * (n // 64)
        else:
            head = 2 * ((n - 512) // 64) + 1
        vperm[n] = 5120 + head * 64 + n % 64
    cols[5120:6144] = vperm
    wp = w_in[:, cols]
    win = np.ascontiguousarray(wp.reshape(8, 128, 56, 128).transpose(2, 1, 0, 3)).reshape(56, 128, 1024)
    for g in range(2):
        blk = wp[:, 5120 + g * 512:5120 + (g + 1) * 512].reshape(8, 128, 512).transpose(1, 0, 2)
        win[40 + 4 * g:44 + 4 * g] = blk.reshape(128, 4, 1024).transpose(1, 0, 2)
    wo = np.asarray(w_out, np.float32)[0]
    wout = np.ascontiguousarray(wo.reshape(16, 128, 1024).transpose(1, 0, 2)).reshape(128, 16 * 1024)
    cw = np.asarray(conv_w, np.float32)[0]
    convwT = np.ascontiguousarray(cw.reshape(CONVW, 8, 128).transpose(2, 1, 0)).reshape(128, 8 * CONVW)
    chvec = np.concatenate([np.asarray(v, np.float32)[0].reshape(8, 128).T for v in (conv_b, ln_g, ln_b)], axis=1)
    gpre = np.ascontiguousarray(np.broadcast_to(np.asarray(pre_g, np.float32)[0][None, :], (128, D)))
    gpost = np.ascontiguousarray(np.broadcast_to(np.asarray(post_g, np.float32)[0][None, :], (128, D)))
    rb = np.asarray(rel_bias, np.float32)[0]
    valid, rel = _col_tables()
    Bt = np.zeros((128, 16, 16, 64), np.float32)
    Mt = np.zeros((128, 16, 64), np.float32)
    for half in range(2):
        for slot in range(16):
            d = 7 - slot
            dr = d + half
            if abs(dr) > 7:
                continue
            Bt[half * 64:(half + 1) * 64, :, slot, :] = rb[:, dr + 7, :][:, rel].transpose(1, 0, 2)
            Mt[half * 64:(half + 1) * 64, slot, :] = valid.astype(np.float32)
    conehot = np.zeros((16, NM), np.float32)
    for r in range(16):
        conehot[r, r * 64:(r + 1) * 64] = 1.0
    return {
        "win": win, "wout": wout, "convwT": convwT, "chvec": np.ascontiguousarray(chvec),
        "gpre": gpre, "gpost": gpost, "meta": np.ascontiguousarray(np.asarray(meta_tokens, np.float32)),
        "Bt": Bt.reshape(128, -1), "Mt": Mt.reshape(128, -1), "conehot": conehot,
        "ident": np.eye(128, dtype=np.float32),
    }


def host_segment(xseq, meta_tokens, R0):
    T = xseq.shape[0]
    rows = T // GW
    xe = np.zeros((NE, D), np.float32)
    t0 = R0 * GW - M0
    lo = max(0, t0)
    hi = min(T, t0 + NE)
    xe[lo - t0:hi - t0] = xseq[lo:hi]
    if t0 < 0:
        xe[-t0 - NMETA:-t0] = meta_tokens
    mask = np.full((16, NE), NEG, np.float32)
    wr = min(8, rows)
    for r in range(16):
        R = R0 + r
        sr = int(np.clip(R - wr // 2, 0, rows - wr))
        for kr in range(sr, sr + wr):
            er = kr - R0 + HALO
            if 0 <= er < EXT_ROWS:
                mask[r, er * GW:(er + 1) * GW] = 0.0
    return xe, mask


def run_segments(seg_lists, wts, n_cores):
    nseg = len(seg_lists[0])
    nc = build_program(nseg)
    in_maps = []
    for c in range(n_cores):
        xe = np.zeros((nseg, NE, D), np.float32)
        mk = np.zeros((nseg, 16, NE), np.float32)
        for s, (xseq, R0) in enumerate(seg_lists[c]):
            xe[s], mk[s] = host_segment(xseq, wts["meta"], R0)
        m = dict(wts)
        m["xe"] = xe
        m["maskA"] = mk
        in_maps.append(m)
    res = run_bass_kernel_spmd(nc, in_maps, core_ids=list(range(n_cores)))
    if int(os.environ.get('KDBG', '0')):
        return [r["ye"] for r in res.results], [(r["dbg_mix"], r["dbg_uT"], r["dbg_w"]) for r in res.results]
    return [r["ye"] for r in res.results]


def kernel(x_prompt, x_sample, meta_tokens, pre_norm_g, w_in, conv_w, conv_b, conv_ln_g, conv_ln_b,
           rel_bias, post_norm_g, w_out):
    x_prompt = np.asarray(x_prompt, np.float32)
    x_sample = np.asarray(x_sample, np.float32)
    wts = host_weights(w_in, w_out, conv_w, conv_b, conv_ln_g, conv_ln_b, rel_bias, pre_norm_g, post_norm_g,
                       meta_tokens)
    seg_lists = []
    where = []
    for c in range(NCORES):
        segs = []
        wh = []
        for i in range(4):
            b = 4 * c + i
            for R0 in (0, 16):
                segs.append((x_prompt[b], R0))
                wh.append((0, b, R0))
        sb_ = c // 2
        for R0 in (32 * (c % 2), 32 * (c % 2) + 16):
            segs.append((x_sample[sb_], R0))
            wh.append((1, sb_, R0))
        seg_lists.append(segs)
        where.append(wh)
    outs = run_segments(seg_lists, wts, NCORES)
    y_prompt = np.empty_like(x_prompt)
    y_sample = np.empty_like(x_sample)
    for c in range(NCORES):
        for s, (which, b, R0) in enumerate(where[c]):
            dst = y_prompt if which == 0 else y_sample
            dst[b, R0 * GW:(R0 + SEG_ROWS) * GW] = outs[c][s]
    return (y_prompt, y_sample)
```

```python
import os
import numpy as np
import concourse.bass as bass
import concourse.mybir as mybir
from concourse.bass_utils import run_bass_kernel_spmd

F32 = mybir.dt.float32
BF16 = mybir.dt.bfloat16
I32 = mybir.dt.int32
MAGIC = 0x5f3759df
ALU = mybir.AluOpType
AF = mybir.ActivationFunctionType

D = 1024
NMETA = 16
GW = 64
SEG_ROWS = 16
HALO = 4
EXT_ROWS = SEG_ROWS + 2 * HALO
NE = EXT_ROWS * GW
NM = SEG_ROWS * GW
M0 = HALO * GW
NTILE = NE // 128
CONVW = 31
PAD = 15
NUG = NM + 2 * PAD
NEG = -30000.0
NCORES = 8
WRING = 8


class Res:
    __slots__ = ("name", "lw", "rd", "sem", "cnt")

    def __init__(self, name):
        self.name = name
        self.lw = None
        self.rd = []
        self.sem = None
        self.cnt = 0


class Group:
    def __init__(self, sem):
        self.sem = sem
        self.cnt = 0


class Op:
    __slots__ = ("eng", "fn", "deps", "signal", "val", "dsem", "dval", "grp", "xw")

    def __init__(self, eng, fn):
        self.eng = eng
        self.fn = fn
        self.deps = []
        self.signal = False
        self.val = 0
        self.dsem = None
        self.dval = 0
        self.grp = None
        self.xw = None


class Sched:
    ENGS = ("pe", "act", "dve", "pool", "sp")

    def __init__(self, nc):
        self.nc = nc
        self.ops = {e: [] for e in self.ENGS}
        self.esem = {}
        self.out_ops = []

    def _deps(self, op, reads, writes):
        deps = []
        for r in reads:
            if r.lw is not None:
                deps.append(r.lw)
        for w in writes:
            if w.lw is not None:
                deps.append(w.lw)
            deps.extend(w.rd)
        seen = set()
        for d in deps:
            if id(d) in seen or d is op:
                continue
            seen.add(id(d))
            if d.dsem is None and d.grp is None:
                if d.eng == "pe" and op.eng == "pe":
                    continue
                d.signal = True
            op.deps.append(d)
        for r in reads:
            r.rd.append(op)
        for w in writes:
            w.lw = op
            w.rd = []

    def add(self, eng, fn, reads=(), writes=()):
        op = Op(eng, fn)
        self._deps(op, reads, writes)
        self.ops[eng].append(op)
        return op

    def dma(self, q, fn, reads=(), writes=(), own=None, grp=None):
        op = Op(q, fn)
        self._deps(op, reads, writes)
        if own is not None:
            own.cnt += 16
            op.dsem = own.sem
            op.dval = own.cnt
        else:
            op.deps = [d for d in op.deps if d.grp is not grp]
            grp.cnt += 16
            op.grp = grp
            if q == "pool" and grp.cnt > 16 * 24:
                op.xw = (grp.sem, grp.cnt - 16 * 24)
        self.ops[q].append(op)
        return op

    def check_deadlock(self, sems, final_waits):
        for e in self.ENGS:
            c = 0
            for op in self.ops[e]:
                if op.dsem is None and op.grp is None and op.signal:
                    c += 1
                    op.val = c

        def ev(d):
            if d.dsem is not None:
                return id(d.dsem), d.dval
            if d.grp is not None:
                return id(d.grp.sem), d.grp.cnt
            return id(sems[d.eng]), d.val
        prog = {}
        for e in self.ENGS:
            lst = []
            for op in self.ops[e]:
                waits = [ev(d) for d in op.deps]
                if op.xw is not None:
                    waits.append((id(op.xw[0]), op.xw[1]))
                if op.dsem is not None:
                    inc = (id(op.dsem), 16)
                elif op.grp is not None:
                    inc = (id(op.grp.sem), 16)
                elif op.signal:
                    inc = (id(sems[e]), 1)
                else:
                    inc = None
                lst.append((waits, inc))
            if e == "sp":
                lst.append(([(id(s_), v) for s_, v in final_waits()], None))
            prog[e] = lst
        val = {}
        pc = {e: 0 for e in self.ENGS}
        progress = True
        while progress:
            progress = False
            for e in self.ENGS:
                while pc[e] < len(prog[e]):
                    waits, inc = prog[e][pc[e]]
                    if all(val.get(s_, 0) >= v for s_, v in waits):
                        if inc is not None:
                            val[inc[0]] = val.get(inc[0], 0) + inc[1]
                        pc[e] += 1
                        progress = True
                    else:
                        break
        stuck = {e: (pc[e], len(prog[e])) for e in self.ENGS if pc[e] < len(prog[e])}
        if stuck:
            names = {id(v): k for k, v in sems.items()}
            msg = []
            for e, (p, n) in stuck.items():
                waits, _ = prog[e][p]
                msg.append(f"{e} stuck at {p}/{n} waiting " + str([(names.get(s_, s_), v, val.get(s_, 0)) for s_, v in waits if val.get(s_, 0) < v]))
            raise RuntimeError("DEADLOCK in schedule: " + "; ".join(msg))

    def emit(self, block, sems, final_waits):
        nc = self.nc
        for e in self.ENGS:
            c = 0
            for op in self.ops[e]:
                if op.dsem is None and op.grp is None and op.signal:
                    c += 1
                    op.val = c
        esem = sems
        sched = self

        def ev(d):
            if d.dsem is not None:
                return d.dsem, d.dval
            if d.grp is not None:
                return d.grp.sem, d.grp.cnt
            return esem[d.eng], d.val

        def run(e, eng):
            known = {}
            for op in sched.ops[e]:
                for d in op.deps:
                    s, v = ev(d)
                    key = id(s)
                    if known.get(key, 0) < v:
                        eng.wait_ge(s, v)
                        known[key] = v
                if op.xw is not None:
                    eng.wait_ge(op.xw[0], op.xw[1])
                ins = op.fn(eng)
                if op.dsem is not None:
                    ins.then_inc(op.dsem, 16)
                elif op.grp is not None:
                    ins.then_inc(op.grp.sem, 16)
                elif op.signal:
                    ins.then_inc(esem[e], 1)
            if e == "sp":
                for s, v in final_waits():
                    eng.wait_ge(s, v)

        @block.tensor
        def _(eng):
            run("pe", eng)

        @block.scalar
        def _(eng):
            run("act", eng)

        @block.vector
        def _(eng):
            run("dve", eng)

        @block.gpsimd
        def _(eng):
            run("pool", eng)

        @block.sync
        def _(eng):
            run("sp", eng)


def tile_rows(p, kind="gen"):
    lo = max(0, 2 * p - 7)
    hi = min(SEG_ROWS - 1, 2 * p + 1)
    rows = set(range(lo, hi + 1)) if lo <= hi else set()
    if p <= 5 and kind in ("gen", "lo"):
        rows |= set(range(0, 4))
    if p >= 6 and kind in ("gen", "hi"):
        rows |= set(range(12, 16))
    rows = sorted(rows)
    assert rows == list(range(rows[0], rows[-1] + 1))
    return rows[0], rows[-1]


def build_program(nseg):
    NJUNK = int(os.environ.get('KJUNK', '0'))
    STAGE = int(os.environ.get('KSTAGE', '9'))
    nc = bass.Bass("TRN2", target_bir_lowering=False)

    def din(name, shape, dt=F32):
        return nc.dram_tensor(name, list(shape), dt, kind="ExternalInput").ap()

    xe = din("xe", [nseg, NE, D])
    maskA = din("maskA", [nseg, 16, NE])
    win = din("win", [56, 128, 1024])
    wout = din("wout", [128, 16 * 1024])
    convwT = din("convwT", [128, 8 * CONVW])
    chvec = din("chvec", [128, 24])
    gpre = din("gpre", [128, D])
    gpost = din("gpost", [128, D])
    meta = din("meta", [NMETA, D])
    Bt = din("Bt", [128, 16 * 16 * 64])
    Mt = din("Mt", [128, 16 * 64])
    conehot = din("conehot", [16, NM])
    identf = din("ident", [128, 128])
    ye = nc.dram_tensor("ye", [nseg, NM, D], F32, kind="ExternalOutput").ap()
    DBG = int(os.environ.get('KDBG', '0'))
    if DBG:
        dbg_mix = nc.dram_tensor("dbg_mix", [128, 16 * NM], BF16, kind="ExternalOutput").ap()
        dbg_uT = nc.dram_tensor("dbg_uT", [128, 8 * NE], BF16, kind="ExternalOutput").ap()
        dbg_w = nc.dram_tensor("dbg_w", [128, 1024], BF16, kind="ExternalOutput").ap()
    win_bf = nc.dram_tensor("win_bf", [56, 128, 1024], BF16, kind="Internal").ap()
    wout_bf = nc.dram_tensor("wout_bf", [128, 16 * 1024], BF16, kind="Internal").ap()
    T_bf = nc.dram_tensor("T_bf", [8, 128, 2048], BF16, kind="Internal").ap()
    diag_bf = nc.dram_tensor("diag_bf", [8, 128, CONVW * 128], BF16, kind="Internal").ap()

    import contextlib
    es = contextlib.ExitStack()
    with es:
        def sb(name, shape, dt):
            return es.enter_context(nc.sbuf_tensor(name, list(shape), dt))

        def ps(name, shape, dt):
            return es.enter_context(nc.psum_tensor(name, list(shape), dt))

        def sem(name):
            return es.enter_context(nc.semaphore(name))

        uT = sb("uT", [128, 8 * NE], BF16)
        mixT = sb("mixT", [128, 16 * NM], BF16)
        wring = sb("wring", [128, WRING * 1024], BF16)
        xt = [sb(f"xt{i}", [128, D], F32) for i in range(2)]
        ubf = [sb(f"ubf{i}", [128, D], BF16) for i in range(2)]
        gpre_s = sb("gpre_s", [128, D], F32)
        gpost_s = sb("gpost_s", [128, D], F32)
        ident_s = sb("ident_s", [128, 128], BF16)
        ones_s = sb("ones_s", [128, 128], BF16)
        kmeta = sb("kmeta", [64, 8 * 2 * 16], BF16)
        vmA = sb("vmA", [16, 8 * 128], BF16)
        vmB = sb("vmB", [16, 8 * 128], BF16)
        uTm = sb("uTm", [128, 8 * 16], BF16)
        chv = sb("chv", [128, 24], F32)
        cwT = sb("cwT", [128, 8 * CONVW], F32)
        halfg = sb("halfg", [128, 8], F32)
        halfb = sb("halfb", [128, 8], F32)
        stat = [sb(f"stat{i}", [128, 8], F32) for i in range(4)]
        A16N = 41984
        A32N = 4096
        ar16 = sb("ar16", [128, A16N], BF16)
        ar32 = sb("ar32", [128, A32N], F32)
        G16 = 512
        G32 = 256
        r16 = [Res(f"a16_{i}") for i in range((A16N + G16 - 1) // G16)]
        r32 = [Res(f"a32_{i}") for i in range((A32N + G32 - 1) // G32)]

        class Buf:
            def __init__(self, ap, res):
                self.ap = ap
                self.res = res

        def a16(off, n, parts=128):
            return Buf(ar16[0:parts, off:off + n], r16[off // G16:(off + n - 1) // G16 + 1])

        def a32(off, n, parts=128):
            return Buf(ar32[0:parts, off:off + n], r32[off // G32:(off + n - 1) // G32 + 1])

        UGP = 1056
        tmpb = [a16(i * 512, 512) for i in range(8)]
        UG0 = 4096 + NTILE * 1536
        gc2 = [a16(UG0 + j * 1024, 1024) for j in range(8)]
        ugT = [a16(UG0 + 8192 + i * UGP, UGP) for i in range(2)]
        DG0 = UG0 + 8192 + 2 * UGP
        diag = [a16(DG0 + i * 3968, 3968) for i in range(2)]
        assert DG0 + 2 * 3968 <= A16N
        stA = a32(0, 1024)
        stB = a32(1024, 1024)
        tf = [a32(2048 + i * 512, 512) for i in range(3)]
        Ttab = [a16(i * 2048, 2048) for i in range(2)]
        VW = 1536
        V0 = 4096
        Vt = [a16(V0 + t * VW, VW) for t in range(NTILE)]
        QK0 = V0 + NTILE * VW
        QKS = 2 * 1024 + 2 * NE + 1024
        QT = [[a16(QK0 + i * QKS + hh * 1024, 1024) for hh in range(2)] for i in range(2)]
        KT = [[a16(QK0 + i * QKS + 2048 + hh * NE, NE) for hh in range(2)] for i in range(2)]
        AG = [a16(QK0 + i * QKS + 2048 + 2 * NE, 1024) for i in range(2)]
        EX0 = QK0 + 2 * QKS
        expS = [a16(EX0 + i * 768, 768) for i in range(2)]
        PT = [a16(EX0 + 1536 + i * 768, 768) for i in range(2)]
        PM0 = EX0 + 3072
        PmT = [[a16(PM0 + i * 2048 + hh * 1024, 1024, parts=16) for hh in range(2)] for i in range(2)]
        assert PM0 + 4096 <= A16N, (PM0, A16N)
        Rb = [a32(i * 512, 512) for i in range(2)]
        attb = [a32(1024 + i * 512, 512) for i in range(2)]
        woutS = a16(0, 16384)
        outb = [a32(i * 1024, 1024) for i in range(2)]
        xres = [a32(2048 + i * 1024, 1024) for i in range(2)]
        btmp = a32(0, 2048)
        btmp2 = a32(2048, 2048)
        tbuild = a16(0, 2048)
        dbuild = a16(4096, 3968)
        xmeta = a32(0, 1024, parts=16)

        S0 = ps("S0", [128, 1024], F32)
        S1 = ps("S1", [128, 1024], F32)
        O0 = ps("O0", [128, 512], F32)
        O1 = ps("O1", [128, 512], F32)
        PJ = ps("PJ", [128, 512], F32)
        TB = ps("TB", [128, 1024], BF16)
        rS0a, rS0b, rS1a, rS1b, rO0, rO1, rPJ, rTB = [Res(n) for n in
                                                    ("S0a", "S0b", "S1a", "S1b", "O0", "O1", "PJ", "TB")]
        rPJa, rPJb = rPJ, Res("PJb")
        banks = [(S0[:, 0:512], rS0a), (S0[:, 512:1024], rS0b), (S1[:, 0:512], rS1a),
                 (S1[:, 512:1024], rS1b), (O0[:, :], rO0), (O1[:, :], rO1), (PJ[:, :], rPJ)]
        Sbuf = [(S0, [rS0a, rS0b]), (S1, [rS1a, rS1b])]
        Obuf = [(O0, rO0), (O1, rO1)]

        esem = {e: sem("s_" + e) for e in Sched.ENGS}
        sch = Sched(nc)
        ginit = Group(sem("g_init"))
        gconst = Group(sem("g_const"))

        def own(r, name):
            r.sem = sem(name)
            return r

        r_xt = [own(Res(f"xt{i}"), f"d_xt{i}") for i in range(2)]
        r_ubf = [Res(f"ubf{i}") for i in range(2)]
        r_w = [own(Res(f"w{i}"), f"d_w{i}") for i in range(WRING)]
        r_mask = [own(Res(f"mask{i}"), f"d_mask{i}") for i in range(8)]
        r_uT = [Res(f"uT{t}") for t in range(NTILE)]
        r_mix = [[Res(f"mix{e}_{g}") for g in range(2)] for e in range(16)]
        r_stat = [Res(f"stat{i}") for i in range(4)]
        r_const = Res("const")
        r_const2 = Res("const2")
        r_scr = Res("scratch")
        r_meta = Res("metabufs")
        r_T = [own(Res(f"T{i}"), f"d_T{i}") for i in range(2)]
        r_diag = [own(Res(f"dg{i}"), f"d_dg{i}") for i in range(2)]
        r_wout = own(Res("woutS"), "d_wout")
        r_xres = [own(Res(f"xres{i}"), f"d_xres{i}") for i in range(2)]
        r_out = [own(Res(f"out{i}"), f"d_out{i}") for i in range(2)]
        r_init = own(Res("initbuf"), "d_initbuf")
        r_tb = own(Res("tb"), "d_tb")
        r_db = own(Res("db"), "d_db")
        r_Tscr = [Res(f"Tscr{i}") for i in range(8)]
        r_dscr = [Res(f"dscr{i}") for i in range(8)]

        def mm(out, lhsT, rhs, start, stop, reads, writes, **kw):
            return sch.add("pe", lambda e: e.matmul(out, lhsT=lhsT, rhs=rhs, start=start, stop=stop, **kw),
                           reads, writes)

        def act(out, in_, func, reads, writes, **kw):
            return sch.add("act", lambda e: e.activation(out=out, in_=in_, func=func, **kw), reads, writes)

        def tt(eng, out, in0, in1, op, reads, writes):
            return sch.add(eng, lambda e: e.tensor_tensor(out=out, in0=in0, in1=in1, op=op), reads, writes)

        def ts(eng, out, in0, s1, s2, op0, op1, reads, writes):
            if s2 is None:
                return sch.add(eng, lambda e: e.tensor_scalar(out=out, in0=in0, scalar1=s1, scalar2=None, op0=op0),
                               reads, writes)
            return sch.add(eng, lambda e: e.tensor_scalar(out=out, in0=in0, scalar1=s1, scalar2=s2, op0=op0, op1=op1),
                           reads, writes)

        def stt(eng, out, in0, scalar, in1, op0, op1, reads, writes):
            return sch.add(eng, lambda e: e.scalar_tensor_tensor(out=out, in0=in0, scalar=scalar, in1=in1,
                                                                 op0=op0, op1=op1), reads, writes)

        def rsqrt(a, r, w, ra, rr, rw):
            ri = r.bitcast(I32)
            ts("dve", ri, a.bitcast(I32), 1, None, ALU.arith_shift_right, None, ra, rr)
            ts("dve", ri, ri, -1, MAGIC, ALU.mult, ALU.add, rr, rr)
            for _ in range(3):
                tt("dve", w, a, r, ALU.mult, ra + rr, rw)
                tt("dve", w, w, r, ALU.mult, rw + rr, rw)
                ts("dve", w, w, -0.5, 1.5, ALU.mult, ALU.add, rw, rw)
                tt("dve", r, r, w, ALU.mult, rr + rw, rr)

        def rsqrt_col(st, n, rr):
            a, r, h, t = st[0:n, 1:2], st[0:n, 2:3], st[0:n, 3:4], st[0:n, 4:5]
            ri = r.bitcast(I32)
            ts("dve", ri, a.bitcast(I32), 1, None, ALU.arith_shift_right, None, rr, rr)
            ts("dve", ri, ri, -1, MAGIC, ALU.mult, ALU.add, rr, rr)
            ts("dve", h, a, -0.5, None, ALU.mult, None, rr, rr)
            for _ in range(3):
                ts("dve", t, h, r, r, ALU.mult, ALU.mult, rr, rr)
                stt("dve", r, t, 1.5, r, ALU.add, ALU.mult, rr, rr)

        def cp(eng, out, in_, reads, writes):
            if eng == "act":
                return sch.add("act", lambda e: e.copy(out=out, in_=in_), reads, writes)
            return sch.add(eng, lambda e: e.tensor_copy(out=out, in_=in_), reads, writes)

        def dma(q, out, in_, reads, writes, own=None, grp=None):
            return sch.dma(q, lambda e: e.dma_start(out=out, in_=in_), reads, writes, own=own, grp=grp)

        def uTv(k, a, b):
            return uT[:, k * NE + a:k * NE + b]

        def mixv(e, a, b):
            return mixT[:, e * NM + a:e * NM + b]

        def wv(slot, k):
            return wring[:, slot * 1024 + k * 128:slot * 1024 + (k + 1) * 128]

        wseq = []
        for s in range(nseg):
            for j in range(8):
                wseq += [8 + j, j]
            for j in range(8):
                wseq += [16 + j]
            while len(wseq) % 4:
                wseq.append(None)
            wseq += [40, 41, 42, 43, 44, 45, 46, 47]
            for hp in range(8):
                wseq += [24 + hp, 32 + hp, 48 + hp]
        wpre = [32 + hp for hp in range(8)] + [40, 41, 42, 43, 44, 45, 46, 47]
        wseq = wpre + wseq
        wstate = {"loaded": 0, "used": 0}

        def w_load_upto(n):
            while wstate["loaded"] < min(n, len(wseq)):
                i = wstate["loaded"]
                e = wseq[i]
                if e is not None:
                    slot = i % WRING
                    dma("sp", wring[:, slot * 1024:(slot + 1) * 1024], win_bf[e],
                        [r_scr], [r_w[slot]], own=r_w[slot])
                wstate["loaded"] += 1

        def w_next(expect):
            i = wstate["used"]
            while wseq[i] is None:
                i += 1
            assert wseq[i] == expect, (i, wseq[i], expect)
            w_load_upto(i + 1)
            wstate["used"] = i + 1
            return i % WRING

        def w_prefetch(k=4):
            assert k <= WRING
            w_load_upto(wstate["used"] + k)

        class StopBuild(Exception):
            pass

        def stage(k):
            if STAGE < k:
                raise StopBuild()

        try:
            for e in range(int(os.environ.get('KNW', '56'))):
                dma("pool", win_bf[e], win[e], [], [r_scr], grp=ginit)
            for q in range(int(os.environ.get('KNO', '16'))):
                dma("pool", wout_bf[:, q * 1024:(q + 1) * 1024], wout[:, q * 1024:(q + 1) * 1024], [], [r_scr], grp=ginit)
            stage(-3)
            dma("sp", gpre_s[:], gpre[:], [], [r_const], grp=gconst)
            dma("sp", gpost_s[:], gpost[:], [], [r_const], grp=gconst)
            dma("sp", chv[:], chvec[:], [], [r_const], grp=gconst)
            dma("sp", cwT[:], convwT[:], [], [r_const], grp=gconst)
            dma("pool", ident_s[:], identf[:], [], [r_const], grp=gconst)
            Mt_s = sb("Mt_s", [128, 1024], F32)
            dma("sp", Mt_s[:], Mt[:], [], [r_const], grp=gconst)
            sch.add("dve", lambda e: e.memset(ones_s[:], 1.0 / 1024.0), [], [r_const2])
            sch.add("dve", lambda e: e.memset(vmA[:], 1.0), [], [r_meta])
            sch.add("dve", lambda e: e.memset(vmB[:], 1.0), [], [r_meta])
            ts("dve", halfg[:], chv[:, 8:16], 0.5, None, ALU.mult, None, [r_const], [r_const2])
            ts("dve", halfb[:], chv[:, 16:24], 0.5, None, ALU.mult, None, [r_const], [r_const2])

            stage(-2)
            for hp in range(8):
                dma("sp", btmp.ap, Bt[:, hp * 2048:(hp + 1) * 2048], [], btmp.res, own=r_init)
                act(btmp2.ap, btmp.ap, AF.Exp, btmp.res, btmp2.res)
                for hh in range(2):
                    tt("dve", tbuild.ap[:, hh * 1024:(hh + 1) * 1024], btmp2.ap[:, hh * 1024:(hh + 1) * 1024],
                       Mt_s[:], ALU.mult, btmp2.res + [r_const], tbuild.res)
                dma("sp", T_bf[hp], tbuild.ap, tbuild.res, [r_Tscr[hp]], own=r_tb)
            stage(-1)
            for j in range(8):
                for s in range(CONVW):
                    eng = "dve" if (s % 2 == 0) else "pool"
                    ts(eng, dbuild.ap[:, s * 128:(s + 1) * 128], ident_s[:],
                       cwT[:, j * CONVW + s:j * CONVW + s + 1], 0.5, ALU.mult, ALU.mult,
                       [r_const], dbuild.res)
                dma("sp", diag_bf[j], dbuild.ap, dbuild.res, [r_dscr[j]], own=r_db)

            cnt = {"x": 0, "st": 0, "cpy": 0, "pj": 0}

            def p0_tile(src_ap, nrows, dst_fn, dst_res, xbuf=None, xres_=None):
                i = cnt["x"] % 2
                cnt["x"] += 1
                si = cnt["st"] % 4
                cnt["st"] += 1
                if xbuf is None:
                    xb, xr = xt[i][0:nrows, :], [r_xt[i]]
                    dma("sp", xb, src_ap, [], xr, own=r_xt[i])
                else:
                    xb, xr = xbuf, xres_
                ub = ubf[i][0:nrows, :]
                st = stat[si]
                act(ub, xb, AF.Square, xr, [r_ubf[i], r_stat[si]], accum_out=st[0:nrows, 0:1])
                ts("dve", st[0:nrows, 1:2], st[0:nrows, 0:1], 1.0 / D, 1e-6, ALU.mult, ALU.add, [r_stat[si]], [r_stat[si]])
                rsqrt_col(st, nrows, [r_stat[si]])
                stt("dve", ub, xb, st[0:nrows, 2:3], gpre_s[0:nrows, :], ALU.mult, ALU.mult,
                    xr + [r_stat[si], r_const], [r_ubf[i]])
                for k in range(8):
                    sch.add("pe", lambda e, k=k: e.transpose(out=TB[:, k * 128:k * 128 + nrows],
                                                               in_=ubf[i][0:nrows, k * 128:(k + 1) * 128],
                                                               identity=ident_s[0:nrows, 0:nrows]),
                            [r_ubf[i], r_const], [rTB])
                ceng = "act" if cnt["cpy"] % 2 == 0 else "dve"
                cnt["cpy"] += 1
                src = TB[:].rearrange("p (k t) -> p k t", t=128)[:, :, 0:nrows]
                cp(ceng, dst_fn(), src, [rTB], dst_res)

            stage(1)
            ssb = sb("ssb", [128, 64], F32)
            r_ssb = Res("ssb")

            def p0_a_tiles(sgi, tiles):
                for t in tiles:
                    i = cnt["x"] % 2
                    cnt["x"] += 1
                    dma("sp", xt[i][:], xe[sgi, t * 128:(t + 1) * 128, :], [], [r_xt[i]], own=r_xt[i])
                    act(ubf[i][:], xt[i][:], AF.Square, [r_xt[i]], [r_ubf[i], r_ssb], accum_out=ssb[:, t:t + 1])

            def p0_chain():
                a_, r_, h_, t_ = ssb[:, 16:28], ssb[:, 32:44], ssb[:, 48:60], ssb[:, 0:12]
                rr = [r_ssb]
                ts("dve", a_, ssb[:, 0:12], 1.0 / D, 1e-6, ALU.mult, ALU.add, rr, rr)
                ri = r_.bitcast(I32)
                ts("dve", ri, a_.bitcast(I32), 1, None, ALU.arith_shift_right, None, rr, rr)
                ts("dve", ri, ri, -1, MAGIC, ALU.mult, ALU.add, rr, rr)
                ts("dve", h_, a_, -0.5, None, ALU.mult, None, rr, rr)
                for _ in range(3):
                    tt("dve", t_, h_, r_, ALU.mult, rr, rr)
                    tt("dve", t_, t_, r_, ALU.mult, rr, rr)
                    stt("dve", r_, t_, 1.5, r_, ALU.add, ALU.mult, rr, rr)

            def p0_c(sgi, tiles):
                for t in tiles:
                    i = cnt["x"] % 2
                    cnt["x"] += 1
                    dma("sp", xt[i][:], xe[sgi, t * 128:(t + 1) * 128, :], [], [r_xt[i]], own=r_xt[i])
                    stt("dve", ubf[i][:], xt[i][:], ssb[:, 32 + t:33 + t], gpre_s[:], ALU.mult, ALU.mult,
                        [r_xt[i], r_ssb, r_const], [r_ubf[i]])
                    if t % 2 == 0:
                        tbank, tres = TB[:], [rTB]
                    else:
                        tbank, tres = PJ[:].bitcast(BF16), [rPJa, rPJb]
                    for k in range(8):
                        sch.add("pe", lambda e, k=k, i=i, tbank=tbank: e.transpose(out=tbank[:, k * 128:(k + 1) * 128],
                                                                                    in_=ubf[i][:, k * 128:(k + 1) * 128],
                                                                                    identity=ident_s[:, :]),
                                [r_ubf[i], r_const], tres)
                    ceng = "act" if t % 2 == 0 else "dve"
                    src = tbank.rearrange("p (k t) -> p k t", t=128)
                    dst = uT[:].rearrange("p (k n) -> p k n", n=NE)[:, :, t * 128:(t + 1) * 128]
                    cp(ceng, dst, src, tres, [r_uT[t]])

            def p0_batched(sgi):
                p0_a_tiles(sgi, range(NTILE))
                p0_chain()
                p0_c(sgi, range(NTILE))

            dma("sp", xmeta.ap, meta[:], [], xmeta.res, own=r_init)
            p0_tile(None, NMETA, lambda: uTm[:].rearrange("p (k t) -> p k t", t=16), [r_meta],
                    xbuf=xmeta.ap, xres_=xmeta.res)
            for hp in range(8):
                slot = w_next(32 + hp)
                for k in range(8):
                    mm(PJ[:, 0:16], wv(slot, k), uTm[:, k * 16:(k + 1) * 16], k == 0, k == 7,
                       [r_w[slot], r_meta], [rPJ])
                cp("dve", kmeta[0:64, hp * 32:hp * 32 + 16], PJ[0:64, 0:16], [rPJ], [r_meta])
                cp("dve", kmeta[0:64, hp * 32 + 16:hp * 32 + 32], PJ[64:128, 0:16], [rPJ], [r_meta])
                w_prefetch(4)
            for g in range(2):
                slots = [w_next(40 + 4 * g + q) for q in range(4)]
                assert slots[0] % 4 == 0
                for k in range(8):
                    rhs = wring[:, slots[0] * 1024 + k * 512:slots[0] * 1024 + (k + 1) * 512]
                    mm(PJ[0:16, 0:512], uTm[:, k * 16:(k + 1) * 16], rhs, k == 0, k == 7,
                       [r_w[s_] for s_ in slots] + [r_meta], [rPJ])
                if g == 0:
                    cp("dve", vmA[0:16, :].rearrange("p (i c) -> p i c", c=128)[:, :, 0:64],
                       PJ[0:16, 0:512].rearrange("p (i c) -> p i c", c=64), [rPJ], [r_meta])
                else:
                    cp("dve", vmB[0:16, :].rearrange("p (i c) -> p i c", c=128)[:, :, 64:128],
                       PJ[0:16, 0:512].rearrange("p (i c) -> p i c", c=64), [rPJ], [r_meta])

            rot = {"pair": 0, "S": 0, "ex": 0, "out": 0, "tmp": 0, "tf": 0, "dg": 0, "ug": 0, "T": 0, "mask": 0}

            for sgi in range(nseg):
                mi = rot["mask"] % 2
                rot["mask"] += 1

                stage(2)
                if sgi == 0:
                    p0_batched(sgi)
                else:
                    p0_c(sgi, range(NTILE))
                nxt = sgi + 1 if sgi + 1 < nseg else None
                all_uT = list(r_uT)
                if DBG and sgi == nseg - 1:
                    dma("sp", dbg_uT[:, :], uT[:, :], all_uT, [], own=r_init)
                    dma("sp", dbg_w[:, :], win_bf[8], [r_scr], [], own=r_init)

                stage(3)
                ug_groups = [(M0 - PAD, 512), (M0 - PAD + 512, 512), (M0 - PAD + 1024, 2 * PAD)]
                w_prefetch(4)
                for j in range(8):
                    ui = rot["ug"] % 2
                    rot["ug"] += 1
                    sg_ = w_next(8 + j)
                    sv_ = w_next(j)
                    di = rot["dg"] % 2
                    rot["dg"] += 1
                    dma("sp", diag[di].ap, diag_bf[j], [r_dscr[j]], diag[di].res + [r_diag[di]], own=r_diag[di])
                    for gi, (e0, n) in enumerate(ug_groups):
                        bg, rg = banks[(2 * gi) % 4]
                        bv, rv = banks[(2 * gi + 1) % 4]
                        for k in range(8):
                            mm(bg[:, 0:n], wv(sg_, k), uTv(k, e0, e0 + n), k == 0, k == 7, [r_w[sg_]] + all_uT, [rg])
                        for k in range(8):
                            mm(bv[:, 0:n], wv(sv_, k), uTv(k, e0, e0 + n), k == 0, k == 7, [r_w[sv_]] + all_uT, [rv])
                        tb = tmpb[rot["tmp"] % 8]
                        rot["tmp"] += 1
                        act(tb.ap[:, 0:n], bg[:, 0:n], AF.Tanh, [rg], tb.res, scale=0.5)
                        o0 = e0 - (M0 - PAD)
                        stt("dve", ugT[ui].ap[:, o0:o0 + n], tb.ap[:, 0:n], 1.0, bv[:, 0:n], ALU.add, ALU.mult,
                            tb.res + [rv], ugT[ui].res)
                    for g in range(2):
                        by, ry = Obuf[g]
                        for s in range(CONVW):
                            mm(by[:, :], diag[di].ap[:, s * 128:(s + 1) * 128],
                               ugT[ui].ap[:, g * 512 + s:g * 512 + s + 512], s == 0, s == CONVW - 1,
                               diag[di].res + [r_diag[di]] + ugT[ui].res, [ry])
                        act(mixv(j, g * 512, (g + 1) * 512), by[:, :], AF.Identity, [ry, r_const], [r_mix[j][g]],
                            bias=chv[:, j:j + 1])
                    if g == 1:
                        w_prefetch(4)
                        if nxt is not None and j < 6:
                            p0_a_tiles(nxt, [2 * j, 2 * j + 1])
                stage(4)
                for g in range(2):
                    bm, rm = banks[0]
                    bq, rq = banks[1]
                    for j in range(8):
                        tb = tmpb[rot["tmp"] % 8]
                        rot["tmp"] += 1
                        act(tb.ap, mixv(j, g * 512, (g + 1) * 512), AF.Square, [r_mix[j][g]], tb.res)
                        mm(bm, ones_s[:], mixv(j, g * 512, (g + 1) * 512), j == 0, j == 7, [r_const2, r_mix[j][g]], [rm])
                        mm(bq, ones_s[:], tb.ap, j == 0, j == 7, [r_const2] + tb.res, [rq])
                    A_ = stA.ap[:, g * 512:(g + 1) * 512]
                    B_ = stB.ap[:, g * 512:(g + 1) * 512]
                    t0 = tf[0]
                    act(t0.ap, bm, AF.Square, [rm], t0.res)
                    tt("dve", t0.ap, bq, t0.ap, ALU.subtract, [rq] + t0.res, t0.res)
                    ts("dve", t0.ap, t0.ap, 1e-5, None, ALU.add, None, t0.res, t0.res)
                    rsqrt(t0.ap, A_, tf[1].ap, t0.res, stA.res, tf[1].res)
                    stt("dve", B_, bm, -1.0, A_, ALU.mult, ALU.mult, [rm] + stA.res, stB.res)
                stage(5)
                if nxt is not None:
                    p0_chain()
                w_prefetch(4)
                for j in range(8):
                    sc_ = w_next(16 + j)
                    for g in range(2):
                        bc, rc = banks[2 + g]
                        for k in range(8):
                            mm(bc, wv(sc_, k), uTv(k, M0 + g * 512, M0 + (g + 1) * 512), k == 0, k == 7,
                               [r_w[sc_]] + all_uT, [rc])
                        tb = tmpb[rot["tmp"] % 8]
                        rot["tmp"] += 1
                        act(tb.ap, bc, AF.Tanh, [rc], tb.res, scale=0.5)
                        stt("dve", gc2[j].ap[:, g * 512:(g + 1) * 512], tb.ap, 1.0, bc, ALU.add, ALU.mult,
                            tb.res + [rc], gc2[j].res)
                    w_prefetch(4)

                def ln_apply(j, g):
                    A_ = stA.ap[:, g * 512:(g + 1) * 512]
                    B_ = stB.ap[:, g * 512:(g + 1) * 512]
                    t1 = tf[1 + (rot["tf"] % 2)]
                    rot["tf"] += 1
                    mv = mixv(j, g * 512, (g + 1) * 512)
                    tt("pool", t1.ap, mv, A_, ALU.mult, [r_mix[j][g]] + stA.res, t1.res)
                    tt("pool", t1.ap, t1.ap, B_, ALU.add, t1.res + stB.res, t1.res)
                    tb2 = tmpb[rot["tmp"] % 8]
                    rot["tmp"] += 1
                    tb3 = tmpb[rot["tmp"] % 8]
                    rot["tmp"] += 1
                    act(tb2.ap, t1.ap, AF.Tanh, t1.res + [r_const2], tb2.res,
                        scale=halfg[:, j:j + 1], bias=halfb[:, j:j + 1])
                    act(tb3.ap, t1.ap, AF.Identity, t1.res + [r_const], tb3.res,
                        scale=chv[:, 8 + j:9 + j], bias=chv[:, 16 + j:17 + j])
                    stt("dve", t1.ap, tb2.ap, 1.0, tb3.ap, ALU.add, ALU.mult, tb2.res + tb3.res, t1.res)
                    stt("dve", mv, t1.ap, 0.25, gc2[j].ap[:, g * 512:(g + 1) * 512], ALU.mult, ALU.mult,
                        t1.res + gc2[j].res, [r_mix[j][g]])

                stage(6)
                for t in range(NTILE):
                    sch.add("pool", lambda e, t=t: e.memset(
                        Vt[t].ap.rearrange("p (i c) -> p i c", c=192)[:, :, 64:128], 1.0), [], Vt[t].res)
                vsl = [[w_next(40 + 4 * g + q) for q in range(4)] for g in range(2)]
                assert vsl[0][0] % 4 == 0 and vsl[1][0] % 4 == 0
                lnq = [(j, g) for j in range(8) for g in range(2)]
                nv = 0
                for g in range(2):
                    s0 = vsl[g][0]
                    for t in range(NTILE):
                        bv, rv = banks[4 + (t % 2)]
                        for k in range(8):
                            rhs = wring[:, s0 * 1024 + k * 512:s0 * 1024 + (k + 1) * 512]
                            mm(bv, uTv(k, t * 128, (t + 1) * 128), rhs, k == 0, k == 7,
                               [r_w[s_] for s_ in vsl[g]] + [r_uT[t]], [rv])
                        v3 = Vt[t].ap.rearrange("p (i c) -> p i c", c=192)
                        dst = v3[:, :, 0:64] if g == 0 else v3[:, :, 128:192]
                        cp("act", dst, bv.rearrange("p (i c) -> p i c", c=64), [rv], Vt[t].res)
                        nv += 1
                        while lnq and (16 - len(lnq)) * 24 < nv * 16:
                            ln_apply(*lnq.pop(0))
                while lnq:
                    ln_apply(*lnq.pop(0))
                w_prefetch(4)

                O3 = [(O0[:, :], rO0), (O1[:, :], rO1), (TB[:].bitcast(F32), rTB)]

                def pair_tasks(hp):
                    pi = hp % 2
                    ti = hp % 2
                    bp, rp = banks[6]
                    st_ = {"n": 0}
                    tasks = []
                    rotb = [banks[6], banks[1], banks[3]]

                    def nextbank():
                        st_["n"] += 1
                        return rotb[st_["n"] % 3]

                    def setup():
                        st_["q"] = w_next(24 + hp)
                        st_["k"] = w_next(32 + hp)
                        st_["a"] = w_next(48 + hp)
                        dma("sp", Ttab[ti].ap, T_bf[hp], [r_Tscr[hp]], Ttab[ti].res + [r_T[ti]], own=r_T[ti])

                    def qgrp(g):
                        bp, rp = nextbank()
                        sq_ = st_["q"]
                        for k in range(8):
                            mm(bp, wv(sq_, k), uTv(k, M0 + g * 512, M0 + (g + 1) * 512), k == 0, k == 7,
                               [r_w[sq_]] + all_uT, [rp])
                        cp("act", QT[pi][0].ap[0:64, g * 512:(g + 1) * 512], bp[0:64, :], [rp], QT[pi][0].res)
                        cp("act", QT[pi][1].ap[0:64, g * 512:(g + 1) * 512], bp[64:128, :], [rp], QT[pi][1].res)

                    def kgrp(g):
                        bp, rp = nextbank()
                        sk_ = st_["k"]
                        for k in range(8):
                            mm(bp, wv(sk_, k), uTv(k, g * 512, (g + 1) * 512), k == 0, k == 7,
                               [r_w[sk_]] + all_uT, [rp])
                        cp("dve", KT[pi][0].ap[0:64, g * 512:(g + 1) * 512], bp[0:64, :], [rp], KT[pi][0].res)
                        cp("act", KT[pi][1].ap[0:64, g * 512:(g + 1) * 512], bp[64:128, :], [rp], KT[pi][1].res)

                    def agrp(g):
                        bp, rp = nextbank()
                        sa_ = st_["a"]
                        for k in range(8):
                            mm(bp, wv(sa_, k), uTv(k, M0 + g * 512, M0 + (g + 1) * 512), k == 0, k == 7,
                               [r_w[sa_]] + all_uT, [rp])
                        act(AG[pi].ap[:, g * 512:(g + 1) * 512], bp, AF.Tanh, [rp], AG[pi].res, scale=0.5)
                        stt("dve", AG[pi].ap[:, g * 512:(g + 1) * 512], AG[pi].ap[:, g * 512:(g + 1) * 512], 1.0, bp,
                            ALU.add, ALU.mult, AG[pi].res + [rp], AG[pi].res)
                        if g == 1:
                            w_prefetch(4)

                    def mgrp(hh):
                        pm = PmT[pi][hh]
                        for b in range(2):
                            mm(bp[0:16, :], kmeta[0:64, hp * 32 + hh * 16:hp * 32 + hh * 16 + 16],
                               QT[pi][hh].ap[0:64, b * 512:(b + 1) * 512], True, True,
                               [r_meta] + QT[pi][hh].res, [rp])
                            act(pm.ap[:, b * 512:(b + 1) * 512], bp[0:16, :], AF.Exp, [rp], pm.res, scale=0.125)

                    tasks.append(lambda: (setup(), qgrp(0)))
                    tasks.append(lambda: qgrp(1))
                    for g in range(3):
                        tasks.append(lambda g=g: kgrp(g))
                    for g in range(2):
                        tasks.append(lambda g=g: agrp(g))
                    for hh in range(2):
                        tasks.append(lambda hh=hh: mgrp(hh))
                    return tasks

                def pair_proj(hp):
                    for t_ in pair_tasks(hp):
                        t_()

                kind = ("lo" if sgi % 2 == 0 else "hi") if sgi < 8 else "gen"
                ptiles = {"gen": list(range(NTILE)), "lo": list(range(2, NTILE)), "hi": list(range(0, NTILE - 2))}[kind]
                steps = [(hp, hh, p) for hp in range(8) for hh in range(2) for p in ptiles]
                plast_b = {}
                for b_ in range(2):
                    plast_b[b_] = max(p_ for p_ in ptiles
                                      if tile_rows(p_, kind)[0] <= 8 * b_ + 7 and tile_rows(p_, kind)[1] >= 8 * b_)

                def step_geom(p):
                    rlo, rhi = tile_rows(p, kind)
                    n = (rhi - rlo + 1) * 64
                    chunks = []
                    c0 = 0
                    while c0 < n:
                        cn = min(512, n - c0)
                        chunks.append((c0, cn))
                        c0 += cn
                    return rlo, rhi, n, chunks

                def emit_qk(idx):
                    hp, hh, p = steps[idx]
                    pi = hp % 2
                    hb = 64 * hh
                    rlo, rhi, n, chunks = step_geom(p)
                    Sb, rS = Sbuf[idx % 2]
                    for _jk in range(NJUNK):
                        mm(Sb[:, 0:512], ident_s[:, :], uTv(0, 0, 512), True, True, [r_const] + all_uT, [rS[0]])
                    for ci, (c0, cn) in enumerate(chunks):
                        q0 = rlo * 64 + c0
                        mm(Sb[:, c0:c0 + cn], KT[pi][hh].ap[0:80, p * 128:(p + 1) * 128],
                           QT[pi][hh].ap[0:80, q0:q0 + cn], True, True,
                           KT[pi][hh].res + QT[pi][hh].res, [rS[ci]])

                def obank(hp, hh, b):
                    return O3[(2 * (2 * hp + hh) + b) % 3]

                def emit_expmult(idx):
                    hp, hh, p = steps[idx]
                    ti = hp % 2
                    rlo, rhi, n, chunks = step_geom(p)
                    Sb, rS = Sbuf[idx % 2]
                    xi = idx % 2
                    rSu = rS[0:len(chunks)]
                    act(expS[xi].ap[:, 0:n], Sb[:, 0:n], AF.Exp, rSu, expS[xi].res, scale=0.125)
                    slot0 = 11 - 2 * p + rlo
                    tab = Ttab[ti].ap[:, hh * 1024 + slot0 * 64:hh * 1024 + slot0 * 64 + n]
                    tt("dve", PT[xi].ap[:, 0:n], expS[xi].ap[:, 0:n], tab, ALU.mult,
                       expS[xi].res + Ttab[ti].res + [r_T[ti]], PT[xi].res)

                def emit_pv(idx):
                    hp, hh, p = steps[idx]
                    pi = hp % 2
                    hb = 64 * hh
                    pm = PmT[pi][hh]
                    rlo, rhi, n, chunks = step_geom(p)
                    xi = idx % 2
                    if p == ptiles[0]:
                        if hh == 0:
                            vmeta = vmA[0:16, hp * 128:(hp + 1) * 128]
                            pmrows = (0, 16)
                        else:
                            vmeta = vmB[0:16, hp * 128:(hp + 1) * 128]
                            pmrows = (0, 16)
                        for b in range(2):
                            ob, ro = obank(hp, hh, b)
                            while id(ro) in pend_bank:
                                pending.pop(0)()
                                pend_bank.pop(0)
                            mm(ob, vmeta, pm.ap[pmrows[0]:pmrows[1], b * 512:(b + 1) * 512], True, False,
                               [r_meta] + pm.res, [ro], skip_group_check=True)
                    vt = Vt[p].ap
                    lhs = vt[:, 192 * hp + 64 * hh:192 * hp + 64 * hh + 128]
                    for b in range(2):
                        lo = max(rlo, 8 * b)
                        hi = min(rhi, 8 * b + 7)
                        if lo > hi:
                            continue
                        ob, ro = obank(hp, hh, b)
                        mm(ob[:, (lo - 8 * b) * 64:(hi - 8 * b + 1) * 64], lhs,
                           PT[xi].ap[:, (lo - rlo) * 64:(hi - rlo + 1) * 64], False, True,
                           Vt[p].res + PT[xi].res, [ro], skip_group_check=True)
                    for b, plast in ((0, plast_b[0]), (1, plast_b[1])):
                        if p != plast:
                            continue
                        ob, ro = obank(hp, hh, b)
                        o_lo, o_hi = hb, hb + 64
                        d_lo, d_hi = 64 - hb, 128 - hb
                        R = Rb[b]
                        at = attb[b]
                        for c4 in range(4):
                            cs = slice(c4 * 128, (c4 + 1) * 128)
                            pend_bank.append(id(ro))
                            pending.append(lambda ob=ob, R=R, ro=ro, cs=cs, d_lo=d_lo, d_hi=d_hi, o_lo=o_lo, o_hi=o_hi:
                                           sch.add("dve", lambda e: e.reciprocal(out=R.ap[o_lo:o_hi, cs], in_=ob[d_lo:d_hi, cs]),
                                                   [ro], R.res))

                        def fin(ob=ob, ro=ro, R=R, at=at, o_lo=o_lo, o_hi=o_hi, hp=hp, b=b, pi=pi):
                            act(at.ap[o_lo:o_hi, :], ob[o_lo:o_hi, :], AF.Identity, [ro], at.res, scale=0.5)
                            tt("pool", at.ap[o_lo:o_hi, :], at.ap[o_lo:o_hi, :], R.ap[o_lo:o_hi, :], ALU.mult,
                               at.res + R.res, at.res)
                            tt("pool", mixT[o_lo:o_hi, (8 + hp) * NM + b * 512:(8 + hp) * NM + (b + 1) * 512],
                               at.ap[o_lo:o_hi, :], AG[pi].ap[o_lo:o_hi, b * 512:(b + 1) * 512], ALU.mult,
                               at.res + AG[pi].res, [r_mix[8 + hp][b]])
                        pend_bank.append(id(ro))
                        pending.append(fin)

                pending = []
                pend_bank = []
                projq = []
                for pi_ in range(2):
                    for hh_ in range(2):
                        kidx = pi_ * 2 + hh_
                        dma("pool", KT[pi_][hh_].ap[64:80, :], maskA[sgi], [], KT[pi_][hh_].res, own=r_mask[kidx])
                        dma("pool", QT[pi_][hh_].ap[64:80, :], conehot[:], [], QT[pi_][hh_].res, own=r_mask[4 + kidx])
                pair_proj(0)
                emit_qk(0)
                for idx in range(len(steps)):
                    hp, hh, p = steps[idx]
                    if hh == 1 and p == 2 and hp + 1 < 8:
                        pair_proj(hp + 1)
                    if idx + 1 < len(steps):
                        emit_qk(idx + 1)
                    emit_expmult(idx)
                    for _d in range(2 if len(pending) > 5 else 1):
                        if pending:
                            pending.pop(0)()
                            pend_bank.pop(0)
                    if idx >= 1:
                        emit_pv(idx - 1)
                emit_pv(len(steps) - 1)
                while pending:
                    pending.pop(0)()
                    pend_bank.pop(0)


                if DBG and sgi == nseg - 1:
                    allmix = [r_mix[e_][g_] for e_ in range(16) for g_ in range(2)]
                    dma("sp", dbg_mix[:, :], mixT[:, :], allmix, [], own=r_init)
                dma("sp", woutS.ap, wout_bf[:, :], [r_scr], woutS.res + [r_wout], own=r_wout)
                for t in range(NM // 128):
                    Sb, rS = Sbuf[rot["S"] % 2]
                    rot["S"] += 1
                    oi = rot["out"] % 2
                    rot["out"] += 1
                    g = t // 4
                    dma("sp", xres[oi].ap, xe[sgi, M0 + t * 128:M0 + (t + 1) * 128, :], [],
                        xres[oi].res + [r_xres[oi]], own=r_xres[oi])
                    for half in range(2):
                        for e_ in range(16):
                            mm(Sb[:, half * 512:(half + 1) * 512], mixv(e_, t * 128, (t + 1) * 128),
                               woutS.ap[:, e_ * 1024 + half * 512:e_ * 1024 + (half + 1) * 512], e_ == 0, e_ == 15,
                               [r_mix[e_][g]] + woutS.res + [r_wout], [rS[half]])
                    si = cnt["st"] % 4
                    cnt["st"] += 1
                    st = stat[si]
                    ob = outb[oi]
                    act(ob.ap, Sb[:, :], AF.Square, rS, ob.res + [r_out[oi], r_stat[si]], accum_out=st[:, 0:1])
                    ts("dve", st[:, 1:2], st[:, 0:1], 1.0 / D, 1e-6, ALU.mult, ALU.add, [r_stat[si]], [r_stat[si]])
                    rsqrt_col(st, 128, [r_stat[si]])
                    stt("dve", ob.ap, Sb[:, :], st[:, 2:3], gpost_s[:], ALU.mult, ALU.mult,
                        rS + [r_stat[si], r_const], ob.res + [r_out[oi]])
                    tt("pool", ob.ap, ob.ap, xres[oi].ap, ALU.add, ob.res + [r_out[oi]] + xres[oi].res + [r_xres[oi]],
                       ob.res + [r_out[oi]])
                    dma("sp", ye[sgi, t * 128:(t + 1) * 128, :], ob.ap, ob.res + [r_out[oi]], [], own=r_out[oi])

        except StopBuild:
            pass

        def final_waits():
            allr = r_out + r_xt + r_w + r_mask + r_T + r_diag + [r_wout, r_init, r_tb, r_db] + r_xres
            return [(r.sem, r.cnt) for r in allr if r.cnt > 0] + [(g.sem, g.cnt) for g in (ginit, gconst) if g.cnt > 0]

        sch.check_deadlock(esem, final_waits)
        with nc.Block() as block:
            sch.emit(block, esem, final_waits)
    return nc


def _col_tables():
    key = np.arange(GW)
    q = np.arange(GW)
    start = np.clip(q - 8, 0, GW - 16)
    off = key[:, None] - start[None, :]
    valid = (off >= 0) & (off < 16)
    rel = np.clip(key[:, None] - q[None, :] + 15, 0, 30)
    return valid, rel


def host_weights(w_in, w_out, conv_w, conv_b, ln_g, ln_b, rel_bias, pre_g, post_g, meta_tokens):
    w_in = np.asarray(w_in, np.float32)[0]
    cols = np.arange(7168)
    vperm = np.zeros(1024, np.int64)
    for n in range(1024):
        if n < 512:
            head = 2 * (n // 64)
        else:
            head = 2 * ((n - 512) // 64) + 1
        vperm[n] = 5120 + head * 64 + n % 64
    cols[5120:6144] = vperm
    wp = w_in[:, cols]
    win = np.ascontiguousarray(wp.reshape(8, 128, 56, 128).transpose(2, 1, 0, 3)).reshape(56, 128, 1024)
    for g in range(2):
        blk = wp[:, 5120 + g * 512:5120 + (g + 1) * 512].reshape(8, 128, 512).transpose(1, 0, 2)
        win[40 + 4 * g:44 + 4 * g] = blk.reshape(128, 4, 1024).transpose(1, 0, 2)
    wo = np.asarray(w_out, np.float32)[0]
    wout = np.ascontiguousarray(wo.reshape(16, 128, 1024).transpose(1, 0, 2)).reshape(128, 16 * 1024)
    cw = np.asarray(conv_w, np.float32)[0]
    convwT = np.ascontiguousarray(cw.reshape(CONVW, 8, 128).transpose(2, 1, 0)).reshape(128, 8 * CONVW)
    chvec = np.concatenate([np.asarray(v, np.float32)[0].reshape(8, 128).T for v in (conv_b, ln_g, ln_b)], axis=1)
    gpre = np.ascontiguousarray(np.broadcast_to(np.asarray(pre_g, np.float32)[0][None, :], (128, D)))
    gpost = np.ascontiguousarray(np.broadcast_to(np.asarray(post_g, np.float32)[0][None, :], (128, D)))
    rb = np.asarray(rel_bias, np.float32)[0]
    valid, rel = _col_tables()
    Bt = np.zeros((128, 16, 16, 64), np.float32)
    Mt = np.zeros((128, 16, 64), np.float32)
    for half in range(2):
        for slot in range(16):
            d = 7 - slot
            dr = d + half
            if abs(dr) > 7:
                continue
            Bt[half * 64:(half + 1) * 64, :, slot, :] = rb[:, dr + 7, :][:, rel].transpose(1, 0, 2)
            Mt[half * 64:(half + 1) * 64, slot, :] = valid.astype(np.float32)
    conehot = np.zeros((16, NM), np.float32)
    for r in range(16):
        conehot[r, r * 64:(r + 1) * 64] = 1.0
    return {
        "win": win, "wout": wout, "convwT": convwT, "chvec": np.ascontiguousarray(chvec),
        "gpre": gpre, "gpost": gpost, "meta": np.ascontiguousarray(np.asarray(meta_tokens, np.float32)),
        "Bt": Bt.reshape(128, -1), "Mt": Mt.reshape(128, -1), "conehot": conehot,
        "ident": np.eye(128, dtype=np.float32),
    }


def host_segment(xseq, meta_tokens, R0):
    T = xseq.shape[0]
    rows = T // GW
    xe = np.zeros((NE, D), np.float32)
    t0 = R0 * GW - M0
    lo = max(0, t0)
    hi = min(T, t0 + NE)
    xe[lo - t0:hi - t0] = xseq[lo:hi]
    if t0 < 0:
        xe[-t0 - NMETA:-t0] = meta_tokens
    mask = np.full((16, NE), NEG, np.float32)
    wr = min(8, rows)
    for r in range(16):
        R = R0 + r
        sr = int(np.clip(R - wr // 2, 0, rows - wr))
        for kr in range(sr, sr + wr):
            er = kr - R0 + HALO
            if 0 <= er < EXT_ROWS:
                mask[r, er * GW:(er + 1) * GW] = 0.0
    return xe, mask


def run_segments(seg_lists, wts, n_cores):
    nseg = len(seg_lists[0])
    nc = build_program(nseg)
    in_maps = []
    for c in range(n_cores):
        xe = np.zeros((nseg, NE, D), np.float32)
        mk = np.zeros((nseg, 16, NE), np.float32)
        for s, (xseq, R0) in enumerate(seg_lists[c]):
            xe[s], mk[s] = host_segment(xseq, wts["meta"], R0)
        m = dict(wts)
        m["xe"] = xe
        m["maskA"] = mk
        in_maps.append(m)
    res = run_bass_kernel_spmd(nc, in_maps, core_ids=list(range(n_cores)))
    if int(os.environ.get('KDBG', '0')):
        return [r["ye"] for r in res.results], [(r["dbg_mix"], r["dbg_uT"], r["dbg_w"]) for r in res.results]
    return [r["ye"] for r in res.results]


def kernel(x_prompt, x_sample, meta_tokens, pre_norm_g, w_in, conv_w, conv_b, conv_ln_g, conv_ln_b,
           rel_bias, post_norm_g, w_out):
    x_prompt = np.asarray(x_prompt, np.float32)
    x_sample = np.asarray(x_sample, np.float32)
    wts = host_weights(w_in, w_out, conv_w, conv_b, conv_ln_g, conv_ln_b, rel_bias, pre_norm_g, post_norm_g,
                       meta_tokens)
    seg_lists = []
    where = []
    for c in range(NCORES):
        segs = []
        wh = []
        for i in range(4):
            b = 4 * c + i
            for R0 in (0, 16):
                segs.append((x_prompt[b], R0))
                wh.append((0, b, R0))
        sb_ = c // 2
        for R0 in (32 * (c % 2), 32 * (c % 2) + 16):
            segs.append((x_sample[sb_], R0))
            wh.append((1, sb_, R0))
        seg_lists.append(segs)
        where.append(wh)
    outs = run_segments(seg_lists, wts, NCORES)
    y_prompt = np.empty_like(x_prompt)
    y_sample = np.empty_like(x_sample)
    for c in range(NCORES):
        for s, (which, b, R0) in enumerate(where[c]):
            dst = y_prompt if which == 0 else y_sample
            dst[b, R0 * GW:(R0 + SEG_ROWS) * GW] = outs[c][s]
    return (y_prompt, y_sample)
```

```python
import os
import numpy as np
import concourse.bass as bass
import concourse.mybir as mybir
from concourse.bass_utils import run_bass_kernel_spmd

F32 = mybir.dt.float32
BF16 = mybir.dt.bfloat16
I32 = mybir.dt.int32
MAGIC = 0x5f3759df
ALU = mybir.AluOpType
AF = mybir.ActivationFunctionType

D = 1024
NMETA = 16
GW = 64
SEG_ROWS = 16
HALO = 4
EXT_ROWS = SEG_ROWS + 2 * HALO
NE = EXT_ROWS * GW
NM = SEG_ROWS * GW
M0 = HALO * GW
NTILE = NE // 128
CONVW = 31
PAD = 15
NUG = NM + 2 * PAD
NEG = -30000.0
NCORES = 8
WRING = 8


class Res:
    __slots__ = ("name", "lw", "rd", "sem", "cnt")

    def __init__(self, name):
        self.name = name
        self.lw = None
        self.rd = []
        self.sem = None
        self.cnt = 0


class Group:
    def __init__(self, sem):
        self.sem = sem
        self.cnt = 0


class Op:
    __slots__ = ("eng", "fn", "deps", "signal", "val", "dsem", "dval", "grp", "xw")

    def __init__(self, eng, fn):
        self.eng = eng
        self.fn = fn
        self.deps = []
        self.signal = False
        self.val = 0
        self.dsem = None
        self.dval = 0
        self.grp = None
        self.xw = None


class Sched:
    ENGS = ("pe", "act", "dve", "pool", "sp")

    def __init__(self, nc):
        self.nc = nc
        self.ops = {e: [] for e in self.ENGS}
        self.esem = {}
        self.out_ops = []

    def _deps(self, op, reads, writes):
        deps = []
        for r in reads:
            if r.lw is not None:
                deps.append(r.lw)
        for w in writes:
            if w.lw is not None:
                deps.append(w.lw)
            deps.extend(w.rd)
        seen = set()
        for d in deps:
            if id(d) in seen or d is op:
                continue
            seen.add(id(d))
            if d.dsem is None and d.grp is None:
                if d.eng == "pe" and op.eng == "pe":
                    continue
                d.signal = True
            op.deps.append(d)
        for r in reads:
            r.rd.append(op)
        for w in writes:
            w.lw = op
            w.rd = []

    def add(self, eng, fn, reads=(), writes=()):
        op = Op(eng, fn)
        self._deps(op, reads, writes)
        self.ops[eng].append(op)
        return op

    def dma(self, q, fn, reads=(), writes=(), own=None, grp=None):
        op = Op(q, fn)
        self._deps(op, reads, writes)
        if own is not None:
            own.cnt += 16
            op.dsem = own.sem
            op.dval = own.cnt
        else:
            op.deps = [d for d in op.deps if d.grp is not grp]
            grp.cnt += 16
            op.grp = grp
        self.ops[q].append(op)
        return op

    def check_deadlock(self, sems, final_waits):
        for e in self.ENGS:
            c = 0
            for op in self.ops[e]:
                if op.dsem is None and op.grp is None and op.signal:
                    c += 1
                    op.val = c

        def ev(d):
            if d.dsem is not None:
                return id(d.dsem), d.dval
            if d.grp is not None:
                return id(d.grp.sem), d.grp.cnt
            return id(sems[d.eng]), d.val
        prog = {}
        for e in self.ENGS:
            lst = []
            for op in self.ops[e]:
                waits = [ev(d) for d in op.deps]
                if op.xw is not None:
                    waits.append((id(op.xw[0]), op.xw[1]))
                if op.dsem is not None:
                    inc = (id(op.dsem), 16)
                elif op.grp is not None:
                    inc = (id(op.grp.sem), 16)
                elif op.signal:
                    inc = (id(sems[e]), 1)
                else:
                    inc = None
                lst.append((waits, inc))
            if e == "sp":
                lst.append(([(id(s_), v) for s_, v in final_waits()], None))
            prog[e] = lst
        val = {}
        pc = {e: 0 for e in self.ENGS}
        progress = True
        while progress:
            progress = False
            for e in self.ENGS:
                while pc[e] < len(prog[e]):
                    waits, inc = prog[e][pc[e]]
                    if all(val.get(s_, 0) >= v for s_, v in waits):
                        if inc is not None:
                            val[inc[0]] = val.get(inc[0], 0) + inc[1]
                        pc[e] += 1
                        progress = True
                    else:
                        break
        stuck = {e: (pc[e], len(prog[e])) for e in self.ENGS if pc[e] < len(prog[e])}
        if stuck:
            names = {id(v): k for k, v in sems.items()}
            msg = []
            for e, (p, n) in stuck.items():
                waits, _ = prog[e][p]
                msg.append(f"{e} stuck at {p}/{n} waiting " + str([(names.get(s_, s_), v, val.get(s_, 0)) for s_, v in waits if val.get(s_, 0) < v]))
            raise RuntimeError("DEADLOCK in schedule: " + "; ".join(msg))

    def emit(self, block, sems, final_waits):
        nc = self.nc
        for e in self.ENGS:
            c = 0
            for op in self.ops[e]:
                if op.dsem is None and op.grp is None and op.signal:
                    c += 1
                    op.val = c
        esem = sems
        sched = self

        def ev(d):
            if d.dsem is not None:
                return d.dsem, d.dval
            if d.grp is not None:
                return d.grp.sem, d.grp.cnt
            return esem[d.eng], d.val

        def run(e, eng):
            known = {}
            for op in sched.ops[e]:
                for d in op.deps:
                    s, v = ev(d)
                    key = id(s)
                    if known.get(key, 0) < v:
                        eng.wait_ge(s, v)
                        known[key] = v
                if op.xw is not None:
                    eng.wait_ge(op.xw[0], op.xw[1])
                ins = op.fn(eng)
                if op.dsem is not None:
                    ins.then_inc(op.dsem, 16)
                elif op.grp is not None:
                    ins.then_inc(op.grp.sem, 16)
                elif op.signal:
                    ins.then_inc(esem[e], 1)
            if e == "sp":
                for s, v in final_waits():
                    eng.wait_ge(s, v)

        @block.tensor
        def _(eng):
            run("pe", eng)

        @block.scalar
        def _(eng):
            run("act", eng)

        @block.vector
        def _(eng):
            run("dve", eng)

        @block.gpsimd
        def _(eng):
            run("pool", eng)

        @block.sync
        def _(eng):
            run("sp", eng)


def tile_rows(p, kind="gen"):
    lo = max(0, 2 * p - 7)
    hi = min(SEG_ROWS - 1, 2 * p + 1)
    rows = set(range(lo, hi + 1)) if lo <= hi else set()
    if p <= 5 and kind in ("gen", "lo"):
        rows |= set(range(0, 4))
    if p >= 6 and kind in ("gen", "hi"):
        rows |= set(range(12, 16))
    rows = sorted(rows)
    assert rows == list(range(rows[0], rows[-1] + 1))
    return rows[0], rows[-1]


def build_program(nseg):
    NJUNK = int(os.environ.get('KJUNK', '0'))
    STAGE = int(os.environ.get('KSTAGE', '9'))
    nc = bass.Bass("TRN2", target_bir_lowering=False)

    def din(name, shape, dt=F32):
        return nc.dram_tensor(name, list(shape), dt, kind="ExternalInput").ap()

    xe = din("xe", [nseg, NE, D])
    maskA = din("maskA", [nseg, 16, NE])
    win = din("win", [56, 128, 1024])
    wout = din("wout", [128, 16 * 1024])
    convwT = din("convwT", [128, 8 * CONVW])
    chvec = din("chvec", [128, 24])
    gpre = din("gpre", [128, D])
    gpost = din("gpost", [128, D])
    meta = din("meta", [NMETA, D])
    Bt = din("Bt", [128, 16 * 16 * 64])
    Mt = din("Mt", [128, 16 * 64])
    conehot = din("conehot", [16, NM])
    identf = din("ident", [128, 128])
    ye = nc.dram_tensor("ye", [nseg, NM, D], F32, kind="ExternalOutput").ap()
    DBG = int(os.environ.get('KDBG', '0'))
    if DBG:
        dbg_mix = nc.dram_tensor("dbg_mix", [128, 16 * NM], BF16, kind="ExternalOutput").ap()
        dbg_uT = nc.dram_tensor("dbg_uT", [128, 8 * NE], BF16, kind="ExternalOutput").ap()
        dbg_w = nc.dram_tensor("dbg_w", [128, 1024], BF16, kind="ExternalOutput").ap()
    win_bf = nc.dram_tensor("win_bf", [56, 128, 1024], BF16, kind="Internal").ap()
    wout_bf = nc.dram_tensor("wout_bf", [128, 16 * 1024], BF16, kind="Internal").ap()
    T_bf = nc.dram_tensor("T_bf", [8, 128, 2048], BF16, kind="Internal").ap()
    diag_bf = nc.dram_tensor("diag_bf", [8, 128, CONVW * 128], BF16, kind="Internal").ap()

    import contextlib
    es = contextlib.ExitStack()
    with es:
        def sb(name, shape, dt):
            return es.enter_context(nc.sbuf_tensor(name, list(shape), dt))

        def ps(name, shape, dt):
            return es.enter_context(nc.psum_tensor(name, list(shape), dt))

        def sem(name):
            return es.enter_context(nc.semaphore(name))

        uT = sb("uT", [128, 8 * NE], BF16)
        mixT = sb("mixT", [128, 16 * NM], BF16)
        wring = sb("wring", [128, WRING * 1024], BF16)
        xt = [sb(f"xt{i}", [128, D], F32) for i in range(2)]
        ubf = [sb(f"ubf{i}", [128, D], BF16) for i in range(2)]
        gpre_s = sb("gpre_s", [128, D], F32)
        gpost_s = sb("gpost_s", [128, D], F32)
        ident_s = sb("ident_s", [128, 128], BF16)
        ones_s = sb("ones_s", [128, 128], BF16)
        kmeta = sb("kmeta", [64, 8 * 2 * 16], BF16)
        vmA = sb("vmA", [16, 8 * 128], BF16)
        vmB = sb("vmB", [16, 8 * 128], BF16)
        uTm = sb("uTm", [128, 8 * 16], BF16)
        chv = sb("chv", [128, 24], F32)
        cwT = sb("cwT", [128, 8 * CONVW], F32)
        halfg = sb("halfg", [128, 8], F32)
        halfb = sb("halfb", [128, 8], F32)
        stat = [sb(f"stat{i}", [128, 8], F32) for i in range(4)]
        A16N = 41984
        A32N = 4096
        ar16 = sb("ar16", [128, A16N], BF16)
        ar32 = sb("ar32", [128, A32N], F32)
        G16 = 512
        G32 = 256
        r16 = [Res(f"a16_{i}") for i in range((A16N + G16 - 1) // G16)]
        r32 = [Res(f"a32_{i}") for i in range((A32N + G32 - 1) // G32)]

        class Buf:
            def __init__(self, ap, res):
                self.ap = ap
                self.res = res

        def a16(off, n, parts=128):
            return Buf(ar16[0:parts, off:off + n], r16[off // G16:(off + n - 1) // G16 + 1])

        def a32(off, n, parts=128):
            return Buf(ar32[0:parts, off:off + n], r32[off // G32:(off + n - 1) // G32 + 1])

        UGP = 1056
        tmpb = [a16(i * 512, 512) for i in range(8)]
        UG0 = 4096 + NTILE * 1536
        gc2 = [a16(UG0 + j * 1024, 1024) for j in range(8)]
        ugT = [a16(UG0 + 8192 + i * UGP, UGP) for i in range(2)]
        DG0 = UG0 + 8192 + 2 * UGP
        diag = [a16(DG0 + i * 3968, 3968) for i in range(2)]
        assert DG0 + 2 * 3968 <= A16N
        stA = a32(0, 1024)
        stB = a32(1024, 1024)
        tf = [a32(2048 + i * 512, 512) for i in range(3)]
        Ttab = [a16(i * 2048, 2048) for i in range(2)]
        VW = 1536
        V0 = 4096
        Vt = [a16(V0 + t * VW, VW) for t in range(NTILE)]
        QK0 = V0 + NTILE * VW
        QKS = 2 * 1024 + 2 * NE + 1024
        QT = [[a16(QK0 + i * QKS + hh * 1024, 1024) for hh in range(2)] for i in range(2)]
        KT = [[a16(QK0 + i * QKS + 2048 + hh * NE, NE) for hh in range(2)] for i in range(2)]
        AG = [a16(QK0 + i * QKS + 2048 + 2 * NE, 1024) for i in range(2)]
        EX0 = QK0 + 2 * QKS
        expS = [a16(EX0 + i * 768, 768) for i in range(2)]
        PT = [a16(EX0 + 1536 + i * 768, 768) for i in range(2)]
        PM0 = EX0 + 3072
        PmT = [[a16(PM0 + i * 2048 + hh * 1024, 1024, parts=16) for hh in range(2)] for i in range(2)]
        assert PM0 + 4096 <= A16N, (PM0, A16N)
        Rb = [a32(i * 512, 512) for i in range(2)]
        attb = [a32(1024 + i * 512, 512) for i in range(2)]
        woutS = a16(0, 16384)
        outb = [a32(i * 1024, 1024) for i in range(2)]
        xres = [a32(2048 + i * 1024, 1024) for i in range(2)]
        btmp = a32(0, 2048)
        btmp2 = a32(2048, 2048)
        tbuild = a16(0, 2048)
        dbuild = a16(4096, 3968)
        xmeta = a32(0, 1024, parts=16)

        S0 = ps("S0", [128, 1024], F32)
        S1 = ps("S1", [128, 1024], F32)
        O0 = ps("O0", [128, 512], F32)
        O1 = ps("O1", [128, 512], F32)
        PJ = ps("PJ", [128, 512], F32)
        TB = ps("TB", [128, 1024], BF16)
        rS0a, rS0b, rS1a, rS1b, rO0, rO1, rPJ, rTB = [Res(n) for n in
                                                    ("S0a", "S0b", "S1a", "S1b", "O0", "O1", "PJ", "TB")]
        rPJa, rPJb = rPJ, Res("PJb")
        banks = [(S0[:, 0:512], rS0a), (S0[:, 512:1024], rS0b), (S1[:, 0:512], rS1a),
                 (S1[:, 512:1024], rS1b), (O0[:, :], rO0), (O1[:, :], rO1), (PJ[:, :], rPJ)]
        Sbuf = [(S0, [rS0a, rS0b]), (S1, [rS1a, rS1b])]
        Obuf = [(O0, rO0), (O1, rO1)]

        esem = {e: sem("s_" + e) for e in Sched.ENGS}
        sch = Sched(nc)
        ginit = Group(sem("g_init"))
        gconst = Group(sem("g_const"))

        def own(r, name):
            r.sem = sem(name)
            return r

        r_xt = [own(Res(f"xt{i}"), f"d_xt{i}") for i in range(2)]
        r_ubf = [Res(f"ubf{i}") for i in range(2)]
        r_w = [own(Res(f"w{i}"), f"d_w{i}") for i in range(WRING)]
        r_mask = [own(Res(f"mask{i}"), f"d_mask{i}") for i in range(8)]
        r_uT = [Res(f"uT{t}") for t in range(NTILE)]
        r_mix = [[Res(f"mix{e}_{g}") for g in range(2)] for e in range(16)]
        r_stat = [Res(f"stat{i}") for i in range(4)]
        r_const = Res("const")
        r_const2 = Res("const2")
        r_scr = Res("scratch")
        r_meta = Res("metabufs")
        r_T = [own(Res(f"T{i}"), f"d_T{i}") for i in range(2)]
        r_diag = [own(Res(f"dg{i}"), f"d_dg{i}") for i in range(2)]
        r_wout = own(Res("woutS"), "d_wout")
        r_xres = [own(Res(f"xres{i}"), f"d_xres{i}") for i in range(2)]
        r_out = [own(Res(f"out{i}"), f"d_out{i}") for i in range(2)]
        r_init = own(Res("initbuf"), "d_initbuf")
        r_tb = own(Res("tb"), "d_tb")
        r_db = own(Res("db"), "d_db")
        r_Tscr = [Res(f"Tscr{i}") for i in range(8)]
        r_dscr = [Res(f"dscr{i}") for i in range(8)]

        def mm(out, lhsT, rhs, start, stop, reads, writes, **kw):
            return sch.add("pe", lambda e: e.matmul(out, lhsT=lhsT, rhs=rhs, start=start, stop=stop, **kw),
                           reads, writes)

        def act(out, in_, func, reads, writes, **kw):
            return sch.add("act", lambda e: e.activation(out=out, in_=in_, func=func, **kw), reads, writes)

        def tt(eng, out, in0, in1, op, reads, writes):
            return sch.add(eng, lambda e: e.tensor_tensor(out=out, in0=in0, in1=in1, op=op), reads, writes)

        def ts(eng, out, in0, s1, s2, op0, op1, reads, writes):
            if s2 is None:
                return sch.add(eng, lambda e: e.tensor_scalar(out=out, in0=in0, scalar1=s1, scalar2=None, op0=op0),
                               reads, writes)
            return sch.add(eng, lambda e: e.tensor_scalar(out=out, in0=in0, scalar1=s1, scalar2=s2, op0=op0, op1=op1),
                           reads, writes)

        def stt(eng, out, in0, scalar, in1, op0, op1, reads, writes):
            return sch.add(eng, lambda e: e.scalar_tensor_tensor(out=out, in0=in0, scalar=scalar, in1=in1,
                                                                 op0=op0, op1=op1), reads, writes)

        def rsqrt(a, r, w, ra, rr, rw):
            ri = r.bitcast(I32)
            ts("dve", ri, a.bitcast(I32), 1, None, ALU.arith_shift_right, None, ra, rr)
            ts("dve", ri, ri, -1, MAGIC, ALU.mult, ALU.add, rr, rr)
            for _ in range(3):
                tt("dve", w, a, r, ALU.mult, ra + rr, rw)
                tt("dve", w, w, r, ALU.mult, rw + rr, rw)
                ts("dve", w, w, -0.5, 1.5, ALU.mult, ALU.add, rw, rw)
                tt("dve", r, r, w, ALU.mult, rr + rw, rr)

        def rsqrt_col(st, n, rr):
            a, r, h, t = st[0:n, 1:2], st[0:n, 2:3], st[0:n, 3:4], st[0:n, 4:5]
            ri = r.bitcast(I32)
            ts("dve", ri, a.bitcast(I32), 1, None, ALU.arith_shift_right, None, rr, rr)
            ts("dve", ri, ri, -1, MAGIC, ALU.mult, ALU.add, rr, rr)
            ts("dve", h, a, -0.5, None, ALU.mult, None, rr, rr)
            for _ in range(3):
                ts("dve", t, h, r, r, ALU.mult, ALU.mult, rr, rr)
                stt("dve", r, t, 1.5, r, ALU.add, ALU.mult, rr, rr)

        def cp(eng, out, in_, reads, writes):
            if eng == "act":
                return sch.add("act", lambda e: e.copy(out=out, in_=in_), reads, writes)
            return sch.add(eng, lambda e: e.tensor_copy(out=out, in_=in_), reads, writes)

        def dma(q, out, in_, reads, writes, own=None, grp=None):
            return sch.dma(q, lambda e: e.dma_start(out=out, in_=in_), reads, writes, own=own, grp=grp)

        def uTv(k, a, b):
            return uT[:, k * NE + a:k * NE + b]

        def mixv(e, a, b):
            return mixT[:, e * NM + a:e * NM + b]

        def wv(slot, k):
            return wring[:, slot * 1024 + k * 128:slot * 1024 + (k + 1) * 128]

        wseq = []
        for s in range(nseg):
            for j in range(8):
                wseq += [8 + j, j]
            for j in range(8):
                wseq += [16 + j]
            while len(wseq) % 4:
                wseq.append(None)
            wseq += [40, 41, 42, 43, 44, 45, 46, 47]
            for hp in range(8):
                wseq += [24 + hp, 32 + hp, 48 + hp]
        wpre = [32 + hp for hp in range(8)] + [40, 41, 42, 43, 44, 45, 46, 47]
        wseq = wpre + wseq
        wstate = {"loaded": 0, "used": 0}

        def w_load_upto(n):
            while wstate["loaded"] < min(n, len(wseq)):
                i = wstate["loaded"]
                e = wseq[i]
                if e is not None:
                    slot = i % WRING
                    dma("sp", wring[:, slot * 1024:(slot + 1) * 1024], win_bf[e],
                        [r_scr], [r_w[slot]], own=r_w[slot])
                wstate["loaded"] += 1

        def w_next(expect):
            i = wstate["used"]
            while wseq[i] is None:
                i += 1
            assert wseq[i] == expect, (i, wseq[i], expect)
            w_load_upto(i + 1)
            wstate["used"] = i + 1
            return i % WRING

        def w_prefetch(k=4):
            assert k <= WRING
            w_load_upto(wstate["used"] + k)

        class StopBuild(Exception):
            pass

        def stage(k):
            if STAGE < k:
                raise StopBuild()

        try:
            for e in range(int(os.environ.get('KNW', '56'))):
                dma("pool", win_bf[e], win[e], [], [r_scr], grp=ginit)
            for q in range(int(os.environ.get('KNO', '16'))):
                dma("pool", wout_bf[:, q * 1024:(q + 1) * 1024], wout[:, q * 1024:(q + 1) * 1024], [], [r_scr], grp=ginit)
            stage(-3)
            dma("sp", gpre_s[:], gpre[:], [], [r_const], grp=gconst)
            dma("sp", gpost_s[:], gpost[:], [], [r_const], grp=gconst)
            dma("sp", chv[:], chvec[:], [], [r_const], grp=gconst)
            dma("sp", cwT[:], convwT[:], [], [r_const], grp=gconst)
            dma("pool", ident_s[:], identf[:], [], [r_const], grp=gconst)
            Mt_s = sb("Mt_s", [128, 1024], F32)
            dma("sp", Mt_s[:], Mt[:], [], [r_const], grp=gconst)
            sch.add("dve", lambda e: e.memset(ones_s[:], 1.0 / 1024.0), [], [r_const2])
            sch.add("dve", lambda e: e.memset(vmA[:], 1.0), [], [r_meta])
            sch.add("dve", lambda e: e.memset(vmB[:], 1.0), [], [r_meta])
            ts("dve", halfg[:], chv[:, 8:16], 0.5, None, ALU.mult, None, [r_const], [r_const2])
            ts("dve", halfb[:], chv[:, 16:24], 0.5, None, ALU.mult, None, [r_const], [r_const2])

            stage(-2)
            for hp in range(8):
                dma("sp", btmp.ap, Bt[:, hp * 2048:(hp + 1) * 2048], [], btmp.res, own=r_init)
                act(btmp2.ap, btmp.ap, AF.Exp, btmp.res, btmp2.res)
                for hh in range(2):
                    tt("dve", tbuild.ap[:, hh * 1024:(hh + 1) * 1024], btmp2.ap[:, hh * 1024:(hh + 1) * 1024],
                       Mt_s[:], ALU.mult, btmp2.res + [r_const], tbuild.res)
                dma("sp", T_bf[hp], tbuild.ap, tbuild.res, [r_Tscr[hp]], own=r_tb)
            stage(-1)
            for j in range(8):
                for s in range(CONVW):
                    eng = "dve" if (s % 2 == 0) else "pool"
                    ts(eng, dbuild.ap[:, s * 128:(s + 1) * 128], ident_s[:],
                       cwT[:, j * CONVW + s:j * CONVW + s + 1], 0.5, ALU.mult, ALU.mult,
                       [r_const], dbuild.res)
                dma("sp", diag_bf[j], dbuild.ap, dbuild.res, [r_dscr[j]], own=r_db)

            cnt = {"x": 0, "st": 0, "cpy": 0, "pj": 0}

            def p0_tile(src_ap, nrows, dst_fn, dst_res, xbuf=None, xres_=None):
                i = cnt["x"] % 2
                cnt["x"] += 1
                si = cnt["st"] % 4
                cnt["st"] += 1
                if xbuf is None:
                    xb, xr = xt[i][0:nrows, :], [r_xt[i]]
                    dma("sp", xb, src_ap, [], xr, own=r_xt[i])
                else:
                    xb, xr = xbuf, xres_
                ub = ubf[i][0:nrows, :]
                st = stat[si]
                act(ub, xb, AF.Square, xr, [r_ubf[i], r_stat[si]], accum_out=st[0:nrows, 0:1])
                ts("dve", st[0:nrows, 1:2], st[0:nrows, 0:1], 1.0 / D, 1e-6, ALU.mult, ALU.add, [r_stat[si]], [r_stat[si]])
                rsqrt_col(st, nrows, [r_stat[si]])
                stt("dve", ub, xb, st[0:nrows, 2:3], gpre_s[0:nrows, :], ALU.mult, ALU.mult,
                    xr + [r_stat[si], r_const], [r_ubf[i]])
                for k in range(8):
                    sch.add("pe", lambda e, k=k: e.transpose(out=TB[:, k * 128:k * 128 + nrows],
                                                               in_=ubf[i][0:nrows, k * 128:(k + 1) * 128],
                                                               identity=ident_s[0:nrows, 0:nrows]),
                            [r_ubf[i], r_const], [rTB])
                ceng = "act" if cnt["cpy"] % 2 == 0 else "dve"
                cnt["cpy"] += 1
                src = TB[:].rearrange("p (k t) -> p k t", t=128)[:, :, 0:nrows]
                cp(ceng, dst_fn(), src, [rTB], dst_res)

            stage(1)
            ssb = sb("ssb", [128, 64], F32)
            r_ssb = Res("ssb")

            def p0_a_tiles(sgi, tiles):
                for t in tiles:
                    i = cnt["x"] % 2
                    cnt["x"] += 1
                    dma("sp", xt[i][:], xe[sgi, t * 128:(t + 1) * 128, :], [], [r_xt[i]], own=r_xt[i])
                    act(ubf[i][:], xt[i][:], AF.Square, [r_xt[i]], [r_ubf[i], r_ssb], accum_out=ssb[:, t:t + 1])

            def p0_chain():
                a_, r_, h_, t_ = ssb[:, 16:28], ssb[:, 32:44], ssb[:, 48:60], ssb[:, 0:12]
                rr = [r_ssb]
                ts("dve", a_, ssb[:, 0:12], 1.0 / D, 1e-6, ALU.mult, ALU.add, rr, rr)
                ri = r_.bitcast(I32)
                ts("dve", ri, a_.bitcast(I32), 1, None, ALU.arith_shift_right, None, rr, rr)
                ts("dve", ri, ri, -1, MAGIC, ALU.mult, ALU.add, rr, rr)
                ts("dve", h_, a_, -0.5, None, ALU.mult, None, rr, rr)
                for _ in range(3):
                    tt("dve", t_, h_, r_, ALU.mult, rr, rr)
                    tt("dve", t_, t_, r_, ALU.mult, rr, rr)
                    stt("dve", r_, t_, 1.5, r_, ALU.add, ALU.mult, rr, rr)

            def p0_c(sgi, tiles):
                for t in tiles:
                    i = cnt["x"] % 2
                    cnt["x"] += 1
                    dma("sp", xt[i][:], xe[sgi, t * 128:(t + 1) * 128, :], [], [r_xt[i]], own=r_xt[i])
                    stt("dve", ubf[i][:], xt[i][:], ssb[:, 32 + t:33 + t], gpre_s[:], ALU.mult, ALU.mult,
                        [r_xt[i], r_ssb, r_const], [r_ubf[i]])
                    if t % 2 == 0:
                        tbank, tres = TB[:], [rTB]
                    else:
                        tbank, tres = PJ[:].bitcast(BF16), [rPJa, rPJb]
                    for k in range(8):
                        sch.add("pe", lambda e, k=k, i=i, tbank=tbank: e.transpose(out=tbank[:, k * 128:(k + 1) * 128],
                                                                                    in_=ubf[i][:, k * 128:(k + 1) * 128],
                                                                                    identity=ident_s[:, :]),
                                [r_ubf[i], r_const], tres)
                    ceng = "act" if t % 2 == 0 else "dve"
                    src = tbank.rearrange("p (k t) -> p k t", t=128)
                    dst = uT[:].rearrange("p (k n) -> p k n", n=NE)[:, :, t * 128:(t + 1) * 128]
                    cp(ceng, dst, src, tres, [r_uT[t]])

            def p0_batched(sgi):
                p0_a_tiles(sgi, range(NTILE))
                p0_chain()
                p0_c(sgi, range(NTILE))

            dma("sp", xmeta.ap, meta[:], [], xmeta.res, own=r_init)
            p0_tile(None, NMETA, lambda: uTm[:].rearrange("p (k t) -> p k t", t=16), [r_meta],
                    xbuf=xmeta.ap, xres_=xmeta.res)
            for hp in range(8):
                slot = w_next(32 + hp)
                for k in range(8):
                    mm(PJ[:, 0:16], wv(slot, k), uTm[:, k * 16:(k + 1) * 16], k == 0, k == 7,
                       [r_w[slot], r_meta], [rPJ])
                cp("dve", kmeta[0:64, hp * 32:hp * 32 + 16], PJ[0:64, 0:16], [rPJ], [r_meta])
                cp("dve", kmeta[0:64, hp * 32 + 16:hp * 32 + 32], PJ[64:128, 0:16], [rPJ], [r_meta])
                w_prefetch(4)
            for g in range(2):
                slots = [w_next(40 + 4 * g + q) for q in range(4)]
                assert slots[0] % 4 == 0
                for k in range(8):
                    rhs = wring[:, slots[0] * 1024 + k * 512:slots[0] * 1024 + (k + 1) * 512]
                    mm(PJ[0:16, 0:512], uTm[:, k * 16:(k + 1) * 16], rhs, k == 0, k == 7,
                       [r_w[s_] for s_ in slots] + [r_meta], [rPJ])
                if g == 0:
                    cp("dve", vmA[0:16, :].rearrange("p (i c) -> p i c", c=128)[:, :, 0:64],
                       PJ[0:16, 0:512].rearrange("p (i c) -> p i c", c=64), [rPJ], [r_meta])
                else:
                    cp("dve", vmB[0:16, :].rearrange("p (i c) -> p i c", c=128)[:, :, 64:128],
                       PJ[0:16, 0:512].rearrange("p (i c) -> p i c", c=64), [rPJ], [r_meta])

            rot = {"pair": 0, "S": 0, "ex": 0, "out": 0, "tmp": 0, "tf": 0, "dg": 0, "ug": 0, "T": 0, "mask": 0}

            for sgi in range(nseg):
                mi = rot["mask"] % 2
                rot["mask"] += 1

                stage(2)
                if sgi == 0:
                    p0_batched(sgi)
                else:
                    p0_c(sgi, range(NTILE))
                nxt = sgi + 1 if sgi + 1 < nseg else None
                all_uT = list(r_uT)
                if DBG and sgi == nseg - 1:
                    dma("sp", dbg_uT[:, :], uT[:, :], all_uT, [], own=r_init)
                    dma("sp", dbg_w[:, :], win_bf[8], [r_scr], [], own=r_init)

                stage(3)
                ug_groups = [(M0 - PAD, 512), (M0 - PAD + 512, 512), (M0 - PAD + 1024, 2 * PAD)]
                w_prefetch(4)
                for j in range(8):
                    ui = rot["ug"] % 2
                    rot["ug"] += 1
                    sg_ = w_next(8 + j)
                    sv_ = w_next(j)
                    di = rot["dg"] % 2
                    rot["dg"] += 1
                    dma("sp", diag[di].ap, diag_bf[j], [r_dscr[j]], diag[di].res + [r_diag[di]], own=r_diag[di])
                    for gi, (e0, n) in enumerate(ug_groups):
                        bg, rg = banks[(2 * gi) % 4]
                        bv, rv = banks[(2 * gi + 1) % 4]
                        for k in range(8):
                            mm(bg[:, 0:n], wv(sg_, k), uTv(k, e0, e0 + n), k == 0, k == 7, [r_w[sg_]] + all_uT, [rg])
                        for k in range(8):
                            mm(bv[:, 0:n], wv(sv_, k), uTv(k, e0, e0 + n), k == 0, k == 7, [r_w[sv_]] + all_uT, [rv])
                        tb = tmpb[rot["tmp"] % 8]
                        rot["tmp"] += 1
                        act(tb.ap[:, 0:n], bg[:, 0:n], AF.Tanh, [rg], tb.res, scale=0.5)
                        o0 = e0 - (M0 - PAD)
                        stt("dve", ugT[ui].ap[:, o0:o0 + n], tb.ap[:, 0:n], 1.0, bv[:, 0:n], ALU.add, ALU.mult,
                            tb.res + [rv], ugT[ui].res)
                    for g in range(2):
                        by, ry = Obuf[g]
                        for s in range(CONVW):
                            mm(by[:, :], diag[di].ap[:, s * 128:(s + 1) * 128],
                               ugT[ui].ap[:, g * 512 + s:g * 512 + s + 512], s == 0, s == CONVW - 1,
                               diag[di].res + [r_diag[di]] + ugT[ui].res, [ry])
                        act(mixv(j, g * 512, (g + 1) * 512), by[:, :], AF.Identity, [ry, r_const], [r_mix[j][g]],
                            bias=chv[:, j:j + 1])
                    if g == 1:
                        w_prefetch(4)
                        if nxt is not None and j < 6:
                            p0_a_tiles(nxt, [2 * j, 2 * j + 1])
                stage(4)
                for g in range(2):
                    bm, rm = banks[0]
                    bq, rq = banks[1]
                    for j in range(8):
                        tb = tmpb[rot["tmp"] % 8]
                        rot["tmp"] += 1
                        act(tb.ap, mixv(j, g * 512, (g + 1) * 512), AF.Square, [r_mix[j][g]], tb.res)
                        mm(bm, ones_s[:], mixv(j, g * 512, (g + 1) * 512), j == 0, j == 7, [r_const2, r_mix[j][g]], [rm])
                        mm(bq, ones_s[:], tb.ap, j == 0, j == 7, [r_const2] + tb.res, [rq])
                    A_ = stA.ap[:, g * 512:(g + 1) * 512]
                    B_ = stB.ap[:, g * 512:(g + 1) * 512]
                    t0 = tf[0]
                    act(t0.ap, bm, AF.Square, [rm], t0.res)
                    tt("dve", t0.ap, bq, t0.ap, ALU.subtract, [rq] + t0.res, t0.res)
                    ts("dve", t0.ap, t0.ap, 1e-5, None, ALU.add, None, t0.res, t0.res)
                    rsqrt(t0.ap, A_, tf[1].ap, t0.res, stA.res, tf[1].res)
                    stt("dve", B_, bm, -1.0, A_, ALU.mult, ALU.mult, [rm] + stA.res, stB.res)
                stage(5)
                if nxt is not None:
                    p0_chain()
                w_prefetch(4)
                for j in range(8):
                    sc_ = w_next(16 + j)
                    for g in range(2):
                        bc, rc = banks[2 + g]
                        for k in range(8):
                            mm(bc, wv(sc_, k), uTv(k, M0 + g * 512, M0 + (g + 1) * 512), k == 0, k == 7,
                               [r_w[sc_]] + all_uT, [rc])
                        tb = tmpb[rot["tmp"] % 8]
                        rot["tmp"] += 1
                        act(tb.ap, bc, AF.Tanh, [rc], tb.res, scale=0.5)
                        stt("dve", gc2[j].ap[:, g * 512:(g + 1) * 512], tb.ap, 1.0, bc, ALU.add, ALU.mult,
                            tb.res + [rc], gc2[j].res)
                    w_prefetch(4)

                def ln_apply(j, g):
                    A_ = stA.ap[:, g * 512:(g + 1) * 512]
                    B_ = stB.ap[:, g * 512:(g + 1) * 512]
                    t1 = tf[1 + (rot["tf"] % 2)]
                    rot["tf"] += 1
                    mv = mixv(j, g * 512, (g + 1) * 512)
                    tt("pool", t1.ap, mv, A_, ALU.mult, [r_mix[j][g]] + stA.res, t1.res)
                    tt("pool", t1.ap, t1.ap, B_, ALU.add, t1.res + stB.res, t1.res)
                    tb2 = tmpb[rot["tmp"] % 8]
                    rot["tmp"] += 1
                    tb3 = tmpb[rot["tmp"] % 8]
                    rot["tmp"] += 1
                    act(tb2.ap, t1.ap, AF.Tanh, t1.res + [r_const2], tb2.res,
                        scale=halfg[:, j:j + 1], bias=halfb[:, j:j + 1])
                    act(tb3.ap, t1.ap, AF.Identity, t1.res + [r_const], tb3.res,
                        scale=chv[:, 8 + j:9 + j], bias=chv[:, 16 + j:17 + j])
                    stt("dve", t1.ap, tb2.ap, 1.0, tb3.ap, ALU.add, ALU.mult, tb2.res + tb3.res, t1.res)
                    stt("dve", mv, t1.ap, 0.25, gc2[j].ap[:, g * 512:(g + 1) * 512], ALU.mult, ALU.mult,
                        t1.res + gc2[j].res, [r_mix[j][g]])

                stage(6)
                for t in range(NTILE):
                    sch.add("pool", lambda e, t=t: e.memset(
                        Vt[t].ap.rearrange("p (i c) -> p i c", c=192)[:, :, 64:128], 1.0), [], Vt[t].res)
                vsl = [[w_next(40 + 4 * g + q) for q in range(4)] for g in range(2)]
                assert vsl[0][0] % 4 == 0 and vsl[1][0] % 4 == 0
                lnq = [(j, g) for j in range(8) for g in range(2)]
                nv = 0
                for g in range(2):
                    s0 = vsl[g][0]
                    for t in range(NTILE):
                        bv, rv = banks[4 + (t % 2)]
                        for k in range(8):
                            rhs = wring[:, s0 * 1024 + k * 512:s0 * 1024 + (k + 1) * 512]
                            mm(bv, uTv(k, t * 128, (t + 1) * 128), rhs, k == 0, k == 7,
                               [r_w[s_] for s_ in vsl[g]] + [r_uT[t]], [rv])
                        v3 = Vt[t].ap.rearrange("p (i c) -> p i c", c=192)
                        dst = v3[:, :, 0:64] if g == 0 else v3[:, :, 128:192]
                        cp("act", dst, bv.rearrange("p (i c) -> p i c", c=64), [rv], Vt[t].res)
                        nv += 1
                        while lnq and (16 - len(lnq)) * 24 < nv * 16:
                            ln_apply(*lnq.pop(0))
                while lnq:
                    ln_apply(*lnq.pop(0))
                w_prefetch(4)

                O3 = [(O0[:, :], rO0), (O1[:, :], rO1), (TB[:].bitcast(F32), rTB)]

                def pair_tasks(hp):
                    pi = hp % 2
                    ti = hp % 2
                    bp, rp = banks[6]
                    st_ = {"n": 0}
                    tasks = []
                    rotb = [banks[6], banks[1], banks[3]]

                    def nextbank():
                        st_["n"] += 1
                        return rotb[st_["n"] % 3]

                    def setup():
                        st_["q"] = w_next(24 + hp)
                        st_["k"] = w_next(32 + hp)
                        st_["a"] = w_next(48 + hp)
                        dma("sp", Ttab[ti].ap, T_bf[hp], [r_Tscr[hp]], Ttab[ti].res + [r_T[ti]], own=r_T[ti])

                    def qgrp(g):
                        bp, rp = nextbank()
                        sq_ = st_["q"]
                        for k in range(8):
                            mm(bp, wv(sq_, k), uTv(k, M0 + g * 512, M0 + (g + 1) * 512), k == 0, k == 7,
                               [r_w[sq_]] + all_uT, [rp])
                        cp("act", QT[pi][0].ap[0:64, g * 512:(g + 1) * 512], bp[0:64, :], [rp], QT[pi][0].res)
                        cp("act", QT[pi][1].ap[0:64, g * 512:(g + 1) * 512], bp[64:128, :], [rp], QT[pi][1].res)

                    def kgrp(g):
                        bp, rp = nextbank()
                        sk_ = st_["k"]
                        for k in range(8):
                            mm(bp, wv(sk_, k), uTv(k, g * 512, (g + 1) * 512), k == 0, k == 7,
                               [r_w[sk_]] + all_uT, [rp])
                        cp("dve", KT[pi][0].ap[0:64, g * 512:(g + 1) * 512], bp[0:64, :], [rp], KT[pi][0].res)
                        cp("act", KT[pi][1].ap[0:64, g * 512:(g + 1) * 512], bp[64:128, :], [rp], KT[pi][1].res)

                    def agrp(g):
                        bp, rp = nextbank()
                        sa_ = st_["a"]
                        for k in range(8):
                            mm(bp, wv(sa_, k), uTv(k, M0 + g * 512, M0 + (g + 1) * 512), k == 0, k == 7,
                               [r_w[sa_]] + all_uT, [rp])
                        act(AG[pi].ap[:, g * 512:(g + 1) * 512], bp, AF.Tanh, [rp], AG[pi].res, scale=0.5)
                        stt("dve", AG[pi].ap[:, g * 512:(g + 1) * 512], AG[pi].ap[:, g * 512:(g + 1) * 512], 1.0, bp,
                            ALU.add, ALU.mult, AG[pi].res + [rp], AG[pi].res)
                        if g == 1:
                            w_prefetch(4)

                    def mgrp(hh):
                        pm = PmT[pi][hh]
                        for b in range(2):
                            mm(bp[0:16, :], kmeta[0:64, hp * 32 + hh * 16:hp * 32 + hh * 16 + 16],
                               QT[pi][hh].ap[0:64, b * 512:(b + 1) * 512], True, True,
                               [r_meta] + QT[pi][hh].res, [rp])
                            act(pm.ap[:, b * 512:(b + 1) * 512], bp[0:16, :], AF.Exp, [rp], pm.res, scale=0.125)

                    tasks.append(lambda: (setup(), qgrp(0)))
                    tasks.append(lambda: qgrp(1))
                    for g in range(3):
                        tasks.append(lambda g=g: kgrp(g))
                    for g in range(2):
                        tasks.append(lambda g=g: agrp(g))
                    for hh in range(2):
                        tasks.append(lambda hh=hh: mgrp(hh))
                    return tasks

                def pair_proj(hp):
                    for t_ in pair_tasks(hp):
                        t_()

                kind = ("lo" if sgi % 2 == 0 else "hi") if sgi < 8 else "gen"
                ptiles = {"gen": list(range(NTILE)), "lo": list(range(2, NTILE)), "hi": list(range(0, NTILE - 2))}[kind]
                steps = [(hp, hh, p) for hp in range(8) for hh in range(2) for p in ptiles]
                plast_b = {}
                for b_ in range(2):
                    plast_b[b_] = max(p_ for p_ in ptiles
                                      if tile_rows(p_, kind)[0] <= 8 * b_ + 7 and tile_rows(p_, kind)[1] >= 8 * b_)

                def step_geom(p):
                    rlo, rhi = tile_rows(p, kind)
                    n = (rhi - rlo + 1) * 64
                    chunks = []
                    c0 = 0
                    while c0 < n:
                        cn = min(512, n - c0)
                        chunks.append((c0, cn))
                        c0 += cn
                    return rlo, rhi, n, chunks

                def emit_qk(idx):
                    hp, hh, p = steps[idx]
                    pi = hp % 2
                    hb = 64 * hh
                    rlo, rhi, n, chunks = step_geom(p)
                    Sb, rS = Sbuf[idx % 2]
                    for _jk in range(NJUNK):
                        mm(Sb[:, 0:512], ident_s[:, :], uTv(0, 0, 512), True, True, [r_const] + all_uT, [rS[0]])
                    for ci, (c0, cn) in enumerate(chunks):
                        q0 = rlo * 64 + c0
                        mm(Sb[:, c0:c0 + cn], KT[pi][hh].ap[0:80, p * 128:(p + 1) * 128],
                           QT[pi][hh].ap[0:80, q0:q0 + cn], True, True,
                           KT[pi][hh].res + QT[pi][hh].res, [rS[ci]])

                def obank(hp, hh, b):
                    return O3[(2 * (2 * hp + hh) + b) % 3]

                def emit_expmult(idx):
                    hp, hh, p = steps[idx]
                    ti = hp % 2
                    rlo, rhi, n, chunks = step_geom(p)
                    Sb, rS = Sbuf[idx % 2]
                    xi = idx % 2
                    rSu = rS[0:len(chunks)]
                    act(expS[xi].ap[:, 0:n], Sb[:, 0:n], AF.Exp, rSu, expS[xi].res, scale=0.125)
                    slot0 = 11 - 2 * p + rlo
                    tab = Ttab[ti].ap[:, hh * 1024 + slot0 * 64:hh * 1024 + slot0 * 64 + n]
                    tt("dve", PT[xi].ap[:, 0:n], expS[xi].ap[:, 0:n], tab, ALU.mult,
                       expS[xi].res + Ttab[ti].res + [r_T[ti]], PT[xi].res)

                def emit_pv(idx):
                    hp, hh, p = steps[idx]
                    pi = hp % 2
                    hb = 64 * hh
                    pm = PmT[pi][hh]
                    rlo, rhi, n, chunks = step_geom(p)
                    xi = idx % 2
                    if p == ptiles[0]:
                        if hh == 0:
                            vmeta = vmA[0:16, hp * 128:(hp + 1) * 128]
                            pmrows = (0, 16)
                        else:
                            vmeta = vmB[0:16, hp * 128:(hp + 1) * 128]
                            pmrows = (0, 16)
                        for b in range(2):
                            ob, ro = obank(hp, hh, b)
                            while id(ro) in pend_bank:
                                pending.pop(0)()
                                pend_bank.pop(0)
                            mm(ob, vmeta, pm.ap[pmrows[0]:pmrows[1], b * 512:(b + 1) * 512], True, False,
                               [r_meta] + pm.res, [ro], skip_group_check=True)
                    vt = Vt[p].ap
                    lhs = vt[:, 192 * hp + 64 * hh:192 * hp + 64 * hh + 128]
                    for b in range(2):
                        lo = max(rlo, 8 * b)
                        hi = min(rhi, 8 * b + 7)
                        if lo > hi:
                            continue
                        ob, ro = obank(hp, hh, b)
                        mm(ob[:, (lo - 8 * b) * 64:(hi - 8 * b + 1) * 64], lhs,
                           PT[xi].ap[:, (lo - rlo) * 64:(hi - rlo + 1) * 64], False, True,
                           Vt[p].res + PT[xi].res, [ro], skip_group_check=True)
                    for b, plast in ((0, plast_b[0]), (1, plast_b[1])):
                        if p != plast:
                            continue
                        ob, ro = obank(hp, hh, b)
                        o_lo, o_hi = hb, hb + 64
                        d_lo, d_hi = 64 - hb, 128 - hb
                        R = Rb[b]
                        at = attb[b]
                        for c4 in range(4):
                            cs = slice(c4 * 128, (c4 + 1) * 128)
                            pend_bank.append(id(ro))
                            pending.append(lambda ob=ob, R=R, ro=ro, cs=cs, d_lo=d_lo, d_hi=d_hi, o_lo=o_lo, o_hi=o_hi:
                                           sch.add("dve", lambda e: e.reciprocal(out=R.ap[o_lo:o_hi, cs], in_=ob[d_lo:d_hi, cs]),
                                                   [ro], R.res))

                        def fin(ob=ob, ro=ro, R=R, at=at, o_lo=o_lo, o_hi=o_hi, hp=hp, b=b, pi=pi):
                            act(at.ap[o_lo:o_hi, :], ob[o_lo:o_hi, :], AF.Identity, [ro], at.res, scale=0.5)
                            tt("pool", at.ap[o_lo:o_hi, :], at.ap[o_lo:o_hi, :], R.ap[o_lo:o_hi, :], ALU.mult,
                               at.res + R.res, at.res)
                            tt("pool", mixT[o_lo:o_hi, (8 + hp) * NM + b * 512:(8 + hp) * NM + (b + 1) * 512],
                               at.ap[o_lo:o_hi, :], AG[pi].ap[o_lo:o_hi, b * 512:(b + 1) * 512], ALU.mult,
                               at.res + AG[pi].res, [r_mix[8 + hp][b]])
                        pend_bank.append(id(ro))
                        pending.append(fin)

                pending = []
                pend_bank = []
                projq = []
                for pi_ in range(2):
                    for hh_ in range(2):
                        kidx = pi_ * 2 + hh_
                        dma("pool", KT[pi_][hh_].ap[64:80, :], maskA[sgi], [], KT[pi_][hh_].res, own=r_mask[kidx])
                        dma("pool", QT[pi_][hh_].ap[64:80, :], conehot[:], [], QT[pi_][hh_].res, own=r_mask[4 + kidx])
                pair_proj(0)
                emit_qk(0)
                for idx in range(len(steps)):
                    hp, hh, p = steps[idx]
                    if hh == 1 and p == 2 and hp + 1 < 8:
                        pair_proj(hp + 1)
                    if idx + 1 < len(steps):
                        emit_qk(idx + 1)
                    emit_expmult(idx)
                    for _d in range(2 if len(pending) > 5 else 1):
                        if pending:
                            pending.pop(0)()
                            pend_bank.pop(0)
                    if idx >= 1:
                        emit_pv(idx - 1)
                emit_pv(len(steps) - 1)
                while pending:
                    pending.pop(0)()
                    pend_bank.pop(0)


                if DBG and sgi == nseg - 1:
                    allmix = [r_mix[e_][g_] for e_ in range(16) for g_ in range(2)]
                    dma("sp", dbg_mix[:, :], mixT[:, :], allmix, [], own=r_init)
                dma("sp", woutS.ap, wout_bf[:, :], [r_scr], woutS.res + [r_wout], own=r_wout)
                for t in range(NM // 128):
                    Sb, rS = Sbuf[rot["S"] % 2]
                    rot["S"] += 1
                    oi = rot["out"] % 2
                    rot["out"] += 1
                    g = t // 4
                    dma("sp", xres[oi].ap, xe[sgi, M0 + t * 128:M0 + (t + 1) * 128, :], [],
                        xres[oi].res + [r_xres[oi]], own=r_xres[oi])
                    for half in range(2):
                        for e_ in range(16):
                            mm(Sb[:, half * 512:(half + 1) * 512], mixv(e_, t * 128, (t + 1) * 128),
                               woutS.ap[:, e_ * 1024 + half * 512:e_ * 1024 + (half + 1) * 512], e_ == 0, e_ == 15,
                               [r_mix[e_][g]] + woutS.res + [r_wout], [rS[half]])
                    si = cnt["st"] % 4
                    cnt["st"] += 1
                    st = stat[si]
                    ob = outb[oi]
                    act(ob.ap, Sb[:, :], AF.Square, rS, ob.res + [r_out[oi], r_stat[si]], accum_out=st[:, 0:1])
                    ts("dve", st[:, 1:2], st[:, 0:1], 1.0 / D, 1e-6, ALU.mult, ALU.add, [r_stat[si]], [r_stat[si]])
                    rsqrt_col(st, 128, [r_stat[si]])
                    stt("dve", ob.ap, Sb[:, :], st[:, 2:3], gpost_s[:], ALU.mult, ALU.mult,
                        rS + [r_stat[si], r_const], ob.res + [r_out[oi]])
                    tt("pool", ob.ap, ob.ap, xres[oi].ap, ALU.add, ob.res + [r_out[oi]] + xres[oi].res + [r_xres[oi]],
                       ob.res + [r_out[oi]])
                    dma("sp", ye[sgi, t * 128:(t + 1) * 128, :], ob.ap, ob.res + [r_out[oi]], [], own=r_out[oi])

        except StopBuild:
            pass

        def final_waits():
            allr = r_out + r_xt + r_w + r_mask + r_T + r_diag + [r_wout, r_init, r_tb, r_db] + r_xres
            return [(r.sem, r.cnt) for r in allr if r.cnt > 0] + [(g.sem, g.cnt) for g in (ginit, gconst) if g.cnt > 0]

        sch.check_deadlock(esem, final_waits)
        with nc.Block() as block:
            sch.emit(block, esem, final_waits)
    return nc


def _col_tables():
    key = np.arange(GW)
    q = np.arange(GW)
    start = np.clip(q - 8, 0, GW - 16)
    off = key[:, None] - start[None, :]
    valid = (off >= 0) & (off < 16)
    rel = np.clip(key[:, None] - q[None, :] + 15, 0, 30)
    return valid, rel


def host_weights(w_in, w_out, conv_w, conv_b, ln_g, ln_b, rel_bias, pre_g, post_g, meta_tokens):
    w_in = np.asarray(w_in, np.float32)[0]
    cols = np.arange(7168)
    vperm = np.zeros(1024, np.int64)
    for n in range(1024):
        if n < 512:
            head = 2 * (n // 64)
        else:
            head = 2 * ((n - 512) // 64) + 1
        vperm[n] = 5120 + head * 64 + n % 64
    cols[5120:6144] = vperm
    wp = w_in[:, cols]
    win = np.ascontiguousarray(wp.reshape(8, 128, 56, 128).transpose(2, 1, 0, 3)).reshape(56, 128, 1024)
    for g in range(2):
        blk = wp[:, 5120 + g * 512:5120 + (g + 1) * 512].reshape(8, 128, 512).transpose(1, 0, 2)
        win[40 + 4 * g:44 + 4 * g] = blk.reshape(128, 4, 1024).transpose(1, 0, 2)
    wo = np.asarray(w_out, np.float32)[0]
    wout = np.ascontiguousarray(wo.reshape(16, 128, 1024).transpose(1, 0, 2)).reshape(128, 16 * 1024)
    cw = np.asarray(conv_w, np.float32)[0]
    convwT = np.ascontiguousarray(cw.reshape(CONVW, 8, 128).transpose(2, 1, 0)).reshape(128, 8 * CONVW)
    chvec = np.concatenate([np.asarray(v, np.float32)[0].reshape(8, 128).T for v in (conv_b, ln_g, ln_b)], axis=1)
    gpre = np.ascontiguousarray(np.broadcast_to(np.asarray(pre_g, np.float32)[0][None, :], (128, D)))
    gpost = np.ascontiguousarray(np.broadcast_to(np.asarray(post_g, np.float32)[0][None, :], (128, D)))
    rb = np.asarray(rel_bias, np.float32)[0]
    valid, rel = _col_tables()
    Bt = np.zeros((128, 16, 16, 64), np.float32)
    Mt = np.zeros((128, 16, 64), np.float32)
    for half in range(2):
        for slot in range(16):
            d = 7 - slot
            dr = d + half
            if abs(dr) > 7:
                continue
            Bt[half * 64:(half + 1) * 64, :, slot, :] = rb[:, dr + 7, :][:, rel].transpose(1, 0, 2)
            Mt[half * 64:(half + 1) * 64, slot, :] = valid.astype(np.float32)
    conehot = np.zeros((16, NM), np.float32)
    for r in range(16):
        conehot[r, r * 64:(r + 1) * 64] = 1.0
    return {
        "win": win, "wout": wout, "convwT": convwT, "chvec": np.ascontiguousarray(chvec),
        "gpre": gpre, "gpost": gpost, "meta": np.ascontiguousarray(np.asarray(meta_tokens, np.float32)),
        "Bt": Bt.reshape(128, -1), "Mt": Mt.reshape(128, -1), "conehot": conehot,
        "ident": np.eye(128, dtype=np.float32),
    }


def host_segment(xseq, meta_tokens, R0):
    T = xseq.shape[0]
    rows = T // GW
    xe = np.zeros((NE, D), np.float32)
    t0 = R0 * GW - M0
    lo = max(0, t0)
    hi = min(T, t0 + NE)
    xe[lo - t0:hi - t0] = xseq[lo:hi]
    if t0 < 0:
        xe[-t0 - NMETA:-t0] = meta_tokens
    mask = np.full((16, NE), NEG, np.float32)
    wr = min(8, rows)
    for r in range(16):
        R = R0 + r
        sr = int(np.clip(R - wr // 2, 0, rows - wr))
        for kr in range(sr, sr + wr):
            er = kr - R0 + HALO
            if 0 <= er < EXT_ROWS:
                mask[r, er * GW:(er + 1) * GW] = 0.0
    return xe, mask


def run_segments(seg_lists, wts, n_cores):
    nseg = len(seg_lists[0])
    nc = build_program(nseg)
    in_maps = []
    for c in range(n_cores):
        xe = np.zeros((nseg, NE, D), np.float32)
        mk = np.zeros((nseg, 16, NE), np.float32)
        for s, (xseq, R0) in enumerate(seg_lists[c]):
            xe[s], mk[s] = host_segment(xseq, wts["meta"], R0)
        m = dict(wts)
        m["xe"] = xe
        m["maskA"] = mk
        in_maps.append(m)
    res = run_bass_kernel_spmd(nc, in_maps, core_ids=list(range(n_cores)))
    if int(os.environ.get('KDBG', '0')):
        return [r["ye"] for r in res.results], [(r["dbg_mix"], r["dbg_uT"], r["dbg_w"]) for r in res.results]
    return [r["ye"] for r in res.results]


def kernel(x_prompt, x_sample, meta_tokens, pre_norm_g, w_in, conv_w, conv_b, conv_ln_g, conv_ln_b,
           rel_bias, post_norm_g, w_out):
    x_prompt = np.asarray(x_prompt, np.float32)
    x_sample = np.asarray(x_sample, np.float32)
    wts = host_weights(w_in, w_out, conv_w, conv_b, conv_ln_g, conv_ln_b, rel_bias, pre_norm_g, post_norm_g,
                       meta_tokens)
    seg_lists = []
    where = []
    for c in range(NCORES):
        segs = []
        wh = []
        for i in range(4):
            b = 4 * c + i
            for R0 in (0, 16):
                segs.append((x_prompt[b], R0))
                wh.append((0, b, R0))
        sb_ = c // 2
        for R0 in (32 * (c % 2), 32 * (c % 2) + 16):
            segs.append((x_sample[sb_], R0))
            wh.append((1, sb_, R0))
        seg_lists.append(segs)
        where.append(wh)
    outs = run_segments(seg_lists, wts, NCORES)
    y_prompt = np.empty_like(x_prompt)
    y_sample = np.empty_like(x_sample)
    for c in range(NCORES):
        for s, (which, b, R0) in enumerate(where[c]):
            dst = y_prompt if which == 0 else y_sample
            dst[b, R0 * GW:(R0 + SEG_ROWS) * GW] = outs[c][s]
    return (y_prompt, y_sample)
```
